# Optimizing a Trainium2 kernel written in Bass

```python
import math
import jax, jax.numpy as jnp
from jax import lax
import numpy as np

D_MODEL = 4096
BATCH = 2
SEQ = 4096
DEPTH = 2

F32 = jnp.float32
EPS = 1e-6
HEAD_DIM = 128
MIX_WIDTH = D_MODEL
N_GROUPS = 4
GROUP_WIDTH = MIX_WIDTH // N_GROUPS
GROUP_HEADS = GROUP_WIDTH // HEAD_DIM

RET_HEADS = GROUP_HEADS
RET_CHUNK = 128
ROPE_BASE = 10000.0
MOBA_HEADS = GROUP_HEADS
MOBA_BLOCK = 256
MOBA_TOPK = 3
MOBA_QCHUNK = 32
LRU_WIDTH = GROUP_WIDTH
LRU_BLOCKS = GROUP_HEADS
LRU_BLOCK_W = LRU_WIDTH // LRU_BLOCKS
CONV_WIDTH = 4
LRU_C = 8.0
NSA_HEADS = GROUP_HEADS
NSA_KV_HEADS = 2
NSA_KV_WIDTH = NSA_KV_HEADS * HEAD_DIM
NSA_BRANCHES = 3
CMP_LEN = 32
CMP_STRIDE = 16
CMP_HIDDEN = 256
SEL_BLOCK = 64
SEL_TOPN = 16
WIN = 512
WIN_QBLOCK = 128
NSA_QCHUNK = 64
D_FF = ((8 * D_MODEL + 3 * 256 - 1) // (3 * 256)) * 256

IN_WIDTHS = ((GROUP_WIDTH,) * 4
             + (GROUP_WIDTH,) * 3
             + (LRU_WIDTH,) * 2
             + (GROUP_WIDTH,)
             + (NSA_KV_WIDTH,) * 6
             + (NSA_HEADS * NSA_BRANCHES,))
IN_WIDTH = sum(IN_WIDTHS)
SPLIT_POINTS = tuple(sum(IN_WIDTHS[:i]) for i in range(1, len(IN_WIDTHS)))

kernel_name = 'hybrid_parallel_heads_trunk'


def rmsnorm(x, w):
    xf = x.astype(F32)
    y = xf * lax.rsqrt(jnp.mean(xf * xf, axis=-1, keepdims=True) + EPS) * w.astype(F32)
    return y.astype(x.dtype)


def masked_softmax(scores, mask):
    s = jnp.where(mask, scores.astype(F32), -jnp.inf)
    m = jnp.max(s, axis=-1, keepdims=True)
    m = jnp.where(jnp.isfinite(m), m, 0.0)
    e = jnp.where(mask, jnp.exp(s - m), 0.0)
    return e / jnp.maximum(jnp.sum(e, axis=-1, keepdims=True), 1e-30)


def to_chunks(t, axis, size):
    n = t.shape[axis] // size
    t = t.reshape(t.shape[:axis] + (n, size) + t.shape[axis + 1:])
    return jnp.moveaxis(t, axis, 0)


def from_chunks(t, axis):
    t = jnp.moveaxis(t, 0, axis)
    return t.reshape(t.shape[:axis] + (t.shape[axis] * t.shape[axis + 1],) + t.shape[axis + 2:])


def rope(x, pos):
    half = x.shape[-1] // 2
    inv = ROPE_BASE ** (-jnp.arange(half, dtype=F32) / half)
    ang = pos.astype(F32)[:, None] * inv[None, :]
    cos = jnp.cos(ang)[None, :, None, :]
    sin = jnp.sin(ang)[None, :, None, :]
    x1, x2 = x[..., :half], x[..., half:]
    return jnp.concatenate([x1 * cos - x2 * sin, x1 * sin + x2 * cos], axis=-1)


def retention(q, k, v, g, gain):
    B_, S_, H, d = q.shape
    pos = jnp.arange(S_)
    q = rope(q.astype(F32), pos)
    k = rope(k.astype(F32), pos) * (d ** -0.5)
    v = v.astype(F32)
    C = RET_CHUNK
    N = S_ // C
    log_gamma = jnp.log1p(-jnp.exp2(-5.0 - jnp.arange(H, dtype=F32)))
    qc = q.reshape(B_, N, C, H, d)
    kc = k.reshape(B_, N, C, H, d)
    vc = v.reshape(B_, N, C, H, d)
    pc = jnp.arange(C, dtype=F32)
    diff = pc[:, None] - pc[None, :]
    decay_intra = jnp.where(diff >= 0, jnp.exp(log_gamma[:, None, None] * jnp.maximum(diff, 0.0)), 0.0)
    scores = jnp.einsum('bnihd,bnjhd->bnhij', qc, kc) * decay_intra
    y_intra = jnp.einsum('bnhij,bnjhd->bnihd', scores, vc)
    decay_to_end = jnp.exp(log_gamma[:, None] * (C - 1.0 - pc)[None, :])
    kv_chunk = jnp.einsum('bnjhd,hj,bnjhe->bnhde', kc, decay_to_end, vc)
    decay_chunk = jnp.exp(log_gamma * C)[None, :, None, None]

    def step(state, kv_n):
        return decay_chunk * state + kv_n, state

    _, state_prev = lax.scan(step, jnp.zeros((B_, H, d, d), F32), jnp.moveaxis(kv_chunk, 1, 0))
    state_prev = jnp.moveaxis(state_prev, 0, 1)
    decay_from_start = jnp.exp(log_gamma[:, None] * (pc + 1.0)[None, :])
    y_cross = jnp.einsum('bnihd,bnhde,hi->bnihe', qc, state_prev, decay_from_start)
    y = (y_intra + y_cross).reshape(B_, S_, H, d)
    y = y - jnp.mean(y, axis=-1, keepdims=True)
    y = y * lax.rsqrt(jnp.mean(y * y, axis=-1, keepdims=True) + EPS)
    y = y.reshape(B_, S_, H * d) * gain.astype(F32)
    return (jax.nn.silu(g.astype(F32)) * y).astype(g.dtype)


def moba_attention(q, k, v):
    B_, S_, H, d = q.shape
    scale = d ** -0.5
    BLK = MOBA_BLOCK
    QC = MOBA_QCHUNK
    NB = -(-S_ // BLK)
    pad = NB * BLK - S_
    qt = q.transpose(0, 2, 1, 3)
    kp = jnp.pad(k.transpose(0, 2, 1, 3), ((0, 0), (0, 0), (0, pad), (0, 0)))
    vp = jnp.pad(v.transpose(0, 2, 1, 3), ((0, 0), (0, 0), (0, pad), (0, 0)))
    kb = kp.reshape(B_, H, NB, BLK, d)
    vb = vp.reshape(B_, H, NB, BLK, d)
    k_mean = jnp.mean(kb.astype(F32), axis=3)
    qblk = jnp.arange(S_) // BLK
    gate = jnp.einsum('bhsd,bhnd->bhsn', qt.astype(F32), k_mean)
    past = jnp.arange(NB)[None, :] < qblk[:, None]
    gate = jnp.where(past, gate, -jnp.inf)
    topk = min(MOBA_TOPK, NB)
    _, sel = lax.top_k(gate, topk)
    sel_valid = sel < qblk[None, None, :, None]
    bi = jnp.arange(B_)[:, None, None, None]
    hi = jnp.arange(H)[None, :, None, None]
    M = topk * BLK

    def one_chunk(args):
        c, q_c, sel_c, val_c = args
        t_c = c * QC + jnp.arange(QC)
        k_sel = kb[bi, hi, sel_c].reshape(B_, H, QC, M, d)
        v_sel = vb[bi, hi, sel_c].reshape(B_, H, QC, M, d)
        own = (c * QC) // BLK * BLK
        k_own = lax.dynamic_slice_in_dim(kp, own, BLK, axis=2)
        v_own = lax.dynamic_slice_in_dim(vp, own, BLK, axis=2)
        s_sel = jnp.einsum('bhqd,bhqmd->bhqm', q_c, k_sel)
        s_own = jnp.einsum('bhqd,bhld->bhql', q_c, k_own)
        m_sel = jnp.broadcast_to(val_c[..., None], (B_, H, QC, topk, BLK)).reshape(B_, H, QC, M)
        m_own = jnp.broadcast_to(((own + jnp.arange(BLK))[None, :] <= t_c[:, None])[None, None], (B_, H, QC, BLK))
        p = masked_softmax(jnp.concatenate([s_sel, s_own], axis=-1) * scale,
                           jnp.concatenate([m_sel, m_own], axis=-1)).astype(v.dtype)
        return (jnp.einsum('bhqm,bhqmd->bhqd', p[..., :M], v_sel)
                + jnp.einsum('bhql,bhld->bhqd', p[..., M:], v_own))

    n_ch = S_ // QC
    out = lax.map(one_chunk, (jnp.arange(n_ch), to_chunks(qt, 2, QC), to_chunks(sel, 2, QC), to_chunks(sel_valid, 2, QC)))
    out = from_chunks(out, 2)
    return out.transpose(0, 2, 1, 3).reshape(B_, S_, H * d)


def rg_lru_block(xb, gb, conv_w, conv_b, wa, ba, wx, bx, lam):
    B_, S_, R = xb.shape
    xp = jnp.pad(xb, ((0, 0), (CONV_WIDTH - 1, 0), (0, 0)))
    xc = conv_b
    for tap in range(CONV_WIDTH):
        xc = xc + xp[:, tap:tap + S_] * conv_w[tap]
    xg = xc.reshape(B_, S_, LRU_BLOCKS, LRU_BLOCK_W)
    r = jax.nn.sigmoid((jnp.einsum('bsgi,gij->bsgj', xg, wa).reshape(B_, S_, R) + ba).astype(F32))
    i = jax.nn.sigmoid((jnp.einsum('bsgi,gij->bsgj', xg, wx).reshape(B_, S_, R) + bx).astype(F32))
    log_a = -LRU_C * r * jax.nn.softplus(-lam.astype(F32))
    a = jnp.exp(log_a)
    u = jnp.sqrt(jnp.maximum(-jnp.expm1(2.0 * log_a), 0.0)) * (i * xc.astype(F32))
    _, h = lax.associative_scan(lambda e1, e2: (e1[0] * e2[0], e2[0] * e1[1] + e2[1]), (a, u), axis=1)
    return (h * jax.nn.gelu(gb.astype(F32))).astype(xb.dtype)


def nsa_attention(q, k_cmp, v_cmp, k_sel, v_sel, k_win, v_win, gate_logits,
                  pos_k, w1_k, w2_k, pos_v, w1_v, w2_v):
    B_, S_, H, d = q.shape
    KVH = k_cmp.shape[2]
    G = H // KVH
    scale = d ** -0.5
    t = jnp.arange(S_)
    qg = q.reshape(B_, S_, KVH, G, d).transpose(0, 2, 3, 1, 4)

    NC = (S_ - CMP_LEN) // CMP_STRIDE + 1
    starts = jnp.arange(NC) * CMP_STRIDE
    cidx = starts[:, None] + jnp.arange(CMP_LEN)[None, :]

    def compress(x, pos, w1, w2):
        blk = x[:, cidx] + pos[None, None, :, None, :]
        flat = blk.transpose(0, 3, 1, 2, 4).reshape(B_, KVH, NC, CMP_LEN * d)
        return jax.nn.gelu(flat @ w1) @ w2

    kc = compress(k_cmp, pos_k, w1_k, w2_k)
    vc = compress(v_cmp, pos_v, w1_v, w2_v)
    s_cmp = jnp.einsum('bkgsd,bknd->bkgsn', qg, kc) * scale
    m_cmp = (starts + CMP_LEN - 1)[None, :] <= t[:, None]
    p_cmp = masked_softmax(s_cmp, m_cmp)
    o_cmp = jnp.einsum('bkgsn,bknd->bkgsd', p_cmp.astype(vc.dtype), vc).astype(F32)

    NSEL = S_ // SEL_BLOCK
    sel_start = jnp.arange(NSEL) * SEL_BLOCK
    overlap = ((starts[:, None] < sel_start[None, :] + SEL_BLOCK)
               & (starts[:, None] + CMP_LEN > sel_start[None, :])).astype(F32)
    imp = jnp.einsum('bkgsn,nj->bksj', p_cmp, overlap)
    qsb = t // SEL_BLOCK
    j = jnp.arange(NSEL)[None, :]
    forced = (j == 0) | (j == qsb[:, None]) | (j == qsb[:, None] - 1)
    allowed = j <= qsb[:, None]
    imp = jnp.where(forced, jnp.inf, jnp.where(allowed, imp, -jnp.inf))
    topn = min(SEL_TOPN, NSEL)
    _, sel = lax.top_k(imp, topn)
    kb = k_sel.transpose(0, 2, 1, 3).reshape(B_, KVH, NSEL, SEL_BLOCK, d)
    vb = v_sel.transpose(0, 2, 1, 3).reshape(B_, KVH, NSEL, SEL_BLOCK, d)
    bi = jnp.arange(B_)[:, None, None, None]
    hi = jnp.arange(KVH)[None, :, None, None]
    QC = NSA_QCHUNK
    M = topn * SEL_BLOCK

    def sel_chunk(args):
        c, q_c, sel_c = args
        t_c = c * QC + jnp.arange(QC)
        kk = kb[bi, hi, sel_c].reshape(B_, KVH, QC, M, d)
        vv = vb[bi, hi, sel_c].reshape(B_, KVH, QC, M, d)
        s = jnp.einsum('bkgqd,bkqmd->bkgqm', q_c, kk) * scale
        kpos = (sel_c[..., None] * SEL_BLOCK + jnp.arange(SEL_BLOCK)).reshape(B_, KVH, QC, M)
        mask = (kpos <= t_c[None, None, :, None])[:, :, None]
        p = masked_softmax(s, mask).astype(vv.dtype)
        return jnp.einsum('bkgqm,bkqmd->bkgqd', p, vv)

    n_ch = S_ // QC
    o_sel = lax.map(sel_chunk, (jnp.arange(n_ch), to_chunks(qg, 3, QC), to_chunks(sel, 2, QC)))
    o_sel = from_chunks(o_sel, 3).astype(F32)

    NQB = S_ // WIN_QBLOCK
    NPB = WIN // WIN_QBLOCK

    def window_blocks(x):
        xp = jnp.pad(x.transpose(0, 2, 1, 3), ((0, 0), (0, 0), (WIN, 0), (0, 0)))
        xp = xp.reshape(B_, KVH, NPB + NQB, WIN_QBLOCK, d)
        return jnp.concatenate([xp[:, :, o:o + NQB] for o in range(NPB + 1)], axis=3)

    kw = window_blocks(k_win)
    vw = window_blocks(v_win)
    qw = qg.reshape(B_, KVH, G, NQB, WIN_QBLOCK, d)
    s_win = jnp.einsum('bkgnqd,bknmd->bkgnqm', qw, kw) * scale
    tq = jnp.arange(S_).reshape(NQB, WIN_QBLOCK)
    kpos = (jnp.arange(NQB) * WIN_QBLOCK - WIN)[:, None] + jnp.arange((NPB + 1) * WIN_QBLOCK)[None, :]
    dist = tq[:, :, None] - kpos[:, None, :]
    m_win = (kpos[:, None, :] >= 0) & (dist >= 0) & (dist < WIN)
    p_win = masked_softmax(s_win, m_win).astype(vw.dtype)
    o_win = jnp.einsum('bkgnqm,bknmd->bkgnqd', p_win, vw).reshape(B_, KVH, G, S_, d).astype(F32)

    gates = jax.nn.sigmoid(gate_logits.astype(F32)).reshape(B_, S_, KVH, G, NSA_BRANCHES).transpose(0, 2, 3, 1, 4)
    o = gates[..., 0:1] * o_cmp + gates[..., 1:2] * o_sel + gates[..., 2:3] * o_win
    return o.transpose(0, 3, 1, 2, 4).reshape(B_, S_, H * d).astype(q.dtype)


def split_heads(t, n_heads):
    return t.reshape(t.shape[0], t.shape[1], n_heads, HEAD_DIM)


def hybrid_layer(x, norm_mix, w_in, w_out, ret_norm, lru_conv_w, lru_conv_b, lru_wa, lru_ba, lru_wx, lru_bx,
                 lru_lambda, cmp_pos_k, cmp_w1_k, cmp_w2_k, cmp_pos_v, cmp_w1_v, cmp_w2_v,
                 norm_ffn, w_gate, w_up, w_down):
    B_, S_, _ = x.shape
    h = rmsnorm(x, norm_mix)
    proj = jnp.einsum('bsd,de->bse', h, w_in)
    (rq, rk, rv, rg, mq, mk, mv, lx, lg, nq, nkc, nvc, nks, nvs, nkw, nvw, ngate) = jnp.split(proj, SPLIT_POINTS, axis=-1)
    y_ret = retention(split_heads(rq, RET_HEADS), split_heads(rk, RET_HEADS), split_heads(rv, RET_HEADS), rg, ret_norm)
    y_moba = moba_attention(split_heads(mq, MOBA_HEADS), split_heads(mk, MOBA_HEADS), split_heads(mv, MOBA_HEADS))
    y_lru = rg_lru_block(lx, lg, lru_conv_w, lru_conv_b, lru_wa, lru_ba, lru_wx, lru_bx, lru_lambda)
    y_nsa = nsa_attention(split_heads(nq, NSA_HEADS),
                          split_heads(nkc, NSA_KV_HEADS), split_heads(nvc, NSA_KV_HEADS),
                          split_heads(nks, NSA_KV_HEADS), split_heads(nvs, NSA_KV_HEADS),
                          split_heads(nkw, NSA_KV_HEADS), split_heads(nvw, NSA_KV_HEADS),
                          ngate.reshape(B_, S_, NSA_HEADS, NSA_BRANCHES),
                          cmp_pos_k, cmp_w1_k, cmp_w2_k, cmp_pos_v, cmp_w1_v, cmp_w2_v)
    y = jnp.concatenate([y_ret, y_moba, y_lru, y_nsa], axis=-1)
    x = x + jnp.einsum('bse,ed->bsd', y, w_out)
    h = rmsnorm(x, norm_ffn)
    u = jax.nn.silu(h @ w_gate) * (h @ w_up)
    return x + u @ w_down


def setup_inputs(seed: int = 0) -> dict:
    key = jax.random.key(seed)
    ks = jax.random.split(key, 24)

    def nrm(k, shape, scale):
        return jax.random.normal(k, shape, F32) * scale

    D = D_MODEL
    a_c = jax.random.uniform(ks[11], (DEPTH, LRU_WIDTH), F32, 0.9, 0.999)
    a = a_c ** (1.0 / LRU_C)
    return {
        'x': nrm(ks[0], (BATCH, SEQ, D), 1.0),
        'norm_mix': 1.0 + nrm(ks[1], (DEPTH, D), 0.02),
        'w_in': nrm(ks[2], (DEPTH, D, IN_WIDTH), D ** -0.5),
        'w_out': nrm(ks[3], (DEPTH, MIX_WIDTH, D), MIX_WIDTH ** -0.5),
        'ret_norm': 1.0 + nrm(ks[4], (DEPTH, GROUP_WIDTH), 0.02),
        'lru_conv_w': nrm(ks[5], (DEPTH, CONV_WIDTH, LRU_WIDTH), CONV_WIDTH ** -0.5),
        'lru_conv_b': nrm(ks[6], (DEPTH, LRU_WIDTH), 0.01),
        'lru_wa': nrm(ks[7], (DEPTH, LRU_BLOCKS, LRU_BLOCK_W, LRU_BLOCK_W), LRU_BLOCK_W ** -0.5),
        'lru_ba': nrm(ks[8], (DEPTH, LRU_WIDTH), 0.01),
        'lru_wx': nrm(ks[9], (DEPTH, LRU_BLOCKS, LRU_BLOCK_W, LRU_BLOCK_W), LRU_BLOCK_W ** -0.5),
        'lru_bx': nrm(ks[10], (DEPTH, LRU_WIDTH), 0.01),
        'lru_lambda': jnp.log(a) - jnp.log1p(-a),
        'cmp_pos_k': nrm(ks[12], (DEPTH, CMP_LEN, HEAD_DIM), 0.02),
        'cmp_w1_k': nrm(ks[13], (DEPTH, CMP_LEN * HEAD_DIM, CMP_HIDDEN), (CMP_LEN * HEAD_DIM) ** -0.5),
        'cmp_w2_k': nrm(ks[14], (DEPTH, CMP_HIDDEN, HEAD_DIM), CMP_HIDDEN ** -0.5),
        'cmp_pos_v': nrm(ks[15], (DEPTH, CMP_LEN, HEAD_DIM), 0.02),
        'cmp_w1_v': nrm(ks[16], (DEPTH, CMP_LEN * HEAD_DIM, CMP_HIDDEN), (CMP_LEN * HEAD_DIM) ** -0.5),
        'cmp_w2_v': nrm(ks[17], (DEPTH, CMP_HIDDEN, HEAD_DIM), CMP_HIDDEN ** -0.5),
        'norm_ffn': 1.0 + nrm(ks[18], (DEPTH, D), 0.02),
        'w_gate': nrm(ks[19], (DEPTH, D, D_FF), D ** -0.5),
        'w_up': nrm(ks[20], (DEPTH, D, D_FF), D ** -0.5),
        'w_down': nrm(ks[21], (DEPTH, D_FF, D), D_FF ** -0.5),
        'norm_final': 1.0 + nrm(ks[22], (D,), 0.02),
    }


def reference(x, norm_mix, w_in, w_out, ret_norm, lru_conv_w, lru_conv_b, lru_wa, lru_ba, lru_wx, lru_bx,
              lru_lambda, cmp_pos_k, cmp_w1_k, cmp_w2_k, cmp_pos_v, cmp_w1_v, cmp_w2_v,
              norm_ffn, w_gate, w_up, w_down, norm_final):
    for l in range(DEPTH):
        x = hybrid_layer(x, norm_mix[l], w_in[l], w_out[l], ret_norm[l], lru_conv_w[l], lru_conv_b[l],
                         lru_wa[l], lru_ba[l], lru_wx[l], lru_bx[l], lru_lambda[l],
                         cmp_pos_k[l], cmp_w1_k[l], cmp_w2_k[l], cmp_pos_v[l], cmp_w1_v[l], cmp_w2_v[l],
                         norm_ffn[l], w_gate[l], w_up[l], w_down[l])
    return rmsnorm(x, norm_final)
```

```python
import math
from contextlib import ExitStack

import numpy as np
import ml_dtypes

import concourse.bass as bass
import concourse.mybir as mybir
from concourse.bass_utils import run_bass_kernel_spmd

F32 = mybir.dt.float32
BF16 = mybir.dt.bfloat16
AF = mybir.ActivationFunctionType
ALU = mybir.AluOpType
AX = mybir.AxisListType

ENGS = ("pe", "act", "dve", "pool", "sp")
DMAQ = ("sp", "pool")
RING = 8


class Op:
    __slots__ = ("eng", "fn", "r", "w", "dma", "sig", "deps", "token", "bp")

    def __init__(self, eng, fn, r, w, dma):
        self.eng, self.fn, self.r, self.w, self.dma = eng, fn, r, w, dma
        self.sig = False
        self.deps = ()
        self.token = None
        self.bp = None


class Sched:
    def __init__(self, nc, stack):
        self.nc = nc
        self.sem = {e: stack.enter_context(nc.semaphore("s_" + e)) for e in ENGS}
        self.cnt = {e: 0 for e in ENGS}
        self.ring = {q: [stack.enter_context(nc.semaphore("d_%s%d" % (q, i))) for i in range(RING)] for q in DMAQ}
        self.ring_cnt = {q: [0] * RING for q in DMAQ}
        self.ring_idx = {q: 0 for q in DMAQ}
        self.waited = {e: {} for e in ENGS}
        self.ops = []
        self.n_emitted = 0

    def add(self, eng, fn, r=(), w=(), dma=False):
        self.ops.append(Op(eng, fn, tuple(r), tuple(w), dma))

    def dma(self, q, out, in_, r=(), w=()):
        self.add(q, lambda e: e.dma_start(out=out, in_=in_), r, w, dma=True)

    def flush(self):
        ops = self.ops
        self.ops = []
        if not ops:
            return
        lastw = {}
        readers = {}
        for i, op in enumerate(ops):
            deps = set()
            for k in op.r:
                j = lastw.get(k)
                if j is not None:
                    deps.add(j)
            for k in op.w:
                j = lastw.get(k)
                if j is not None:
                    deps.add(j)
                for j in readers.get(k, {}).values():
                    deps.add(j)
            deps.discard(i)
            if op.eng == "pe" and not op.dma:
                deps = {j for j in deps if not (ops[j].eng == "pe" and not ops[j].dma)}
            op.deps = tuple(sorted(deps))
            for j in op.deps:
                ops[j].sig = True
            for k in op.w:
                lastw[k] = i
                readers[k] = {}
            for k in op.r:
                d = readers.setdefault(k, {})
                d[(op.eng, i) if op.dma else op.eng] = i
        last_of = {}
        for i, op in enumerate(ops):
            if not op.dma:
                last_of[op.eng] = i
        for i in last_of.values():
            ops[i].sig = True
        for op in ops:
            if op.dma:
                q = op.eng
                s = self.ring_idx[q]
                self.ring_idx[q] = (s + 1) % RING
                prev = self.ring_cnt[q][s]
                self.ring_cnt[q][s] = prev + 16
                op.bp = (self.ring[q][s], prev) if prev > 0 else None
                op.token = (self.ring[q][s], prev + 16)
            elif op.sig:
                self.cnt[op.eng] += 1
                op.token = (self.sem[op.eng], self.cnt[op.eng])
        final = []
        for e in ENGS:
            if self.cnt[e] > 0:
                final.append((self.sem[e], self.cnt[e]))
        for q in DMAQ:
            for s in range(RING):
                if self.ring_cnt[q][s] > 0:
                    final.append((self.ring[q][s], self.ring_cnt[q][s]))
        per_eng = {e: [] for e in ENGS}
        for op in ops:
            per_eng[op.eng].append(op)
        self.n_emitted += len(ops)

        def run(eng_name, eng):
            waited = self.waited[eng_name]

            def wait(tok):
                sem, val = tok
                key = id(sem)
                if waited.get(key, 0) >= val:
                    return
                waited[key] = val
                eng.wait_ge(sem, val)

            for op in per_eng[eng_name]:
                for j in op.deps:
                    wait(ops[j].token)
                if op.bp is not None:
                    wait(op.bp)
                ins = op.fn(eng)
                if op.dma:
                    ins.then_inc(op.token[0], 16)
                elif op.sig:
                    ins.then_inc(op.token[0], 1)
            for tok in final:
                wait(tok)

        with self.nc.Block() as block:
            @block.tensor
            def _(e):
                run("pe", e)

            @block.scalar
            def _(e):
                run("act", e)

            @block.vector
            def _(e):
                run("dve", e)

            @block.gpsimd
            def _(e):
                run("pool", e)

            @block.sync
            def _(e):
                run("sp", e)


D_MODEL = 4096
SEQ = 4096
DEPTH = 2
HD = 128
NH = 8
EPS = 1e-6
D_FF = 11008
ROPE_BASE = 10000.0
MOBA_BLOCK = 256
MOBA_TOPK = 3
LRU_C = 8.0
NSA_KVH = 2
CMP_LEN, CMP_STRIDE, CMP_HIDDEN = 32, 16, 256
SEL_BLOCK, SEL_TOPN, WIN = 64, 16, 512
NEG = -30000.0

C_RQ, C_RK, C_RV, C_RG = 0, 1024, 2048, 3072
C_MQ, C_MK, C_MV = 4096, 5120, 6144
C_LX, C_LG = 7168, 8192
C_NQ = 9216
C_NKC, C_NVC, C_NKS, C_NVS, C_NKW, C_NVW = 10240, 10496, 10752, 11008, 11264, 11520
C_NG = 11776
IN_WIDTH = 11800


def host_constants(S):
    c = {}
    half = HD // 2
    inv = ROPE_BASE ** (-np.arange(half, dtype=np.float32) / half)
    ang = np.arange(S, dtype=np.float32)[None, :] * inv[:, None]
    cos = np.cos(ang).astype(np.float32)
    sin = np.sin(ang).astype(np.float32)
    cos2 = np.concatenate([cos, cos], 0)
    sin2 = np.concatenate([-sin, sin], 0)
    sc = HD ** -0.5
    c["rope"] = np.stack([cos2, sin2, cos2 * sc, sin2 * sc], 1).astype(np.float32)
    perm = np.zeros((128, 128), np.float32)
    for d in range(128):
        perm[(d + 64) % 128, d] = 1.0
    ident = np.eye(128, dtype=np.float32)
    c["perm"] = perm.astype(ml_dtypes.bfloat16)
    c["ident"] = ident.astype(ml_dtypes.bfloat16)
    c["identf"] = ident
    gam = 1.0 - np.exp2(-5.0 - np.arange(NH, dtype=np.float64))
    lg = np.log(gam)
    i = np.arange(128)
    diff = i[None, :] - i[:, None]
    dec = np.where(diff[None] >= 0, np.exp(lg[:, None, None] * np.maximum(diff[None], 0)), 0.0)
    c["ret_decT"] = np.ascontiguousarray(dec.transpose(1, 0, 2)).astype(np.float32)
    dfs = np.exp(lg[:, None] * (i[None, :] + 1.0))
    c["ret_dfs"] = np.broadcast_to(dfs[None], (128, NH, 128)).astype(np.float32).copy()
    dte = np.exp(lg[:, None] * (127.0 - i[None, :]))
    c["ret_dte"] = np.ascontiguousarray(dte.T).astype(np.float32)
    c["ret_dchunk"] = [float(np.exp(l * 128.0)) for l in lg]
    k = np.arange(128)[:, None]
    q = np.arange(128)[None, :]
    tri = np.where(k <= q, 0.0, NEG)
    tris = np.where(k > q, 0.0, NEG)
    c["tri4"] = np.tile(tri, (1, 4)).astype(ml_dtypes.bfloat16)
    c["tris4"] = np.tile(tris, (1, 4)).astype(ml_dtypes.bfloat16)
    nb = S // MOBA_BLOCK
    nt = S // 128
    e = np.zeros((16, nt, 128), np.float32)
    for kt in range(nt):
        e[kt // 2, kt, :] = 1.0
    c["moba_E"] = e.astype(ml_dtypes.bfloat16)
    e2 = np.zeros((64, nt, 128), np.float32)
    for kt in range(nt):
        e2[2 * kt, kt, :64] = 1.0
        e2[2 * kt + 1, kt, 64:] = 1.0
    c["nsa_E2"] = e2.astype(ml_dtypes.bfloat16)
    NC = (S - CMP_LEN) // CMP_STRIDE + 1
    NCP = 256
    starts = np.arange(NC) * CMP_STRIDE
    t = np.arange(S)
    cm = ((starts[:, None] + CMP_LEN - 1) <= t[None, :]).astype(np.float32)
    cmp_mask = np.zeros((NCP, S), np.float32)
    cmp_mask[:NC] = cm
    c["nsa_cmask"] = np.ascontiguousarray(cmp_mask.reshape(2, 128, S).transpose(1, 0, 2)).astype(ml_dtypes.bfloat16)
    nsel = S // SEL_BLOCK
    sel_start = np.arange(nsel) * SEL_BLOCK
    ov = ((starts[:, None] < sel_start[None, :] + SEL_BLOCK) & (starts[:, None] + CMP_LEN > sel_start[None, :])).astype(np.float32)
    ovp = np.zeros((NCP, 64), np.float32)
    ovp[:NC, :nsel] = ov
    c["nsa_ov"] = np.ascontiguousarray(ovp.reshape(2, 128, 64).transpose(1, 0, 2)).astype(ml_dtypes.bfloat16)
    qsb = t // SEL_BLOCK
    j = np.arange(64)[None, :]
    forced = (j == 0) | (j == qsb[:, None]) | (j == qsb[:, None] - 1)
    allowed = j <= qsb[:, None]
    mul = (allowed & ~forced).astype(np.float32)
    addt = np.where(forced, 1e9 + 1e4 * (64 - j), np.where(allowed, 0.0, -1e9 - 1e4 * j)).astype(np.float32)
    c["nsa_selmul"] = np.ascontiguousarray(mul.reshape(nt, 128, 64).transpose(1, 0, 2))
    c["nsa_seladd"] = np.ascontiguousarray(addt.reshape(nt, 128, 64).transpose(1, 0, 2))
    return c


CONST_SPECS = {
    "rope": (lambda S: [128, 4, S], F32),
    "perm": (lambda S: [128, 128], BF16),
    "ident": (lambda S: [128, 128], BF16),
    "identf": (lambda S: [128, 128], F32),
    "ret_decT": (lambda S: [128, NH, 128], F32),
    "ret_dfs": (lambda S: [128, NH, 128], F32),
    "ret_dte": (lambda S: [128, NH], F32),
    "tri4": (lambda S: [128, 512], BF16),
    "tris4": (lambda S: [128, 512], BF16),
    "moba_E": (lambda S: [16, S // 128, 128], BF16),
    "nsa_E2": (lambda S: [64, S // 128, 128], BF16),
    "nsa_cmask": (lambda S: [128, 2, S], BF16),
    "nsa_ov": (lambda S: [128, 2, 64], BF16),
    "nsa_selmul": (lambda S: [128, S // 128, 64], F32),
    "nsa_seladd": (lambda S: [128, S // 128, 64], F32),
}


_SBN = [0]


def _sb(nc, st, name, shape, dt):
    _SBN[0] += 1
    return st.enter_context(nc.sbuf_tensor("%s_%d" % (name, _SBN[0]), shape, dt))


def emit_retention(sc, nc, ps, cst, qT_d, kT_d, v_d, g_d, gain_d, y_d, S, heads=range(NH)):
    NT = S // 128
    NG = S // 512
    dchunk = cst["ret_dchunk"]
    with ExitStack() as st:
        rope = _sb(nc, st, "rt_rope", [128, 4, S], F32)
        perm = _sb(nc, st, "rt_perm", [128, 128], BF16)
        ident = _sb(nc, st, "rt_ident", [128, 128], BF16)
        decT = _sb(nc, st, "rt_decT", [128, NH, 128], F32)
        dfs = _sb(nc, st, "rt_dfs", [128, NH, 128], F32)
        dte = _sb(nc, st, "rt_dte", [128, NH], F32)
        gain = _sb(nc, st, "rt_gain", [128, NH * 128], F32)
        qT = _sb(nc, st, "rt_qT", [128, S], BF16)
        kT = _sb(nc, st, "rt_kT", [128, S], BF16)
        qr = _sb(nc, st, "rt_qr", [128, S], BF16)
        qrs = _sb(nc, st, "rt_qrs", [128, S], BF16)
        kr = _sb(nc, st, "rt_kr", [128, S], BF16)
        ktok = _sb(nc, st, "rt_ktok", [128, NT, 128], BF16)
        vsb = _sb(nc, st, "rt_v", [128, NT, 128], BF16)
        vs = _sb(nc, st, "rt_vs", [128, NT, 128], BF16)
        gsb = _sb(nc, st, "rt_g", [128, NT, 128], F32)
        yout = _sb(nc, st, "rt_yout", [128, NT, 128], BF16)
        t1 = [_sb(nc, st, "rt_t1_%d" % i, [128, 512], F32) for i in range(2)]
        t2 = [_sb(nc, st, "rt_t2_%d" % i, [128, 512], F32) for i in range(2)]
        state = _sb(nc, st, "rt_state", [128, 128], F32)
        state_bf = _sb(nc, st, "rt_state_bf", [128, 128], BF16)
        s_sb = [_sb(nc, st, "rt_s_%d" % i, [128, 128], BF16) for i in range(2)]
        stats = _sb(nc, st, "rt_stats", [128, 8], F32)
        mv = _sb(nc, st, "rt_mv", [128, 4], F32)
        rstd = _sb(nc, st, "rt_rstd", [128, 1], F32)
        yn = [_sb(nc, st, "rt_yn_%d" % i, [128, 128], F32) for i in range(2)]
        epsc = _sb(nc, st, "rt_eps", [128, 1], F32)
        sc.add("pool", lambda e: e.memset(epsc[:], EPS), w=["epsc"])

        sc.dma("sp", rope[:], cst["rope"], w=["rope"])
        sc.dma("sp", perm[:], cst["perm"], w=["perm"])
        sc.dma("sp", ident[:], cst["ident"], w=["ident"])
        sc.dma("sp", decT[:], cst["ret_decT"], w=["decT"])
        sc.dma("sp", dfs[:], cst["ret_dfs"], w=["dfs"])
        sc.dma("sp", dte[:], cst["ret_dte"], w=["dte"])
        sc.dma("sp", gain[:], gain_d.partition_broadcast(128), w=["gain"])
        psT = ps[2][:].bitcast(BF16)

        for h in heads:
            hs = slice(h * 128, (h + 1) * 128)
            sc.dma("sp", qT[:], qT_d[hs, :], w=["qT"])
            sc.dma("sp", kT[:], kT_d[hs, :], w=["kT"])
            sc.dma("sp", vsb[:], v_d[:, hs].rearrange("(n j) e -> j n e", j=128), w=["v"])
            sc.dma("sp", gsb[:], g_d[:, hs].rearrange("(n j) e -> j n e", j=128), w=["g"])
            cnt = 0
            for (src, skey, ci, si, dst, dkey, do_s) in ((qT, "qT", 0, 1, qr, "qr", True), (kT, "kT", 2, 3, kr, "kr", False)):
                for g in range(NG):
                    cs = slice(g * 512, (g + 1) * 512)
                    b = cnt % 2
                    cnt += 1
                    sc.add("pe", lambda e, b=b, src=src, cs=cs: e.matmul(ps[b][:, :], perm[:], src[:, cs], start=True, stop=True),
                           r=[skey, "perm"], w=[("ps", b)])
                    sc.add("dve", lambda e, b=b, src=src, cs=cs, ci=ci: e.tensor_tensor(out=t1[b][:], in0=src[:, cs], in1=rope[:, ci, cs], op=ALU.mult),
                           r=[skey, "rope"], w=[("t1", b)])
                    sc.add("dve", lambda e, b=b, cs=cs, si=si: e.tensor_tensor(out=t2[b][:], in0=ps[b][:, :], in1=rope[:, si, cs], op=ALU.mult),
                           r=[("ps", b), "rope"], w=[("t2", b)])
                    sc.add("pool", lambda e, b=b: e.tensor_tensor(out=t1[b][:], in0=t1[b][:], in1=t2[b][:], op=ALU.add),
                           r=[("t1", b), ("t2", b)], w=[("t1", b)])
                    sc.add("act", lambda e, b=b, dst=dst, cs=cs: e.copy(out=dst[:, cs], in_=t1[b][:]),
                           r=[("t1", b)], w=[dkey])
                    if do_s:
                        sc.add("pool", lambda e, b=b, cs=cs, h=h: e.tensor_tensor(
                            out=qrs[:, cs].rearrange("p (a c) -> p a c", c=128), in0=t1[b][:].rearrange("p (a c) -> p a c", c=128),
                            in1=dfs[:, h, :].unsqueeze(1).broadcast_to([128, 4, 128]), op=ALU.mult),
                            r=[("t1", b), "dfs"], w=["qrs"])
            for n in range(NT):
                sl = n % 8
                sc.add("pe", lambda e, n=n, sl=sl: e.transpose(psT[:, sl * 128:(sl + 1) * 128], kr[:, n * 128:(n + 1) * 128], ident[:]),
                       r=["kr", "ident"], w=[("ps", 2)])
                if sl == 7 or n == NT - 1:
                    n0 = n - sl
                    sc.add("act", lambda e, n0=n0, n=n, sl=sl: e.copy(
                        out=ktok[:, n0:n + 1, :], in_=psT[:, 0:(sl + 1) * 128].rearrange("p (a c) -> p a c", c=128)),
                        r=[("ps", 2)], w=["ktok"])
            sc.add("dve", lambda e, h=h: e.tensor_scalar(out=vs[:], in0=vsb[:], scalar1=dte[:, h:h + 1], scalar2=None, op0=ALU.mult),
                   r=["v", "dte"], w=["vs"])
            sc.add("act", lambda e: e.activation(out=gsb[:], in_=gsb[:], func=AF.Silu), r=["g"], w=["g"])
            units = []
            for n in range(NT):
                c = slice(n * 128, (n + 1) * 128)
                b = n % 2
                sbk = (3, 2)[n % 2]
                u = Unit()
                units.append(u)

                def stage_s(c=c, b=b, sbk=sbk, h=h):
                    sc.add("pe", lambda e: e.matmul(ps[sbk][:, :128], kr[:, c], qr[:, c], start=True, stop=True),
                           r=["kr", "qr"], w=[("ps", sbk)])
                    sc.add("dve", lambda e: e.tensor_tensor(out=s_sb[b][:], in0=ps[sbk][:, :128], in1=decT[:, h, :], op=ALU.mult),
                           r=[("ps", sbk), "decT"], w=[("s", b)])
                u.qk.append(stage_s)

                def stage_y(c=c, b=b, n=n, h=h):
                    sc.add("pe", lambda e: e.matmul(ps[6 + b][:, :128], ktok[:, n, :], vs[:, n, :], start=True, stop=True),
                           r=["ktok", "vs"], w=[("ps", 6 + b)])
                    sc.add("pe", lambda e: e.matmul(ps[4 + b][:, :128], s_sb[b][:], vsb[:, n, :], start=True, stop=(n == 0)),
                           r=[("s", b), "v"], w=[("ps", 4 + b)])
                    if n > 0:
                        sc.add("pe", lambda e: e.matmul(ps[4 + b][:, :128], qrs[:, c], state_bf[:], start=False, stop=True),
                               r=["qrs", "state_bf"], w=[("ps", 4 + b)])
                    if n == 0:
                        sc.add("dve", lambda e: e.tensor_copy(out=state[:], in_=ps[6 + b][:, :128]), r=[("ps", 6 + b)], w=["state"])
                    else:
                        sc.add("dve", lambda e: e.scalar_tensor_tensor(out=state[:], in0=state[:], scalar=dchunk[h], in1=ps[6 + b][:, :128],
                                                                       op0=ALU.mult, op1=ALU.add),
                               r=[("ps", 6 + b), "state"], w=["state"])
                    sc.add("act", lambda e: e.copy(out=state_bf[:], in_=state[:]), r=["state"], w=["state_bf"])
                u.pv.append(stage_y)

                def stage_e(b=b, n=n, hs=hs):
                    sc.add("dve", lambda e: e.bn_stats(out=stats[:, 0:6], in_=ps[4 + b][:, :128]), r=[("ps", 4 + b)], w=["stats"])
                    sc.add("dve", lambda e: e.bn_aggr(out=mv[:, 0:2], in_=stats[:, 0:6]), r=["stats"], w=["mv"])
                    sc.add("act", lambda e: e.activation(out=rstd[:], in_=mv[:, 1:2], func=AF.Sqrt, bias=epsc[:, 0:1], scale=1.0),
                           r=["mv", "epsc"], w=["rstd"])
                    sc.add("dve", lambda e: e.reciprocal(out=rstd[:], in_=rstd[:]), r=["rstd"], w=["rstd"])
                    sc.add("dve", lambda e: e.tensor_scalar(out=yn[b][:], in0=ps[4 + b][:, :128], scalar1=mv[:, 0:1], scalar2=rstd[:, 0:1],
                                                            op0=ALU.subtract, op1=ALU.mult),
                           r=[("ps", 4 + b), "mv", "rstd"], w=[("yn", b)])
                    sc.add("pool", lambda e: e.tensor_tensor(out=yn[b][:], in0=yn[b][:], in1=gain[:, hs], op=ALU.mult),
                           r=[("yn", b), "gain"], w=[("yn", b)])
                    sc.add("pool", lambda e: e.tensor_tensor(out=yout[:, n, :], in0=yn[b][:], in1=gsb[:, n, :], op=ALU.mult),
                           r=[("yn", b), "g"], w=["yout"])
                u.post.append(stage_e)
            run_pipelined(units, depth=1)
            sc.dma("sp", y_d[:, hs].rearrange("(n j) e -> j n e", j=128), yout[:], r=["yout"], w=[("y_d", h)])
        sc.flush()


class Unit:
    __slots__ = ("pre", "qk", "act", "pv", "post")

    def __init__(self):
        self.pre, self.qk, self.act, self.pv, self.post = [], [], [], [], []


def run_pipelined(units, depth=1):
    n = len(units)
    for idx in range(n + depth):
        if idx < n:
            for f in units[idx].pre:
                f()
            for f in units[idx].qk:
                f()
        if idx >= depth:
            u = units[idx - depth]
            for f in u.act:
                f()
            for f in u.pv:
                f()
            for f in u.post:
                f()


def emit_moba(sc, nc, ps, cst, qT_d, kT_d, v_d, y_d, S, heads=range(NH), side=None):
    NT = S // 128
    NB = S // MOBA_BLOCK
    scale = HD ** -0.5
    with ExitStack() as st:
        ident = _sb(nc, st, "mb_ident", [128, 128], BF16)
        identf = _sb(nc, st, "mb_identf", [128, 128], F32)
        tri = _sb(nc, st, "mb_tri", [128, 512], BF16)
        E = _sb(nc, st, "mb_E", [16, NT, 128], BF16)
        qT = [_sb(nc, st, "mb_qT%d" % i, [128, S], BF16) for i in range(2)]
        kT = [_sb(nc, st, "mb_kT%d" % i, [128, S], BF16) for i in range(2)]
        vaug = [_sb(nc, st, "mb_vaug%d" % i, [128, NT, 130], BF16) for i in range(2)]
        kmf = [_sb(nc, st, "mb_kmf%d" % i, [128, 16], F32) for i in range(2)]
        kmb = [_sb(nc, st, "mb_kmb%d" % i, [128, 16], BF16) for i in range(2)]
        gpad = [_sb(nc, st, "mb_gpad%d" % i, [128, 16], F32) for i in range(2)]
        max8 = [_sb(nc, st, "mb_max8%d" % i, [128, 8], F32) for i in range(2)]
        negm = [_sb(nc, st, "mb_negm%d" % i, [128, 16], F32) for i in range(2)]
        negmT = [_sb(nc, st, "mb_negmT%d" % i, [16, 128], BF16) for i in range(2)]
        p_sb = [_sb(nc, st, "mb_p%d" % i, [128, 512], BF16) for i in range(3)]
        rden = [_sb(nc, st, "mb_rden%d" % i, [128, 1], F32) for i in range(2)]
        yout = [_sb(nc, st, "mb_yout%d" % i, [128, NT, 128], BF16) for i in range(2)]
        sc.dma("sp", ident[:], cst["ident"], w=["ident"])
        sc.dma("sp", identf[:], cst["identf"], w=["identf"])
        sc.dma("sp", tri[:], cst["tri4"], w=["tri"])
        sc.dma("sp", E[:], cst["moba_E"], w=["E"])
        for i in range(2):
            sc.add("pool", lambda e, i=i: e.memset(vaug[i][:, :, 128:130], 1.0), w=[("vones", i)])
            sc.add("pool", lambda e, i=i: e.memset(kmf[i][:], 0.0), w=[("kmf", i)])
        units = []
        pcount = 0
        for h in heads:
            hs = slice(h * 128, (h + 1) * 128)
            hb = h % 2
            head_pre = []

            def load_head(hb=hb, hs=hs):
                sc.dma("sp", qT[hb][:], qT_d[hs, :], w=[("qT", hb)])
                sc.dma("sp", kT[hb][:], kT_d[hs, :], w=[("kT", hb)])
                sc.dma("sp", vaug[hb][:, :, 0:128], v_d[:, hs].rearrange("(n j) e -> j n e", j=128), w=[("v", hb)])
                sc.add("dve", lambda e: e.tensor_reduce(out=kmf[hb][:, 0:NB], in_=kT[hb][:].rearrange("p (n t) -> p n t", t=MOBA_BLOCK),
                                                        axis=AX.X, op=ALU.add), r=[("kT", hb)], w=[("kmf", hb)])
                sc.add("act", lambda e: e.activation(out=kmb[hb][:], in_=kmf[hb][:], func=AF.Copy, scale=1.0 / MOBA_BLOCK), r=[("kmf", hb)], w=[("kmb", hb)])
            head_pre.append(load_head)

            def mask_part1(i, hb=hb):
                mb = i % 2
                qs = slice(i * 128, (i + 1) * 128)
                qb = i // 2
                sc.add("pe", lambda e: e.matmul(ps[0][:, 0:16], qT[hb][:, qs], kmb[hb][:], start=True, stop=True),
                       r=[("qT", hb), ("kmb", hb)], w=[("ps", 0)])
                sc.add("pool", lambda e: e.memset(gpad[mb][:], -1e30), w=[("gpad", mb)])
                sc.add("dve", lambda e: e.tensor_copy(out=gpad[mb][:, 0:qb], in_=ps[0][:, 0:qb]), r=[("ps", 0)], w=[("gpad", mb)])
                sc.add("dve", lambda e: e.max(out=max8[mb][:], in_=gpad[mb][:]), r=[("gpad", mb)], w=[("max8", mb)])
                sc.add("dve", lambda e: e.tensor_scalar(out=negm[mb][:], in0=gpad[mb][:], scalar1=max8[mb][:, 2:3], scalar2=NEG,
                                                        op0=ALU.is_lt, op1=ALU.mult), r=[("gpad", mb), ("max8", mb)], w=[("negm", mb)])

            def mask_part2(i):
                mb = i % 2
                sc.add("pe", lambda e: e.transpose(ps[1][0:16, 0:128], negm[mb][:], identf[:]), r=[("negm", mb), "identf"], w=[("ps", 1)])
                sc.add("act", lambda e: e.copy(out=negmT[mb][:], in_=ps[1][0:16, 0:128]), r=[("ps", 1)], w=[("negmT", mb)])

            first_units = {}
            last_units = {}
            for i in range(NT):
                qs = slice(i * 128, (i + 1) * 128)
                qb = i // 2
                mb = i % 2
                ob = 5 + (i % 2)
                kts = list(range(i + 1))
                ngr = (len(kts) + 3) // 4
                for gi in range(ngr):
                    grp = kts[gi * 4:gi * 4 + 4]
                    u = Unit()
                    units.append(u)
                    if gi == 0:
                        first_units[i] = u
                        if i == 0:
                            u.pre.extend(head_pre)
                    if gi == ngr - 1:
                        last_units[i] = u
                    sb_ = (3, 4, 2)[pcount % 3]
                    pb = pcount % 3
                    pcount += 1
                    for a, kt in enumerate(grp):
                        ks = slice(kt * 128, (kt + 1) * 128)
                        cs = slice(a * 128, (a + 1) * 128)
                        masked = kt < 2 * qb
                        diag = kt == i

                        def qk(sb_=sb_, cs=cs, ks=ks, qs=qs, masked=masked, diag=diag, kt=kt, hb=hb, mb=mb):
                            sc.add("pe", lambda e: e.matmul(ps[sb_][:, cs], kT[hb][:, ks], qT[hb][:, qs], start=True, stop=not (masked or diag)),
                                   r=[("kT", hb), ("qT", hb)], w=[("ps", sb_)])
                            if masked:
                                sc.add("pe", lambda e: e.matmul(ps[sb_][:, cs], E[:, kt, :], negmT[mb][:], start=False, stop=True),
                                       r=["E", ("negmT", mb)], w=[("ps", sb_)])
                            elif diag:
                                sc.add("pe", lambda e: e.matmul(ps[sb_][:, cs], ident[:], tri[:, 0:128], start=False, stop=True),
                                       r=["ident", "tri"], w=[("ps", sb_)])
                        u.qk.append(qk)
                    w_ = len(grp) * 128

                    def act(sb_=sb_, pb=pb, w_=w_):
                        sc.add("act", lambda e: e.activation(out=p_sb[pb][:, 0:w_], in_=ps[sb_][:, 0:w_], func=AF.Exp, scale=scale),
                               r=[("ps", sb_)], w=[("p", pb)])
                    u.act.append(act)
                    for a, kt in enumerate(grp):
                        cs = slice(a * 128, (a + 1) * 128)

                        def pv(ob=ob, pb=pb, cs=cs, kt=kt, i=i, hb=hb):
                            sc.add("pe", lambda e: e.matmul(ps[ob][:, 0:129], p_sb[pb][:, cs], vaug[hb][:, kt, 0:129], start=(kt == 0), stop=(kt == i)),
                                   r=[("p", pb), ("v", hb), ("vones", hb)], w=[("ps", ob)])
                        u.pv.append(pv)

                def post(ob=ob, i=i, hb=hb, mb=mb, hs=hs):
                    sc.add("dve", lambda e: e.reciprocal(out=rden[mb][:], in_=ps[ob][:, 128:129]), r=[("ps", ob)], w=[("rden", mb)])
                    sc.add("dve", lambda e: e.tensor_scalar(out=yout[hb][:, i, :], in0=ps[ob][:, 0:128], scalar1=rden[mb][:, 0:1], scalar2=None, op0=ALU.mult),
                           r=[("ps", ob), ("rden", mb)], w=[("yout", hb)])
                    if i == NT - 1:
                        sc.dma("sp", y_d[:, hs].rearrange("(n j) e -> j n e", j=128), yout[hb][:], r=[("yout", hb)], w=[("y_d", hs.start)])
                last_units[i].post.append(post)
            for i in range(2, NT):
                first_units[i - 1].pre.append(lambda i=i, f=mask_part1: f(i))
                last_units[i - 1].pre.append(lambda i=i, f=mask_part2: f(i))
        if side is not None:
            sops = side(st)
            nu = len(units)
            for j, f in enumerate(sops):
                units[min(nu - 1, (j * nu) // len(sops))].pre.append(f)
        run_pipelined(units, depth=2)
        sc.flush()


def emit_moba2(sc, nc, ps, cst, qT_d, kT_d, v_d, y_d, S, heads=range(NH), side=None):
    NT = S // 128
    NB = S // MOBA_BLOCK
    NG = NT // 4
    scale = HD ** -0.5
    with ExitStack() as st:
        ident = _sb(nc, st, "mb_ident", [128, 128], BF16)
        identf = _sb(nc, st, "mb_identf", [128, 128], F32)
        tri = _sb(nc, st, "mb_tri", [128, 512], BF16)
        zeroT = _sb(nc, st, "mb_zeroT", [128, 128], BF16)
        E = _sb(nc, st, "mb_E", [16, NT, 128], BF16)
        qT = [_sb(nc, st, "mb_qT%d" % i, [128, S], BF16) for i in range(2)]
        kT = [_sb(nc, st, "mb_kT%d" % i, [128, S], BF16) for i in range(2)]
        vaug = [_sb(nc, st, "mb_vaug%d" % i, [128, NT, 130], BF16) for i in range(2)]
        kmf = [_sb(nc, st, "mb_kmf%d" % i, [128, 16], F32) for i in range(2)]
        kmb = [_sb(nc, st, "mb_kmb%d" % i, [128, 16], BF16) for i in range(2)]
        gpad = [_sb(nc, st, "mb_gpad%d" % i, [128, 16], F32) for i in range(2)]
        max8 = [_sb(nc, st, "mb_max8%d" % i, [128, 8], F32) for i in range(2)]
        negm = [_sb(nc, st, "mb_negm%d" % i, [128, 16], F32) for i in range(2)]
        negmT4 = [_sb(nc, st, "mb_negmT4%d" % i, [16, 512], BF16) for i in range(2)]
        p_sb = [_sb(nc, st, "mb_p%d" % i, [128, 512], BF16) for i in range(2)]
        rden = [_sb(nc, st, "mb_rden%d" % i, [128, 1], F32) for i in range(2)]
        yout = [_sb(nc, st, "mb_yout%d" % i, [128, NT, 128], BF16) for i in range(2)]
        sc.dma("sp", ident[:], cst["ident"], w=["ident"])
        sc.dma("sp", identf[:], cst["identf"], w=["identf"])
        sc.dma("sp", tri[:], cst["tri4"], w=["tri"])
        sc.dma("sp", E[:], cst["moba_E"], w=["E"])
        sc.add("pool", lambda e: e.memset(zeroT[:], 0.0), w=["zeroT"])
        for i in range(2):
            sc.add("pool", lambda e, i=i: e.memset(vaug[i][:, :, 128:130], 1.0), w=[("vones", i)])
            sc.add("pool", lambda e, i=i: e.memset(kmf[i][:], 0.0), w=[("kmf", i)])
            sc.add("pool", lambda e, i=i: e.memset(negmT4[i][:], 0.0), w=[("negmT4", i)])
        units = []
        pcount = 0
        for h in heads:
            hs = slice(h * 128, (h + 1) * 128)
            hb = h % 2

            def load_head(hb=hb, hs=hs):
                sc.dma("sp", qT[hb][:], qT_d[hs, :], w=[("qT", hb)])
                sc.dma("sp", kT[hb][:], kT_d[hs, :], w=[("kT", hb)])
                sc.dma("sp", vaug[hb][:, :, 0:128], v_d[:, hs].rearrange("(n j) e -> j n e", j=128), w=[("v", hb)])
                sc.add("dve", lambda e: e.tensor_reduce(out=kmf[hb][:, 0:NB], in_=kT[hb][:].rearrange("p (n t) -> p n t", t=MOBA_BLOCK),
                                                        axis=AX.X, op=ALU.add), r=[("kT", hb)], w=[("kmf", hb)])
                sc.add("act", lambda e: e.activation(out=kmb[hb][:], in_=kmf[hb][:], func=AF.Copy, scale=1.0 / MOBA_BLOCK), r=[("kmf", hb)], w=[("kmb", hb)])

            def mask_part1(i, hb=hb):
                mb = i % 2
                qs = slice(i * 128, (i + 1) * 128)
                qb = i // 2
                sc.add("pe", lambda e: e.matmul(ps[0][:, 0:16], qT[hb][:, qs], kmb[hb][:], start=True, stop=True),
                       r=[("qT", hb), ("kmb", hb)], w=[("ps", 0)])
                sc.add("pool", lambda e: e.memset(gpad[mb][:], -1e30), w=[("gpad", mb)])
                sc.add("dve", lambda e: e.tensor_copy(out=gpad[mb][:, 0:qb], in_=ps[0][:, 0:qb]), r=[("ps", 0)], w=[("gpad", mb)])
                sc.add("dve", lambda e: e.max(out=max8[mb][:], in_=gpad[mb][:]), r=[("gpad", mb)], w=[("max8", mb)])
                sc.add("dve", lambda e: e.tensor_scalar(out=negm[mb][:], in0=gpad[mb][:], scalar1=max8[mb][:, 2:3], scalar2=NEG,
                                                        op0=ALU.is_lt, op1=ALU.mult), r=[("gpad", mb), ("max8", mb)], w=[("negm", mb)])

            def mask_part2(i):
                mb = i % 2
                gp = (i // 4) % 2
                j = i % 4
                sc.add("pe", lambda e: e.transpose(ps[0][0:16, 128:256], negm[mb][:], identf[:]), r=[("negm", mb), "identf"], w=[("ps", 0)])
                sc.add("act", lambda e: e.copy(out=negmT4[gp][:, j * 128:(j + 1) * 128], in_=ps[0][0:16, 128:256]), r=[("ps", 0)], w=[("negmT4", gp)])

            group_units = {}
            for G in range(NG):
                i0 = 4 * G
                gp = G % 2
                obanks = (5, 6) if gp == 0 else (1, 2)
                gl = []
                group_units[G] = gl
                first_pv = [True]

                def zero_banks(obanks=obanks):
                    for ob in obanks:
                        sc.add("pe", lambda e, ob=ob: e.matmul(ps[ob][:, :], zeroT[:], tri[:], start=True, stop=False), r=["zeroT", "tri"], w=[("ps", ob)])
                for kt in range(i0):
                    u = Unit()
                    units.append(u)
                    gl.append(u)
                    sb_ = 3 + (pcount % 2)
                    pb = pcount % 2
                    pcount += 1
                    ks = slice(kt * 128, (kt + 1) * 128)

                    def qk(sb_=sb_, ks=ks, kt=kt, hb=hb, gp=gp, i0=i0):
                        sc.add("pe", lambda e: e.matmul(ps[sb_][:, :], kT[hb][:, ks], qT[hb][:, i0 * 128:(i0 + 4) * 128], start=True, stop=False),
                               r=[("kT", hb), ("qT", hb)], w=[("ps", sb_)])
                        sc.add("pe", lambda e: e.matmul(ps[sb_][:, :], E[:, kt, :], negmT4[gp][:], start=False, stop=True),
                               r=["E", ("negmT4", gp)], w=[("ps", sb_)])
                    u.qk.append(qk)

                    def act(sb_=sb_, pb=pb):
                        sc.add("act", lambda e: e.activation(out=p_sb[pb][:], in_=ps[sb_][:, :], func=AF.Exp, scale=scale), r=[("ps", sb_)], w=[("p", pb)])
                    u.act.append(act)
                    if first_pv[0]:
                        u.pv.append(zero_banks)
                        first_pv[0] = False

                    def pv(pb=pb, kt=kt, hb=hb, obanks=obanks):
                        for j in range(4):
                            ob = obanks[j // 2]
                            oc = (j % 2) * 256
                            sc.add("pe", lambda e, j=j, ob=ob, oc=oc: e.matmul(ps[ob][:, oc:oc + 129], p_sb[pb][:, j * 128:(j + 1) * 128], vaug[hb][:, kt, 0:129],
                                                                               start=False, stop=False),
                                   r=[("p", pb), ("v", hb), ("vones", hb)], w=[("ps", ob)])
                    u.pv.append(pv)
                for j in range(4):
                    i = i0 + j
                    qs = slice(i * 128, (i + 1) * 128)
                    qb = i // 2
                    mb = i % 2
                    ob = obanks[j // 2]
                    oc = (j % 2) * 256
                    kts = list(range(i0, i + 1))
                    u = Unit()
                    units.append(u)
                    gl.append(u)
                    if G == 0 and j == 0:
                        u.pre.append(load_head)
                    sb_ = 3 + (pcount % 2)
                    pb = pcount % 2
                    pcount += 1
                    for a, kt in enumerate(kts):
                        ks = slice(kt * 128, (kt + 1) * 128)
                        cs = slice(a * 128, (a + 1) * 128)
                        masked = kt < 2 * qb
                        diag = kt == i

                        def qk(sb_=sb_, cs=cs, ks=ks, qs=qs, masked=masked, diag=diag, kt=kt, hb=hb, gp=gp, j=j):
                            sc.add("pe", lambda e: e.matmul(ps[sb_][:, cs], kT[hb][:, ks], qT[hb][:, qs], start=True, stop=not (masked or diag)),
                                   r=[("kT", hb), ("qT", hb)], w=[("ps", sb_)])
                            if masked:
                                sc.add("pe", lambda e: e.matmul(ps[sb_][:, cs], E[:, kt, :], negmT4[gp][:, j * 128:(j + 1) * 128], start=False, stop=True),
                                       r=["E", ("negmT4", gp)], w=[("ps", sb_)])
                            elif diag:
                                sc.add("pe", lambda e: e.matmul(ps[sb_][:, cs], ident[:], tri[:, 0:128], start=False, stop=True),
                                       r=["ident", "tri"], w=[("ps", sb_)])
                        u.qk.append(qk)
                    w_ = len(kts) * 128

                    def act(sb_=sb_, pb=pb, w_=w_):
                        sc.add("act", lambda e: e.activation(out=p_sb[pb][:, 0:w_], in_=ps[sb_][:, 0:w_], func=AF.Exp, scale=scale),
                               r=[("ps", sb_)], w=[("p", pb)])
                    u.act.append(act)
                    if first_pv[0]:
                        u.pv.append(zero_banks)
                        first_pv[0] = False
                    for a, kt in enumerate(kts):
                        cs = slice(a * 128, (a + 1) * 128)

                        def pv(ob=ob, oc=oc, pb=pb, cs=cs, kt=kt, i=i, hb=hb):
                            sc.add("pe", lambda e: e.matmul(ps[ob][:, oc:oc + 129], p_sb[pb][:, cs], vaug[hb][:, kt, 0:129], start=False, stop=(kt == i)),
                                   r=[("p", pb), ("v", hb), ("vones", hb)], w=[("ps", ob)])
                        u.pv.append(pv)

                    def post(ob=ob, oc=oc, i=i, hb=hb, mb=mb, hs=hs):
                        sc.add("dve", lambda e: e.reciprocal(out=rden[mb][:], in_=ps[ob][:, oc + 128:oc + 129]), r=[("ps", ob)], w=[("rden", mb)])
                        sc.add("dve", lambda e: e.tensor_scalar(out=yout[hb][:, i, :], in0=ps[ob][:, oc:oc + 128], scalar1=rden[mb][:, 0:1], scalar2=None, op0=ALU.mult),
                               r=[("ps", ob), ("rden", mb)], w=[("yout", hb)])
                        if i == NT - 1:
                            sc.dma("sp", y_d[:, hs].rearrange("(n j) e -> j n e", j=128), yout[hb][:], r=[("yout", hb)], w=[("y_d", hs.start)])
                    u.post.append(post)
            g0 = group_units[0]
            g0[0].pre.append(lambda f=mask_part1: f(2))
            g0[1].pre.append(lambda f=mask_part2: f(2))
            g0[1].pre.append(lambda f=mask_part1: f(3))
            g0[2].pre.append(lambda f=mask_part2: f(3))
            for G in range(NG - 1):
                gl = group_units[G]
                n = len(gl)
                for j in range(4):
                    i = 4 * (G + 1) + j
                    a = min(n - 1, (j * n) // 4)
                    b = min(n - 1, a + max(1, n // 8))
                    if G == 0:
                        a, b = j, min(3, j + 1)
                    gl[a].pre.append(lambda i=i, f=mask_part1: f(i))
                    gl[b].pre.append(lambda i=i, f=mask_part2: f(i))
        if side is not None:
            sops = side(st)
            nu = len(units)
            for j, f in enumerate(sops):
                units[min(nu - 1, (j * nu) // len(sops))].pre.append(f)
        run_pipelined(units, depth=1)
        sc.flush()


def _col(ap1d, lo):
    return ap1d[lo:lo + 128].rearrange("(c o) -> c o", o=1)


def lru_side(sc, nc, ps, st, lxT_d, lgT_d, conv_w_d, conv_b_d, wa_d, ba_d, wx_d, bx_d, lam_d, yT_d, S, blocks=range(8), bank=7):
    GK = 1.5957691216057308
    CW = min(1024, S)
    xpad = _sb(nc, st, "lr_xpad", [128, S + 4], F32)
    xc = _sb(nc, st, "lr_xc", [128, S], F32)
    xcb = _sb(nc, st, "lr_xcb", [128, S], BF16)
    r_sb = _sb(nc, st, "lr_r", [128, S], F32)
    i_sb = _sb(nc, st, "lr_i", [128, S], F32)
    lg = _sb(nc, st, "lr_lg", [128, S], F32)
    t_sb = _sb(nc, st, "lr_t", [128, S], F32)
    yo = _sb(nc, st, "lr_yo", [128, S], BF16)
    wa = _sb(nc, st, "lr_wa", [128, 128], BF16)
    wx = _sb(nc, st, "lr_wx", [128, 128], BF16)
    par = _sb(nc, st, "lr_par", [128, 8], F32)
    sp = _sb(nc, st, "lr_sp", [128, 2], F32)
    one = _sb(nc, st, "lr_one", [128, 1], F32)
    h_sb = xc
    pk = ("ps", bank)
    ops = []

    def init():
        sc.add("pool", lambda e: e.memset(xpad[:, 0:3], 0.0), w=["lr_xpad0"])
        sc.add("pool", lambda e: e.memset(one[:], 1.0), w=["lr_one"])
    ops.append(init)
    for c in blocks:
        lo = c * 128
        cs = slice(lo, lo + 128)

        def loads(c=c, lo=lo, cs=cs):
            sc.dma("sp", xpad[:, 3:3 + S], lxT_d[cs, :], w=["lr_xpad"])
            sc.dma("sp", lg[:], lgT_d[cs, :], w=["lr_lg"])
            for tap in range(4):
                sc.dma("sp", par[:, tap:tap + 1], _col(conv_w_d[tap], lo), w=["lr_par"])
            for j, d in enumerate((conv_b_d, ba_d, bx_d, lam_d)):
                sc.dma("sp", par[:, 4 + j:5 + j], _col(d, lo), w=["lr_par"])
            sc.dma("pool", wa[:], wa_d[c], w=["lr_wa"])
            sc.dma("pool", wx[:], wx_d[c], w=["lr_wx"])
        ops.append(loads)

        def spchain():
            sc.add("act", lambda e: e.activation(out=sp[:, 0:1], in_=par[:, 7:8], func=AF.Exp, scale=-1.0), r=["lr_par"], w=["lr_sp0"])
            sc.add("act", lambda e: e.activation(out=sp[:, 1:2], in_=sp[:, 0:1], func=AF.Ln, bias=one[:, 0:1], scale=1.0), r=["lr_sp0", "lr_one"], w=["lr_sp1"])
            sc.add("dve", lambda e: e.tensor_scalar(out=sp[:, 1:2], in0=sp[:, 1:2], scalar1=-LRU_C, scalar2=None, op0=ALU.mult), r=["lr_sp1"], w=["lr_sp1"])
        ops.append(spchain)
        for k0 in range(0, S, CW):
            ks = slice(k0, k0 + CW)

            def conv(k0=k0, ks=ks):
                sc.add("dve", lambda e: e.tensor_scalar(out=xc[:, ks], in0=xpad[:, k0:k0 + CW], scalar1=par[:, 0:1], scalar2=par[:, 4:5], op0=ALU.mult, op1=ALU.add),
                       r=["lr_xpad", "lr_xpad0", "lr_par"], w=["lr_xc"])
                for tap in range(1, 4):
                    sc.add("dve", lambda e, tap=tap: e.scalar_tensor_tensor(out=xc[:, ks], in0=xpad[:, k0 + tap:k0 + tap + CW], scalar=par[:, tap:tap + 1], in1=xc[:, ks],
                                                                            op0=ALU.mult, op1=ALU.add),
                           r=["lr_xpad", "lr_xpad0", "lr_par", "lr_xc"], w=["lr_xc"])
                sc.add("act", lambda e: e.copy(out=xcb[:, ks], in_=xc[:, ks]), r=["lr_xc"], w=["lr_xcb"])
            ops.append(conv)
            for g0 in range(k0, k0 + CW, 512):
                gs = slice(g0, g0 + 512)

                def gates(gs=gs):
                    sc.add("pe", lambda e: e.matmul(ps[bank][:, :], wa[:], xcb[:, gs], start=True, stop=True), r=["lr_wa", "lr_xcb"], w=[pk])
                    sc.add("act", lambda e: e.activation(out=r_sb[:, gs], in_=ps[bank][:, :], func=AF.Sigmoid, bias=par[:, 5:6], scale=1.0),
                           r=[pk, "lr_par"], w=["lr_r"])
                    sc.add("pe", lambda e: e.matmul(ps[bank][:, :], wx[:], xcb[:, gs], start=True, stop=True), r=["lr_wx", "lr_xcb"], w=[pk])
                    sc.add("act", lambda e: e.activation(out=i_sb[:, gs], in_=ps[bank][:, :], func=AF.Sigmoid, bias=par[:, 6:7], scale=1.0),
                           r=[pk, "lr_par"], w=["lr_i"])
                ops.append(gates)

            def recur(k0=k0, ks=ks):
                sc.add("act", lambda e: e.activation(out=r_sb[:, ks], in_=r_sb[:, ks], func=AF.Exp, scale=sp[:, 1:2]), r=["lr_r", "lr_sp1"], w=["lr_r"])
                sc.add("dve", lambda e: e.tensor_tensor(out=t_sb[:, ks], in0=r_sb[:, ks], in1=r_sb[:, ks], op=ALU.mult), r=["lr_r"], w=["lr_t"])
                sc.add("dve", lambda e: e.tensor_scalar(out=t_sb[:, ks], in0=t_sb[:, ks], scalar1=-1.0, scalar2=1.0, op0=ALU.mult, op1=ALU.add), r=["lr_t"], w=["lr_t"])
                sc.add("dve", lambda e: e.tensor_scalar_max(out=t_sb[:, ks], in0=t_sb[:, ks], scalar1=0.0), r=["lr_t"], w=["lr_t"])
                sc.add("act", lambda e: e.activation(out=t_sb[:, ks], in_=t_sb[:, ks], func=AF.Sqrt), r=["lr_t"], w=["lr_t"])
                sc.add("pool", lambda e: e.tensor_tensor(out=i_sb[:, ks], in0=i_sb[:, ks], in1=xc[:, ks], op=ALU.mult), r=["lr_i", "lr_xc"], w=["lr_i"])
                sc.add("dve", lambda e: e.tensor_tensor(out=i_sb[:, ks], in0=i_sb[:, ks], in1=t_sb[:, ks], op=ALU.mult), r=["lr_i", "lr_t"], w=["lr_i"])
                init_ = 0.0 if k0 == 0 else h_sb[:, k0 - 1:k0]
                sc.add("dve", lambda e: e.tensor_tensor_scan(out=h_sb[:, ks], data0=r_sb[:, ks], data1=i_sb[:, ks], initial=init_, op0=ALU.mult, op1=ALU.add),
                       r=["lr_r", "lr_i", "lr_xc"], w=["lr_xc"])
            ops.append(recur)

            def gelu(ks=ks):
                sc.add("pool", lambda e: e.tensor_tensor(out=t_sb[:, ks], in0=lg[:, ks], in1=lg[:, ks], op=ALU.mult), r=["lr_lg", "lr_t"], w=["lr_t"])
                sc.add("pool", lambda e: e.tensor_tensor(out=t_sb[:, ks], in0=t_sb[:, ks], in1=lg[:, ks], op=ALU.mult), r=["lr_lg", "lr_t"], w=["lr_t"])
                sc.add("dve", lambda e: e.scalar_tensor_tensor(out=t_sb[:, ks], in0=t_sb[:, ks], scalar=0.044715, in1=lg[:, ks], op0=ALU.mult, op1=ALU.add),
                       r=["lr_lg", "lr_t"], w=["lr_t"])
                sc.add("act", lambda e: e.activation(out=t_sb[:, ks], in_=t_sb[:, ks], func=AF.Sigmoid, scale=GK), r=["lr_t"], w=["lr_t"])
                sc.add("dve", lambda e: e.tensor_tensor(out=t_sb[:, ks], in0=t_sb[:, ks], in1=lg[:, ks], op=ALU.mult), r=["lr_lg", "lr_t"], w=["lr_t"])
                sc.add("dve", lambda e: e.tensor_tensor(out=yo[:, ks], in0=t_sb[:, ks], in1=h_sb[:, ks], op=ALU.mult), r=["lr_t", "lr_xc"], w=["lr_yo"])
            ops.append(gelu)

        def store(cs=cs, c=c):
            sc.dma("sp", yT_d[cs, :], yo[:], r=["lr_yo"], w=[("lr_yT_d", c)])
        ops.append(store)
    return ops


def emit_lru(sc, nc, ps, lxT_d, lgT_d, conv_w_d, conv_b_d, wa_d, ba_d, wx_d, bx_d, lam_d, yT_d, S, blocks=range(8)):
    with ExitStack() as st:
        for f in lru_side(sc, nc, ps, st, lxT_d, lgT_d, conv_w_d, conv_b_d, wa_d, ba_d, wx_d, bx_d, lam_d, yT_d, S, blocks=blocks):
            f()
        sc.flush()


def emit_nsa(sc, nc, ps, cst, qT_d, kcT_d, vcT_d, ksT_d, vs_d, kwT_d, vw_d, gate_d,
             pos_k_d, w1_k_d, w2_k_d, pos_v_d, w1_v_d, w2_v_d, y_d, S, kvhs=range(NSA_KVH)):
    NT = S // 128
    NC = (S - CMP_LEN) // CMP_STRIDE + 1
    scale = HD ** -0.5
    GK = 1.5957691216057308
    with ExitStack() as st:
        ident = _sb(nc, st, "ns_ident", [128, 128], BF16)
        identf = _sb(nc, st, "ns_identf", [128, 128], F32)
        tri = _sb(nc, st, "ns_tri", [128, 512], BF16)
        tris = _sb(nc, st, "ns_tris", [128, 512], BF16)
        E2 = _sb(nc, st, "ns_E2", [64, NT, 128], BF16)
        cmask = _sb(nc, st, "ns_cmask", [128, 2, S], BF16)
        ovt = _sb(nc, st, "ns_ov", [128, 2, 64], BF16)
        selmul = _sb(nc, st, "ns_selmul", [128, NT, 64], F32)
        seladd = _sb(nc, st, "ns_seladd", [128, NT, 64], F32)
        gsb = _sb(nc, st, "ns_gate", [128, NT, 24], F32)
        xT = _sb(nc, st, "ns_xT", [128, S], BF16)
        Xl = _sb(nc, st, "ns_Xl", [128, 32, 256], BF16)
        w1 = _sb(nc, st, "ns_w1", [128, 32, 256], BF16)
        w2 = _sb(nc, st, "ns_w2", [128, 2, 128], BF16)
        pos = _sb(nc, st, "ns_pos", [32, 128], F32)
        posT = _sb(nc, st, "ns_posT", [128, 32], F32)
        tg = _sb(nc, st, "ns_tg", [128, 256], F32)
        hf = _sb(nc, st, "ns_hf", [128, 256], F32)
        hid = _sb(nc, st, "ns_hid", [128, 2, 256], BF16)
        kcT = _sb(nc, st, "ns_kcT", [128, 256], BF16)
        vcaug = _sb(nc, st, "ns_vcaug", [128, 2, 194], BF16)
        ksT = _sb(nc, st, "ns_ksT", [128, S], BF16)
        kwT = _sb(nc, st, "ns_kwT", [128, S], BF16)
        vsaug = _sb(nc, st, "ns_vsaug", [128, NT, 130], BF16)
        vwaug = _sb(nc, st, "ns_vwaug", [128, NT, 130], BF16)
        q4 = [_sb(nc, st, "ns_q4_%d" % i, [128, 4, 128], BF16) for i in range(2)]
        pc = _sb(nc, st, "ns_pc", [128, 512], BF16)
        pcm = [_sb(nc, st, "ns_pcm%d" % i, [128, 512], BF16) for i in range(2)]
        p_sb = [_sb(nc, st, "ns_p%d" % i, [128, 512], BF16) for i in range(2)]
        ocmp = [_sb(nc, st, "ns_ocmp%d" % i, [128, 4, 128], F32) for i in range(2)]
        rdc = [_sb(nc, st, "ns_rdc%d" % i, [128, 4], F32) for i in range(2)]
        impacc = [_sb(nc, st, "ns_imp%d" % i, [128, 64], F32) for i in range(2)]
        imp3 = _sb(nc, st, "ns_imp3", [128, 64], F32)
        max8a = _sb(nc, st, "ns_max8a", [128, 8], F32)
        max8b = _sb(nc, st, "ns_max8b", [128, 8], F32)
        negs = [_sb(nc, st, "ns_negs%d" % i, [128, 64], F32) for i in range(2)]
        negsT4 = [_sb(nc, st, "ns_negsT4%d" % i, [64, 4, 128], BF16) for i in range(2)]
        rsw = _sb(nc, st, "ns_rsw", [128, 8], F32)
        acc = _sb(nc, st, "ns_acc", [128, 128], F32)
        yt = [_sb(nc, st, "ns_yt%d" % i, [128, 512], BF16) for i in range(2)]
        zeroT = _sb(nc, st, "ns_zeroT", [128, 128], BF16)
        sc.add("pool", lambda e: e.memset(zeroT[:], 0.0), w=["zeroT"])

        for (t_, name) in ((ident, "ident"), (identf, "identf"), (tri, "tri4"), (tris, "tris4"), (E2, "nsa_E2"), (cmask, "nsa_cmask"),
                           (ovt, "nsa_ov"), (selmul, "nsa_selmul"), (seladd, "nsa_seladd")):
            sc.dma("sp", t_[:], cst[name], w=[name])
        sc.dma("sp", gsb[:], gate_d.rearrange("(n j) c -> j n c", j=128), w=["gate"])
        sc.add("act", lambda e: e.activation(out=gsb[:], in_=gsb[:], func=AF.Sigmoid), r=["gate"], w=["gate"])
        sc.add("pool", lambda e: e.memset(vsaug[:, :, 128:130], 1.0), w=["vs1"])
        sc.add("pool", lambda e: e.memset(vwaug[:, :, 128:130], 1.0), w=["vw1"])
        sc.add("pool", lambda e: e.memset(vcaug[:, :, 128:129], 1.0), w=["vc1"])
        sc.add("pool", lambda e: e.tensor_copy(out=vcaug[:, :, 129:193], in_=ovt[:]), r=["nsa_ov"], w=["vcov"])
        sc.add("pool", lambda e: e.memset(hid[:], 0.0), w=["hid"])
        pcount = 0
        tcount = 0
        for kvh in kvhs:
            ks_ = slice(kvh * 128, (kvh + 1) * 128)
            for which, src_d, pos_d, w1_d, w2_d in (("k", kcT_d, pos_k_d, w1_k_d, w2_k_d), ("v", vcT_d, pos_v_d, w1_v_d, w2_v_d)):
                sc.dma("sp", xT[:], src_d[ks_, :], w=["xT"])
                sc.dma("sp", pos[:], pos_d, w=["pos"])
                sc.dma("pool", w1[:], w1_d.rearrange("(l d) c -> d l c", d=128), w=["w1"])
                sc.dma("pool", w2[:], w2_d.rearrange("(m p) d -> p m d", p=128), w=["w2"])
                sc.add("pe", lambda e: e.transpose(ps[0][:, 0:32], pos[:], identf[0:32, 0:32]), r=["pos", "identf"], w=[("ps", 0)])
                sc.add("act", lambda e: e.copy(out=posT[:], in_=ps[0][:, 0:32]), r=[("ps", 0)], w=["posT"])
                V = xT[:].rearrange("p (m s) -> p m s", s=16)
                for a in range(2):
                    sc.add("dve", lambda e, a=a, V=V: e.tensor_tensor(
                        out=Xl[:, a * 16:(a + 1) * 16, 0:NC], in0=V[:, a:a + NC, :].rearrange("p n s -> p s n"),
                        in1=posT[:, a * 16:(a + 1) * 16].unsqueeze(2).broadcast_to([128, 16, NC]), op=ALU.add),
                        r=["xT", "posT"], w=["Xl"])
                for m in range(2):
                    for l in range(32):
                        sc.add("pe", lambda e, m=m, l=l: e.matmul(ps[1 + m][:, 0:NC], w1[:, l, m * 128:(m + 1) * 128], Xl[:, l, 0:NC],
                                                                  start=(l == 0), stop=(l == 31)),
                               r=["w1", "Xl"], w=[("ps", 1 + m)])
                    sc.add("act", lambda e, m=m: e.copy(out=hf[:, 0:NC], in_=ps[1 + m][:, 0:NC]), r=[("ps", 1 + m)], w=["hf"])
                    pm = hf[:, 0:NC]
                    sc.add("dve", lambda e, pm=pm: e.tensor_tensor(out=tg[:, 0:NC], in0=pm, in1=pm, op=ALU.mult), r=["hf"], w=["tg"])
                    sc.add("dve", lambda e, pm=pm: e.tensor_tensor(out=tg[:, 0:NC], in0=tg[:, 0:NC], in1=pm, op=ALU.mult), r=["hf", "tg"], w=["tg"])
                    sc.add("dve", lambda e, pm=pm: e.scalar_tensor_tensor(out=tg[:, 0:NC], in0=tg[:, 0:NC], scalar=0.044715, in1=pm, op0=ALU.mult, op1=ALU.add),
                           r=["hf", "tg"], w=["tg"])
                    sc.add("act", lambda e: e.activation(out=tg[:, 0:NC], in_=tg[:, 0:NC], func=AF.Sigmoid, scale=GK), r=["tg"], w=["tg"])
                    sc.add("dve", lambda e, pm=pm, m=m: e.tensor_tensor(out=hid[:, m, 0:NC], in0=tg[:, 0:NC], in1=pm, op=ALU.mult),
                           r=["hf", "tg"], w=["hid"])
                if which == "k":
                    for m in range(2):
                        sc.add("pe", lambda e, m=m: e.matmul(ps[3][:, 0:256], w2[:, m, :], hid[:, m, :], start=(m == 0), stop=(m == 1)),
                               r=["w2", "hid"], w=[("ps", 3)])
                    sc.add("act", lambda e: e.copy(out=kcT[:], in_=ps[3][:, 0:256]), r=[("ps", 3)], w=["kcT"])
                else:
                    for nt in range(2):
                        for m in range(2):
                            sc.add("pe", lambda e, m=m, nt=nt: e.matmul(ps[3][:, nt * 128:(nt + 1) * 128], hid[:, m, nt * 128:(nt + 1) * 128], w2[:, m, :],
                                                                        start=(m == 0), stop=(m == 1)),
                                   r=["w2", "hid"], w=[("ps", 3)])
                    sc.add("act", lambda e: e.copy(out=vcaug[:, :, 0:128], in_=ps[3][:, 0:256].rearrange("p (a c) -> p a c", c=128)),
                           r=[("ps", 3)], w=["vc"])
            sc.dma("sp", ksT[:], ksT_d[ks_, :], w=["ksT"])
            sc.dma("sp", kwT[:], kwT_d[ks_, :], w=["kwT"])
            sc.dma("sp", vsaug[:, :, 0:128], vs_d[:, ks_].rearrange("(n j) e -> j n e", j=128), w=["vs"])
            sc.dma("sp", vwaug[:, :, 0:128], vw_d[:, ks_].rearrange("(n j) e -> j n e", j=128), w=["vw"])

            def cmp_a(j, kvh=kvh):
                s_ = j % 2
                qs = slice(j * 128, (j + 1) * 128)
                qq = q4[s_]
                sc.dma("sp", qq[:], qT_d[kvh * 512:(kvh + 1) * 512, qs].rearrange("(g d) q -> d g q", d=128), w=[("q4", s_)])
                nts = [0] if 8 * j + 6 < 128 else [0, 1]
                for nt in nts:
                    sc.add("pe", lambda e, nt=nt: e.matmul(ps[0][:, :], kcT[:, nt * 128:(nt + 1) * 128], qq[:], start=True, stop=True),
                           r=["kcT", ("q4", s_)], w=[("ps", 0)])
                    sc.add("act", lambda e: e.activation(out=pc[:], in_=ps[0][:, :], func=AF.Exp, scale=scale), r=[("ps", 0)], w=["pc"])
                    sc.add("dve", lambda e, nt=nt: e.tensor_tensor(
                        out=pcm[nt][:].rearrange("p (g q) -> p g q", g=4), in0=pc[:].rearrange("p (g q) -> p g q", g=4),
                        in1=cmask[:, nt, qs].unsqueeze(1).broadcast_to([128, 4, 128]), op=ALU.mult), r=["pc", "nsa_cmask"], w=[("pcm", nt)])

            def cmp_b(j):
                s_ = j % 2
                nts = [0] if 8 * j + 6 < 128 else [0, 1]
                for pss in range(2):
                    sc.add("pe", lambda e: e.matmul(ps[7][:, :], zeroT[:], tri[:], start=True, stop=False), r=["zeroT", "tri4"], w=[("ps", 7)])
                    for g in (2 * pss, 2 * pss + 1):
                        oc = (g % 2) * 256
                        for nt in nts:
                            sc.add("pe", lambda e, g=g, oc=oc, nt=nt: e.matmul(
                                ps[7][:, oc:oc + 193], pcm[nt][:, g * 128:(g + 1) * 128], vcaug[:, nt, 0:193], start=False, stop=(nt == nts[-1])),
                                r=[("pcm", nt), "vc", "vc1", "vcov"], w=[("ps", 7)])
                    for g in (2 * pss, 2 * pss + 1):
                        oc = (g % 2) * 256
                        sc.add("dve", lambda e, g=g, oc=oc: e.tensor_scalar_max(out=rdc[s_][:, g:g + 1], in0=ps[7][:, oc + 128:oc + 129], scalar1=1e-30),
                               r=[("ps", 7)], w=[("rdc", s_)])
                        sc.add("dve", lambda e, g=g: e.reciprocal(out=rdc[s_][:, g:g + 1], in_=rdc[s_][:, g:g + 1]), r=[("rdc", s_)], w=[("rdc", s_)])
                        sc.add("dve", lambda e, g=g, oc=oc: e.tensor_scalar(out=ocmp[s_][:, g, :], in0=ps[7][:, oc:oc + 128], scalar1=rdc[s_][:, g:g + 1],
                                                                          scalar2=None, op0=ALU.mult), r=[("ps", 7), ("rdc", s_)], w=[("ocmp", s_)])
                        if g == 0:
                            sc.add("dve", lambda e, g=g, oc=oc: e.tensor_scalar(out=impacc[s_][:], in0=ps[7][:, oc + 129:oc + 193], scalar1=rdc[s_][:, g:g + 1],
                                                                              scalar2=None, op0=ALU.mult), r=[("ps", 7), ("rdc", s_)], w=[("imp", s_)])
                        else:
                            sc.add("dve", lambda e, g=g, oc=oc: e.scalar_tensor_tensor(out=impacc[s_][:], in0=ps[7][:, oc + 129:oc + 193], scalar=rdc[s_][:, g:g + 1],
                                                                                     in1=impacc[s_][:], op0=ALU.mult, op1=ALU.add),
                                   r=[("ps", 7), ("rdc", s_), ("imp", s_)], w=[("imp", s_)])
                sc.add("dve", lambda e: e.tensor_tensor(out=impacc[s_][:], in0=impacc[s_][:], in1=selmul[:, j, :], op=ALU.mult), r=[("imp", s_), "nsa_selmul"], w=[("imp", s_)])
                sc.add("dve", lambda e: e.tensor_tensor(out=impacc[s_][:], in0=impacc[s_][:], in1=seladd[:, j, :], op=ALU.add), r=[("imp", s_), "nsa_seladd"], w=[("imp", s_)])
                sc.add("dve", lambda e: e.max(out=max8a[:], in_=impacc[s_][:]), r=[("imp", s_)], w=["max8a"])
                sc.add("dve", lambda e: e.match_replace(out=imp3[:], in_to_replace=max8a[:], in_values=impacc[s_][:], imm_value=-3e9), r=[("imp", s_), "max8a"], w=["imp3"])
                sc.add("dve", lambda e: e.max(out=max8b[:], in_=imp3[:]), r=["imp3"], w=["max8b"])
                sc.add("dve", lambda e: e.tensor_scalar(out=negs[s_][:], in0=impacc[s_][:], scalar1=max8b[:, 7:8], scalar2=NEG, op0=ALU.is_lt, op1=ALU.mult),
                       r=[("imp", s_), "max8b"], w=[("negs", s_)])

            def cmp_c(j):
                s_ = j % 2
                sc.add("pe", lambda e: e.transpose(ps[0][0:64, 0:128], negs[s_][:], identf[:]), r=[("negs", s_), "identf"], w=[("ps", 0)])
                sc.add("act", lambda e: e.copy(out=negsT4[s_][:], in_=ps[0][0:64, 0:128].unsqueeze(1).broadcast_to([64, 4, 128])),
                       r=[("ps", 0)], w=[("negsT4", s_)])

            units = []
            tile_units = {}
            for i in range(NT):
                qs = slice(i * 128, (i + 1) * 128)
                s_ = i % 2
                qq = q4[s_]
                tl = []
                for branch in ("win", "sel"):
                    if branch == "win":
                        kts = list(range(max(0, i - 4), i + 1))
                        kT_, kkey, vaug_, vkeys, obase = kwT, "kwT", vwaug, ["vw", "vw1"], 1
                    else:
                        kts = list(range(i + 1))
                        kT_, kkey, vaug_, vkeys, obase = ksT, "ksT", vsaug, ["vs", "vs1"], 5
                    for kt in kts:
                        u = Unit()
                        units.append(u)
                        tl.append(u)
                        ksl = slice(kt * 128, (kt + 1) * 128)
                        sb_ = 3 + (pcount % 2)
                        pb = pcount % 2
                        pcount += 1
                        extra = []
                        if branch == "sel":
                            extra.append((E2[:, kt, :], negsT4[s_][:].rearrange("p g q -> p (g q)"), ["nsa_E2", ("negsT4", s_)]))
                        if kt == i:
                            extra.append((ident[:], tri[:], ["ident", "tri4"]))
                        if branch == "win" and kt == i - 4:
                            extra.append((ident[:], tris[:], ["ident", "tris4"]))

                        def qk(sb_=sb_, ksl=ksl, kT_=kT_, kkey=kkey, qq=qq, s_=s_, extra=extra):
                            sc.add("pe", lambda e: e.matmul(ps[sb_][:, :], kT_[:, ksl], qq[:], start=True, stop=(len(extra) == 0)),
                                   r=[kkey, ("q4", s_)], w=[("ps", sb_)])
                            for xi, (l_, r_, keys) in enumerate(extra):
                                sc.add("pe", lambda e, l_=l_, r_=r_, last=(xi == len(extra) - 1): e.matmul(ps[sb_][:, :], l_, r_, start=False, stop=last),
                                       r=keys, w=[("ps", sb_)])
                        u.qk.append(qk)

                        def act(sb_=sb_, pb=pb):
                            sc.add("act", lambda e: e.activation(out=p_sb[pb][:], in_=ps[sb_][:, :], func=AF.Exp, scale=scale),
                                   r=[("ps", sb_)], w=[("p", pb)])
                        u.act.append(act)

                        def pv(pb=pb, kt=kt, kts=kts, vaug_=vaug_, vkeys=vkeys, obase=obase):
                            if kt == kts[0]:
                                for ob in (obase, obase + 1):
                                    sc.add("pe", lambda e, ob=ob: e.matmul(ps[ob][:, :], zeroT[:], tri[:], start=True, stop=False), r=["zeroT", "tri4"], w=[("ps", ob)])
                            for g in range(4):
                                ob = obase + g // 2
                                oc = (g % 2) * 256
                                sc.add("pe", lambda e, g=g, ob=ob, oc=oc: e.matmul(
                                    ps[ob][:, oc:oc + 129], p_sb[pb][:, g * 128:(g + 1) * 128], vaug_[:, kt, 0:129], start=False, stop=(kt == kts[-1])),
                                    r=[("p", pb)] + vkeys, w=[("ps", ob)])
                        u.pv.append(pv)

                def post(i=i, s_=s_, qs=qs, kvh=kvh):
                    yb = i % 2
                    for g in range(4):
                        hq = kvh * 4 + g
                        oc = (g % 2) * 256
                        bw, bs = 1 + g // 2, 5 + g // 2
                        sc.add("dve", lambda e, bw=bw, oc=oc: e.reciprocal(out=rsw[:, 0:1], in_=ps[bw][:, oc + 128:oc + 129]), r=[("ps", bw)], w=["rsw"])
                        sc.add("dve", lambda e, bs=bs, oc=oc: e.reciprocal(out=rsw[:, 1:2], in_=ps[bs][:, oc + 128:oc + 129]), r=[("ps", bs)], w=["rsw"])
                        sc.add("dve", lambda e, hq=hq: e.tensor_tensor(out=rsw[:, 2:3], in0=rsw[:, 0:1], in1=gsb[:, i, hq * 3 + 2:hq * 3 + 3], op=ALU.mult),
                               r=["rsw", "gate"], w=["rsw"])
                        sc.add("dve", lambda e, hq=hq: e.tensor_tensor(out=rsw[:, 3:4], in0=rsw[:, 1:2], in1=gsb[:, i, hq * 3 + 1:hq * 3 + 2], op=ALU.mult),
                               r=["rsw", "gate"], w=["rsw"])
                        sc.add("dve", lambda e, g=g, hq=hq: e.tensor_scalar(out=acc[:], in0=ocmp[s_][:, g, :], scalar1=gsb[:, i, hq * 3:hq * 3 + 1], scalar2=None, op0=ALU.mult),
                               r=[("ocmp", s_), "gate"], w=["acc"])
                        sc.add("dve", lambda e, bw=bw, oc=oc: e.scalar_tensor_tensor(out=acc[:], in0=ps[bw][:, oc:oc + 128], scalar=rsw[:, 2:3], in1=acc[:],
                                                                                    op0=ALU.mult, op1=ALU.add), r=[("ps", bw), "rsw", "acc"], w=["acc"])
                        sc.add("dve", lambda e, bs=bs, oc=oc, g=g: e.scalar_tensor_tensor(out=yt[yb][:, g * 128:(g + 1) * 128], in0=ps[bs][:, oc:oc + 128], scalar=rsw[:, 3:4],
                                                                                         in1=acc[:], op0=ALU.mult, op1=ALU.add),
                               r=[("ps", bs), "rsw", "acc"], w=[("yt", yb)])
                    sc.dma("sp", y_d[qs, kvh * 512:(kvh + 1) * 512], yt[yb][:], r=[("yt", yb)], w=[("y_d", kvh, i)])
                tl[-1].post.append(post)
                tile_units[i] = tl
            tile_units[0][0].pre.extend([lambda f=cmp_a: f(0), lambda f=cmp_b: f(0), lambda f=cmp_c: f(0)])
            for i in range(NT - 1):
                tl = tile_units[i]
                n = len(tl)
                tl[1].pre.append(lambda j=i + 1, f=cmp_a: f(j))
                tl[min(2, n - 1)].pre.append(lambda j=i + 1, f=cmp_b: f(j))
                tl[n - 1].pre.append(lambda j=i + 1, f=cmp_c: f(j))
            run_pipelined(units)
        sc.flush()


class PsRot:
    def __init__(self, banks):
        self.banks = list(banks)
        self.i = 0

    def next(self):
        b = self.banks[self.i % len(self.banks)]
        self.i += 1
        return b


def emit_fill_norm(sc, nc, ps, AT, TG, x_d, tok0, w_d, ident_d):
    KC = D_MODEL // 128
    with ExitStack() as st:
        ident = _sb(nc, st, "fn_ident", [128, 128], BF16)
        wb = _sb(nc, st, "fn_wb", [128, D_MODEL], F32)
        xt = [_sb(nc, st, "fn_x%d" % i, [128, D_MODEL], F32) for i in range(2)]
        xn = [_sb(nc, st, "fn_xn%d" % i, [128, D_MODEL], BF16) for i in range(2)]
        ssq = _sb(nc, st, "fn_ssq", [128, 2], F32)
        epsc = _sb(nc, st, "fn_eps", [128, 1], F32)
        sc.dma("sp", ident[:], ident_d, w=["ident"])
        sc.dma("sp", wb[:], w_d.partition_broadcast(128), w=["wb"])
        sc.add("pool", lambda e: e.memset(epsc[:], EPS), w=["epsc"])
        rot = PsRot(range(8))
        ev = 0
        for ti in range(TG // 128):
            b = ti % 2
            t0 = tok0 + ti * 128
            hD = D_MODEL // 2
            sc.dma("sp", xt[b][:, 0:hD], x_d[t0:t0 + 128, 0:hD], w=[("x", b)])
            sc.dma("pool", xt[b][:, hD:D_MODEL], x_d[t0:t0 + 128, hD:D_MODEL], w=[("x", b)])
            sc.add("pool", lambda e: e.memset(ssq[:, 0:1], 0.0), w=["ssq"])
            sc.add("act", lambda e, b=b: e.activation(out=xn[b][:], in_=xt[b][:], func=AF.Square, accum_out=ssq[:, 0:1]),
                   r=[("x", b)], w=[("xn", b), "ssq"])
            sc.add("act", lambda e: e.activation(out=ssq[:, 1:2], in_=ssq[:, 0:1], func=AF.Sqrt, bias=epsc[:, 0:1], scale=1.0 / D_MODEL),
                   r=["ssq", "epsc"], w=["rstd"])
            sc.add("dve", lambda e: e.reciprocal(out=ssq[:, 1:2], in_=ssq[:, 1:2]), r=["rstd"], w=["rstd"])
            sc.add("dve", lambda e, b=b: e.scalar_tensor_tensor(out=xn[b][:], in0=xt[b][:], scalar=ssq[:, 1:2], in1=wb[:], op0=ALU.mult, op1=ALU.mult),
                   r=[("x", b), "rstd", "wb"], w=[("xn", b)])
            for k0 in range(0, KC, 8):
                pb = rot.next()
                pT = ps[pb][:].bitcast(BF16)
                for j in range(8):
                    kc = k0 + j
                    sc.add("pe", lambda e, pT=pT, j=j, kc=kc, b=b: e.transpose(pT[:, j * 128:(j + 1) * 128], xn[b][:, kc * 128:(kc + 1) * 128], ident[:]),
                           r=[("xn", b), "ident"], w=[("ps", pb)])
                eng = "act" if ev % 2 == 0 else "dve"
                ev += 1
                dst = AT[:, k0:k0 + 8, ti * 128:(ti + 1) * 128]
                src = pT[:, 0:1024].rearrange("p (a c) -> p a c", c=128)
                if eng == "act":
                    sc.add("act", lambda e, dst=dst, src=src: e.copy(out=dst, in_=src), r=[("ps", pb)], w=[("AT", ti)])
                else:
                    sc.add("dve", lambda e, dst=dst, src=src: e.tensor_copy(out=dst, in_=src), r=[("ps", pb)], w=[("AT", ti)])
        sc.flush()


def emit_fill_y(sc, nc, ps, AT, TG, ycat_d, lruT_d, tok0, ident_d):
    with ExitStack() as st:
        ident = _sb(nc, st, "fy_ident", [128, 128], BF16)
        yt = [_sb(nc, st, "fy_y%d" % i, [128, D_MODEL], BF16) for i in range(2)]
        sc.dma("sp", ident[:], ident_d, w=["ident"])
        sc.dma("sp", AT[:, 16:24, :], lruT_d[:, tok0:tok0 + TG].rearrange("(kc p) t -> p kc t", p=128), w=["AT_lru"])
        rot = PsRot(range(8))
        ev = 0
        for ti in range(TG // 128):
            b = ti % 2
            t0 = tok0 + ti * 128
            sc.dma("sp", yt[b][:, 0:2048], ycat_d[t0:t0 + 128, 0:2048], w=[("y", b)])
            sc.dma("pool", yt[b][:, 3072:4096], ycat_d[t0:t0 + 128, 3072:4096], w=[("y", b)])
            for k0 in (0, 8, 24):
                pb = rot.next()
                pT = ps[pb][:].bitcast(BF16)
                for j in range(8):
                    kc = k0 + j
                    sc.add("pe", lambda e, pT=pT, j=j, kc=kc, b=b: e.transpose(pT[:, j * 128:(j + 1) * 128], yt[b][:, kc * 128:(kc + 1) * 128], ident[:]),
                           r=[("y", b), "ident"], w=[("ps", pb)])
                dst = AT[:, k0:k0 + 8, ti * 128:(ti + 1) * 128]
                src = pT[:, 0:1024].rearrange("p (a c) -> p a c", c=128)
                if ev % 2 == 0:
                    sc.add("act", lambda e, dst=dst, src=src: e.copy(out=dst, in_=src), r=[("ps", pb)], w=[("AT", ti)])
                else:
                    sc.add("dve", lambda e, dst=dst, src=src: e.tensor_copy(out=dst, in_=src), r=[("ps", pb)], w=[("AT", ti)])
                ev += 1
        sc.flush()


def emit_gemm(sc, nc, ps, AT, KC, TG, W_d, blocks, wslots, at_keys=None):
    rot = PsRot(range(8))
    atk = at_keys if at_keys is not None else [("AT", i) for i in range(TG // 128)] + ["AT_lru"]
    for bi, blk in enumerate(blocks):
        slot = bi % len(wslots)
        wsl = wslots[slot]
        c0, width = blk["c0"], blk["width"]
        for k0 in range(0, KC, 16):
            k1 = min(KC, k0 + 16)
            sc.dma("pool", wsl[:, k0:k1, 0:width], W_d[k0 * 128:k1 * 128, c0:c0 + width].rearrange("(kc p) n -> p kc n", p=128), w=[("w", slot)])
        if blk["variant"] == "F":
            for m in range((width + 127) // 128):
                mw = min(128, width - m * 128)
                for tg in range(TG // 512):
                    pb = rot.next()
                    for kc in range(KC):
                        sc.add("pe", lambda e, pb=pb, mw=mw, wsl=wsl, kc=kc, m=m, tg=tg: e.matmul(
                            ps[pb][0:mw, :], wsl[:, kc, m * 128:m * 128 + mw], AT[:, kc, tg * 512:(tg + 1) * 512], start=(kc == 0), stop=(kc == KC - 1)),
                            r=[("w", slot)] + atk, w=[("ps", pb)])
                    blk["epi"](sc, ps[pb][0:mw, :], ("ps", pb), blk, (m, tg, mw))
        else:
            for tt in range(TG // 128):
                pb = rot.next()
                for kc in range(KC):
                    sc.add("pe", lambda e, pb=pb, wsl=wsl, kc=kc, tt=tt, width=width: e.matmul(
                        ps[pb][:, 0:width], AT[:, kc, tt * 128:(tt + 1) * 128], wsl[:, kc, 0:width], start=(kc == 0), stop=(kc == KC - 1)),
                        r=[("w", slot)] + atk, w=[("ps", pb)])
                blk["epi"](sc, ps[pb][:, 0:width], ("ps", pb), blk, (tt,))


class Stager:
    def __init__(self, nc, st, name, shape, dt, n=4):
        self.t = [_sb(nc, st, "%s%d" % (name, i), shape, dt) for i in range(n)]
        self.name = name
        self.i = 0

    def next(self):
        k = self.i % len(self.t)
        self.i += 1
        return self.t[k], (self.name, k)


def _split_blocks(c0, width, step=512):
    out = []
    o = 0
    while o < width:
        w = min(step, width - o)
        out.append((c0 + o, w, o))
        o += w
    return out


def emit_gateup(sc, nc, ps, AT, TG, tok0, wg_d, wu_d, uT_d, wslots_g, wslots_u, st_f, st_b):
    KC = D_MODEL // 128
    rot = PsRot(range(8))
    atk = [("AT", i) for i in range(TG // 128)]
    WC = 256
    for bi, (c0, width, _) in enumerate(_split_blocks(0, D_FF, WC)):
        slot = bi % 2
        wg, wu = wslots_g[slot], wslots_u[slot]
        for k0 in range(0, KC, 16):
            sc.dma("pool", wg[:, k0:k0 + 16, 0:width], wg_d[k0 * 128:(k0 + 16) * 128, c0:c0 + width].rearrange("(kc p) n -> p kc n", p=128), w=[("wg", slot)])
            sc.dma("pool", wu[:, k0:k0 + 16, 0:width], wu_d[k0 * 128:(k0 + 16) * 128, c0:c0 + width].rearrange("(kc p) n -> p kc n", p=128), w=[("wu", slot)])
        for m in range(width // 128):
            for tg in range(TG // 512):
                pg, pu = rot.next(), rot.next()
                for kc in range(KC):
                    sc.add("pe", lambda e, pg=pg, wg=wg, kc=kc, m=m, tg=tg: e.matmul(
                        ps[pg][:, :], wg[:, kc, m * 128:(m + 1) * 128], AT[:, kc, tg * 512:(tg + 1) * 512], start=(kc == 0), stop=(kc == KC - 1)),
                        r=[("wg", slot)] + atk, w=[("ps", pg)])
                for kc in range(KC):
                    sc.add("pe", lambda e, pu=pu, wu=wu, kc=kc, m=m, tg=tg: e.matmul(
                        ps[pu][:, :], wu[:, kc, m * 128:(m + 1) * 128], AT[:, kc, tg * 512:(tg + 1) * 512], start=(kc == 0), stop=(kc == KC - 1)),
                        r=[("wu", slot)] + atk, w=[("ps", pu)])
                sg, sgk = st_f.next()
                ub, ubk = st_b.next()
                sc.add("act", lambda e, sg=sg, pg=pg: e.activation(out=sg[:], in_=ps[pg][:, :], func=AF.Silu), r=[("ps", pg)], w=[sgk])
                sc.add("dve", lambda e, sg=sg, ub=ub, pu=pu: e.tensor_tensor(out=ub[:], in0=sg[:], in1=ps[pu][:, :], op=ALU.mult), r=[sgk, ("ps", pu)], w=[ubk])
                r0 = c0 + m * 128
                t0 = tok0 + tg * 512
                sc.dma("sp", uT_d[r0:r0 + 128, t0:t0 + 512], ub[:], r=[ubk], w=[("uT", r0, t0)])


def emit_final_norm(sc, nc, x_d, w_d, out_d, S):
    with ExitStack() as st:
        wb = _sb(nc, st, "ff_wb", [128, D_MODEL], F32)
        xt = [_sb(nc, st, "ff_x%d" % i, [128, D_MODEL], F32) for i in range(2)]
        xo = [_sb(nc, st, "ff_o%d" % i, [128, D_MODEL], F32) for i in range(2)]
        ssq = _sb(nc, st, "ff_ssq", [128, 2], F32)
        epsc = _sb(nc, st, "ff_eps", [128, 1], F32)
        sc.dma("sp", wb[:], w_d.partition_broadcast(128), w=["wb"])
        sc.add("pool", lambda e: e.memset(epsc[:], EPS), w=["epsc"])
        for ti in range(S // 128):
            b = ti % 2
            sc.dma("sp", xt[b][:, 0:2048], x_d[ti * 128:(ti + 1) * 128, 0:2048], w=[("x", b)])
            sc.dma("pool", xt[b][:, 2048:4096], x_d[ti * 128:(ti + 1) * 128, 2048:4096], w=[("x", b)])
            sc.add("pool", lambda e: e.memset(ssq[:, 0:1], 0.0), w=["ssq"])
            sc.add("act", lambda e, b=b: e.activation(out=xo[b][:], in_=xt[b][:], func=AF.Square, accum_out=ssq[:, 0:1]), r=[("x", b)], w=[("xo", b), "ssq"])
            sc.add("act", lambda e: e.activation(out=ssq[:, 1:2], in_=ssq[:, 0:1], func=AF.Sqrt, bias=epsc[:, 0:1], scale=1.0 / D_MODEL), r=["ssq", "epsc"], w=["rstd"])
            sc.add("dve", lambda e: e.reciprocal(out=ssq[:, 1:2], in_=ssq[:, 1:2]), r=["rstd"], w=["rstd"])
            sc.add("dve", lambda e, b=b: e.scalar_tensor_tensor(out=xo[b][:], in0=xt[b][:], scalar=ssq[:, 1:2], in1=wb[:], op0=ALU.mult, op1=ALU.mult),
                   r=[("x", b), "rstd", "wb"], w=[("xo", b)])
            sc.dma("sp", out_d[ti * 128:(ti + 1) * 128, :], xo[b][:], r=[("xo", b)], w=[("out", ti)])
        sc.flush()


WEIGHT_SPECS = [
    ("norm_mix", [D_MODEL]), ("w_in", [D_MODEL, IN_WIDTH]), ("w_out", [D_MODEL, D_MODEL]), ("ret_norm", [1024]),
    ("lru_conv_w", [4, 1024]), ("lru_conv_b", [1024]), ("lru_wa", [8, 128, 128]), ("lru_ba", [1024]),
    ("lru_wx", [8, 128, 128]), ("lru_bx", [1024]), ("lru_lambda", [1024]),
    ("cmp_pos_k", [32, 128]), ("cmp_w1_k", [4096, 256]), ("cmp_w2_k", [256, 128]),
    ("cmp_pos_v", [32, 128]), ("cmp_w1_v", [4096, 256]), ("cmp_w2_v", [256, 128]),
    ("norm_ffn", [D_MODEL]), ("w_gate", [D_MODEL, D_FF]), ("w_up", [D_MODEL, D_FF]), ("w_down", [D_FF, D_MODEL]),
]

W_IN_SEGS = [
    (C_RQ, 1024, "F", "bf", "PF_bf", 0), (C_RK, 1024, "F", "bf", "PF_bf", 1024),
    (C_RV, 1024, "T", "bf", "PT_bf", 0), (C_RG, 1024, "T", "f", "PT_f", 0),
    (C_MQ, 1024, "F", "bf", "PF_bf", 2048), (C_MK, 1024, "F", "bf", "PF_bf", 3072),
    (C_MV, 1024, "T", "bf", "PT_bf", 1024),
    (C_LX, 1024, "F", "f", "PF_f", 0), (C_LG, 1024, "F", "f", "PF_f", 1024),
    (C_NQ, 1024, "F", "bf", "PF_bf", 4096),
    (C_NKC, 256, "F", "bf", "PF_bf", 5120), (C_NVC, 256, "F", "bf", "PF_bf", 5376),
    (C_NKS, 256, "F", "bf", "PF_bf", 5632), (C_NVS, 256, "T", "bf", "PT_bf", 2048),
    (C_NKW, 256, "F", "bf", "PF_bf", 5888), (C_NVW, 256, "T", "bf", "PT_bf", 2304),
    (C_NG, 24, "T", "f", "PT_f", 1024),
]


def build_program(S, depth, stages=("all",)):
    nc = bass.Bass("TRN2", target_bir_lowering=False)
    ALL = "all" in stages
    need = {"win": ["norm_mix", "w_in"], "wout": ["w_out"], "gateup": ["norm_ffn", "w_gate", "w_up"], "down": ["w_down"],
            "mix": ["ret_norm", "lru_conv_w", "lru_conv_b", "lru_wa", "lru_ba", "lru_wx", "lru_bx", "lru_lambda",
                    "cmp_pos_k", "cmp_w1_k", "cmp_w2_k", "cmp_pos_v", "cmp_w1_v", "cmp_w2_v"]}
    needed = set(n for st_ in stages if st_ in need for n in need[st_])
    TGK = min(2048, S)
    KC = D_MODEL // 128
    KCF = D_FF // 128
    x_d = nc.dram_tensor("x", [S, D_MODEL], F32, kind="ExternalInput").ap()
    W = {}
    for name, shp in WEIGHT_SPECS:
        if ALL or name in needed:
            W[name] = nc.dram_tensor(name, [depth] + shp, F32, kind="ExternalInput").ap()
    nf_d = nc.dram_tensor("norm_final", [D_MODEL], F32, kind="ExternalInput").ap()
    out_d = nc.dram_tensor("out", [S, D_MODEL], F32, kind="ExternalOutput").ap()
    cst = {}
    for n, (shf, dt) in CONST_SPECS.items():
        cst[n] = nc.dram_tensor("c_" + n, list(shf(S)), dt, kind="ExternalInput").ap()
    cst["ret_dchunk"] = host_constants(128)["ret_dchunk"]
    D = {
        "xa": nc.dram_tensor("s_xa", [S, D_MODEL], F32).ap(),
        "xb": nc.dram_tensor("s_xb", [S, D_MODEL], F32).ap(),
        "PF_bf": nc.dram_tensor("s_pfb", [6144, S], BF16).ap(),
        "PF_f": nc.dram_tensor("s_pff", [2048, S], F32).ap(),
        "PT_bf": nc.dram_tensor("s_ptb", [S, 2560], BF16).ap(),
        "PT_f": nc.dram_tensor("s_ptf", [S, 1048], F32).ap(),
        "ycat": nc.dram_tensor("s_ycat", [S, D_MODEL], BF16).ap(),
        "lruT": nc.dram_tensor("s_lruT", [1024, S], BF16).ap(),
        "uT": nc.dram_tensor("s_uT", [D_FF, S], BF16).ap(),
    }
    with ExitStack() as gst:
        ps = [gst.enter_context(nc.psum_tensor("ps%d" % i, [128, 512], F32)) for i in range(8)]
        sc = Sched(nc, gst)
        evc = [0]

        def evac(dst, src, rkeys, wkeys):
            if evc[0] % 2 == 0:
                sc.add("act", lambda e: e.copy(out=dst, in_=src), r=rkeys, w=wkeys)
            else:
                sc.add("dve", lambda e: e.tensor_copy(out=dst, in_=src), r=rkeys, w=wkeys)
            evc[0] += 1

        for l in range(depth):
            xin = x_d if l == 0 else D["xb"]
            for t0 in (range(0, S, TGK) if (ALL or "win" in stages) else ()):
                with ExitStack() as st:
                    AT = _sb(nc, st, "AT", [128, KC, TGK], BF16)
                    emit_fill_norm(sc, nc, ps, AT, TGK, xin, t0, W["norm_mix"][l], cst["ident"])
                    with ExitStack() as st2:
                        wslots = [_sb(nc, st2, "wsl%d" % i, [128, KC, 512], BF16) for i in range(2)]
                        st_f = Stager(nc, st2, "stf", [128, 512], F32, 3)
                        st_b = Stager(nc, st2, "stb", [128, 512], BF16, 3)
                        blocks = []
                        for (c0, width, variant, kind, dname, doff) in W_IN_SEGS:
                            for (bc0, bw, bo) in _split_blocks(c0, width):
                                def epi(sc_, ps_ap, pskey, blk, sub, variant=variant, kind=kind, dname=dname, doff=doff, bo=bo, bw=bw, t0=t0):
                                    stg, sk = (st_f if kind == "f" else st_b).next()
                                    if variant == "F":
                                        m, tg, mw = sub
                                        evac(stg[0:mw, :], ps_ap, [pskey], [sk])
                                        r0 = doff + bo + m * 128
                                        c = t0 + tg * 512
                                        sc_.dma("sp", D[dname][r0:r0 + mw, c:c + 512], stg[0:mw, :], r=[sk], w=[(dname, r0, c)])
                                    else:
                                        (tt,) = sub
                                        evac(stg[:, 0:bw], ps_ap, [pskey], [sk])
                                        r0 = t0 + tt * 128
                                        c = doff + bo
                                        sc_.dma("sp", D[dname][r0:r0 + 128, c:c + bw], stg[:, 0:bw], r=[sk], w=[(dname, r0, c)])
                                blocks.append(dict(c0=bc0, width=bw, variant=variant, epi=epi))
                        emit_gemm(sc, nc, ps, AT, KC, TGK, W["w_in"][l], blocks, wslots)
                        sc.flush()
            PFb, PFf, PTb, PTf = D["PF_bf"], D["PF_f"], D["PT_bf"], D["PT_f"]
            if ALL or "mix" in stages:
                emit_retention(sc, nc, ps, cst, PFb[0:1024, :], PFb[1024:2048, :], PTb[:, 0:1024], PTf[:, 0:1024], W["ret_norm"][l],
                               D["ycat"][:, 0:1024], S)
                emit_moba2(sc, nc, ps, cst, PFb[2048:3072, :], PFb[3072:4096, :], PTb[:, 1024:2048], D["ycat"][:, 1024:2048], S,
                          side=lambda st, l=l: lru_side(sc, nc, ps, st, PFf[0:1024, :], PFf[1024:2048, :], W["lru_conv_w"][l], W["lru_conv_b"][l],
                                                        W["lru_wa"][l], W["lru_ba"][l], W["lru_wx"][l], W["lru_bx"][l], W["lru_lambda"][l], D["lruT"], S))
                emit_nsa(sc, nc, ps, cst, PFb[4096:5120, :], PFb[5120:5376, :], PFb[5376:5632, :], PFb[5632:5888, :], PTb[:, 2048:2304],
                         PFb[5888:6144, :], PTb[:, 2304:2560], PTf[:, 1024:1048],
                         W["cmp_pos_k"][l], W["cmp_w1_k"][l], W["cmp_w2_k"][l], W["cmp_pos_v"][l], W["cmp_w1_v"][l], W["cmp_w2_v"][l],
                         D["ycat"][:, 3072:4096], S)
            for t0 in (range(0, S, TGK) if (ALL or "wout" in stages) else ()):
                with ExitStack() as st:
                    AT = _sb(nc, st, "AT", [128, KC, TGK], BF16)
                    emit_fill_y(sc, nc, ps, AT, TGK, D["ycat"], D["lruT"], t0, cst["ident"])
                    with ExitStack() as st2:
                        wslots = [_sb(nc, st2, "wsl%d" % i, [128, KC, 512], BF16) for i in range(2)]
                        st_r = Stager(nc, st2, "str", [128, 512], F32, 4)
                        blocks = []
                        for (bc0, bw, bo) in _split_blocks(0, D_MODEL):
                            def epi(sc_, ps_ap, pskey, blk, sub, bc0=bc0, bw=bw, t0=t0):
                                (tt,) = sub
                                stg, sk = st_r.next()
                                r0 = t0 + tt * 128
                                sc_.dma("sp", stg[:, 0:bw], xin[r0:r0 + 128, bc0:bc0 + bw], w=[sk])
                                sc_.add("dve", lambda e: e.tensor_tensor(out=stg[:, 0:bw], in0=stg[:, 0:bw], in1=ps_ap, op=ALU.add), r=[pskey, sk], w=[sk])
                                sc_.dma("sp", D["xa"][r0:r0 + 128, bc0:bc0 + bw], stg[:, 0:bw], r=[sk], w=[("xa", r0, bc0)])
                            blocks.append(dict(c0=bc0, width=bw, variant="T", epi=epi))
                        emit_gemm(sc, nc, ps, AT, KC, TGK, W["w_out"][l], blocks, wslots)
                        sc.flush()
            for t0 in (range(0, S, TGK) if (ALL or "gateup" in stages) else ()):
                with ExitStack() as st:
                    AT = _sb(nc, st, "AT", [128, KC, TGK], BF16)
                    emit_fill_norm(sc, nc, ps, AT, TGK, D["xa"], t0, W["norm_ffn"][l], cst["ident"])
                    with ExitStack() as st2:
                        wsg = [_sb(nc, st2, "wsg%d" % i, [128, KC, 256], BF16) for i in range(2)]
                        wsu = [_sb(nc, st2, "wsu%d" % i, [128, KC, 256], BF16) for i in range(2)]
                        st_f = Stager(nc, st2, "gsf", [128, 512], F32, 3)
                        st_b = Stager(nc, st2, "gsb", [128, 512], BF16, 3)
                        emit_gateup(sc, nc, ps, AT, TGK, t0, W["w_gate"][l], W["w_up"][l], D["uT"], wsg, wsu, st_f, st_b)
                        sc.flush()
            TGD = min(2048, S)
            kq = [(0, 29), (29, 58), (58, 86)]
            with ExitStack() as st:
                ATd = _sb(nc, st, "ATd", [128, 29, TGD], BF16)
                wslots = [_sb(nc, st, "wsd%d" % i, [128, 29, 512], BF16) for i in range(2)]
                st_r = Stager(nc, st, "dsr", [128, 512], F32, 4)
                for t0 in (range(0, S, TGD) if (ALL or "down" in stages) else ()):
                    for qi, (k0, k1) in enumerate(kq):
                        for ka in range(k0, k1, 8):
                            kb = min(k1, ka + 8)
                            sc.dma("sp", ATd[:, ka - k0:kb - k0, :],
                                   D["uT"][ka * 128:kb * 128, t0:t0 + TGD].rearrange("(kc p) t -> p kc t", p=128), w=["ATd"])
                        src_d = D["xa"] if qi == 0 else D["xb"]
                        blocks = []
                        for (bc0, bw, bo) in _split_blocks(0, D_MODEL, 512):
                            def epi(sc_, ps_ap, pskey, blk, sub, bc0=bc0, bw=bw, t0=t0, src_d=src_d):
                                (tt,) = sub
                                stg, sk = st_r.next()
                                r0 = t0 + tt * 128
                                sc_.dma("sp", stg[:, 0:bw], src_d[r0:r0 + 128, bc0:bc0 + bw], r=[("xb", r0, bc0)], w=[sk])
                                sc_.add("dve", lambda e: e.tensor_tensor(out=stg[:, 0:bw], in0=stg[:, 0:bw], in1=ps_ap, op=ALU.add), r=[pskey, sk], w=[sk])
                                sc_.dma("sp", D["xb"][r0:r0 + 128, bc0:bc0 + bw], stg[:, 0:bw], r=[sk], w=[("xb", r0, bc0)])
                            blocks.append(dict(c0=bc0, width=bw, variant="T", epi=epi))
                        emit_gemm(sc, nc, ps, ATd, k1 - k0, TGD, W["w_down"][l][k0 * 128:k1 * 128, :], blocks, wslots, at_keys=["ATd"])
                sc.flush()
        emit_final_norm(sc, nc, D["xb"], nf_d, out_d, S)
        sc.flush()
        n_ops = sc.n_emitted
    return nc, n_ops


_CACHE = {}


def _get_program(S, depth):
    key = (S, depth)
    if key not in _CACHE:
        _CACHE[key] = build_program(S, depth)
    return _CACHE[key]


def kernel(**inputs):
    x = np.asarray(inputs["x"], np.float32)
    B, S, Dm = x.shape
    depth = int(np.asarray(inputs["w_in"]).shape[0])
    nc, _ = _get_program(S, depth)
    hc = host_constants(S)
    shared = {}
    for name, _shp in WEIGHT_SPECS:
        shared[name] = np.ascontiguousarray(np.asarray(inputs[name], np.float32))
    shared["norm_final"] = np.ascontiguousarray(np.asarray(inputs["norm_final"], np.float32))
    for n in CONST_SPECS:
        shared["c_" + n] = np.ascontiguousarray(hc[n])
    in_maps = []
    for b in range(B):
        m = dict(shared)
        m["x"] = np.ascontiguousarray(x[b])
        in_maps.append(m)
    res = run_bass_kernel_spmd(nc, in_maps, core_ids=list(range(B)))
    out = np.stack([np.asarray(res.results[b]["out"], np.float32) for b in range(B)], 0)
    return out
```

```python
import math
from contextlib import ExitStack

import numpy as np
import ml_dtypes

import concourse.bass as bass
import concourse.mybir as mybir
from concourse.bass_utils import run_bass_kernel_spmd

F32 = mybir.dt.float32
BF16 = mybir.dt.bfloat16
AF = mybir.ActivationFunctionType
ALU = mybir.AluOpType
AX = mybir.AxisListType

ENGS = ("pe", "act", "dve", "pool", "sp")
DMAQ = ("sp", "pool")
RING = 8


class Op:
    __slots__ = ("eng", "fn", "r", "w", "dma", "sig", "deps", "token", "bp")

    def __init__(self, eng, fn, r, w, dma):
        self.eng, self.fn, self.r, self.w, self.dma = eng, fn, r, w, dma
        self.sig = False
        self.deps = ()
        self.token = None
        self.bp = None


class Sched:
    def __init__(self, nc, stack):
        self.nc = nc
        self.sem = {e: stack.enter_context(nc.semaphore("s_" + e)) for e in ENGS}
        self.cnt = {e: 0 for e in ENGS}
        self.ring = {q: [stack.enter_context(nc.semaphore("d_%s%d" % (q, i))) for i in range(RING)] for q in DMAQ}
        self.ring_cnt = {q: [0] * RING for q in DMAQ}
        self.ring_idx = {q: 0 for q in DMAQ}
        self.waited = {e: {} for e in ENGS}
        self.ops = []
        self.n_emitted = 0

    def add(self, eng, fn, r=(), w=(), dma=False):
        self.ops.append(Op(eng, fn, tuple(r), tuple(w), dma))

    def dma(self, q, out, in_, r=(), w=()):
        self.add(q, lambda e: e.dma_start(out=out, in_=in_), r, w, dma=True)

    def flush(self):
        ops = self.ops
        self.ops = []
        if not ops:
            return
        lastw = {}
        readers = {}
        for i, op in enumerate(ops):
            deps = set()
            for k in op.r:
                j = lastw.get(k)
                if j is not None:
                    deps.add(j)
            for k in op.w:
                j = lastw.get(k)
                if j is not None:
                    deps.add(j)
                for j in readers.get(k, {}).values():
                    deps.add(j)
            deps.discard(i)
            if op.eng == "pe" and not op.dma:
                deps = {j for j in deps if not (ops[j].eng == "pe" and not ops[j].dma)}
            op.deps = tuple(sorted(deps))
            for j in op.deps:
                ops[j].sig = True
            for k in op.w:
                lastw[k] = i
                readers[k] = {}
            for k in op.r:
                d = readers.setdefault(k, {})
                d[(op.eng, i) if op.dma else op.eng] = i
        last_of = {}
        for i, op in enumerate(ops):
            if not op.dma:
                last_of[op.eng] = i
        for i in last_of.values():
            ops[i].sig = True
        for op in ops:
            if op.dma:
                q = op.eng
                s = self.ring_idx[q]
                self.ring_idx[q] = (s + 1) % RING
                prev = self.ring_cnt[q][s]
                self.ring_cnt[q][s] = prev + 16
                op.bp = (self.ring[q][s], prev) if prev > 0 else None
                op.token = (self.ring[q][s], prev + 16)
            elif op.sig:
                self.cnt[op.eng] += 1
                op.token = (self.sem[op.eng], self.cnt[op.eng])
        final = []
        for e in ENGS:
            if self.cnt[e] > 0:
                final.append((self.sem[e], self.cnt[e]))
        for q in DMAQ:
            for s in range(RING):
                if self.ring_cnt[q][s] > 0:
                    final.append((self.ring[q][s], self.ring_cnt[q][s]))
        per_eng = {e: [] for e in ENGS}
        for op in ops:
            per_eng[op.eng].append(op)
        self.n_emitted += len(ops)

        def run(eng_name, eng):
            waited = self.waited[eng_name]

            def wait(tok):
                sem, val = tok
                key = id(sem)
                if waited.get(key, 0) >= val:
                    return
                waited[key] = val
                eng.wait_ge(sem, val)

            for op in per_eng[eng_name]:
                for j in op.deps:
                    wait(ops[j].token)
                if op.bp is not None:
                    wait(op.bp)
                ins = op.fn(eng)
                if op.dma:
                    ins.then_inc(op.token[0], 16)
                elif op.sig:
                    ins.then_inc(op.token[0], 1)
            for tok in final:
                wait(tok)

        with self.nc.Block() as block:
            @block.tensor
            def _(e):
                run("pe", e)

            @block.scalar
            def _(e):
                run("act", e)

            @block.vector
            def _(e):
                run("dve", e)

            @block.gpsimd
            def _(e):
                run("pool", e)

            @block.sync
            def _(e):
                run("sp", e)


D_MODEL = 4096
SEQ = 4096
DEPTH = 2
HD = 128
NH = 8
EPS = 1e-6
D_FF = 11008
ROPE_BASE = 10000.0
MOBA_BLOCK = 256
MOBA_TOPK = 3
LRU_C = 8.0
NSA_KVH = 2
CMP_LEN, CMP_STRIDE, CMP_HIDDEN = 32, 16, 256
SEL_BLOCK, SEL_TOPN, WIN = 64, 16, 512
NEG = -30000.0

C_RQ, C_RK, C_RV, C_RG = 0, 1024, 2048, 3072
C_MQ, C_MK, C_MV = 4096, 5120, 6144
C_LX, C_LG = 7168, 8192
C_NQ = 9216
C_NKC, C_NVC, C_NKS, C_NVS, C_NKW, C_NVW = 10240, 10496, 10752, 11008, 11264, 11520
C_NG = 11776
IN_WIDTH = 11800


def host_constants(S):
    c = {}
    half = HD // 2
    inv = ROPE_BASE ** (-np.arange(half, dtype=np.float32) / half)
    ang = np.arange(S, dtype=np.float32)[None, :] * inv[:, None]
    cos = np.cos(ang).astype(np.float32)
    sin = np.sin(ang).astype(np.float32)
    cos2 = np.concatenate([cos, cos], 0)
    sin2 = np.concatenate([-sin, sin], 0)
    sc = HD ** -0.5
    c["rope"] = np.stack([cos2, sin2, cos2 * sc, sin2 * sc], 1).astype(np.float32)
    perm = np.zeros((128, 128), np.float32)
    for d in range(128):
        perm[(d + 64) % 128, d] = 1.0
    ident = np.eye(128, dtype=np.float32)
    c["perm"] = perm.astype(ml_dtypes.bfloat16)
    c["ident"] = ident.astype(ml_dtypes.bfloat16)
    c["identf"] = ident
    gam = 1.0 - np.exp2(-5.0 - np.arange(NH, dtype=np.float64))
    lg = np.log(gam)
    i = np.arange(128)
    diff = i[None, :] - i[:, None]
    dec = np.where(diff[None] >= 0, np.exp(lg[:, None, None] * np.maximum(diff[None], 0)), 0.0)
    c["ret_decT"] = np.ascontiguousarray(dec.transpose(1, 0, 2)).astype(np.float32)
    dfs = np.exp(lg[:, None] * (i[None, :] + 1.0))
    c["ret_dfs"] = np.broadcast_to(dfs[None], (128, NH, 128)).astype(np.float32).copy()
    dte = np.exp(lg[:, None] * (127.0 - i[None, :]))
    c["ret_dte"] = np.ascontiguousarray(dte.T).astype(np.float32)
    c["ret_dchunk"] = [float(np.exp(l * 128.0)) for l in lg]
    k = np.arange(128)[:, None]
    q = np.arange(128)[None, :]
    tri = np.where(k <= q, 0.0, NEG)
    tris = np.where(k > q, 0.0, NEG)
    c["tri4"] = np.tile(tri, (1, 4)).astype(ml_dtypes.bfloat16)
    c["tris4"] = np.tile(tris, (1, 4)).astype(ml_dtypes.bfloat16)
    nb = S // MOBA_BLOCK
    nt = S // 128
    e = np.zeros((16, nt, 128), np.float32)
    for kt in range(nt):
        e[kt // 2, kt, :] = 1.0
    c["moba_E"] = e.astype(ml_dtypes.bfloat16)
    e2 = np.zeros((64, nt, 128), np.float32)
    for kt in range(nt):
        e2[2 * kt, kt, :64] = 1.0
        e2[2 * kt + 1, kt, 64:] = 1.0
    c["nsa_E2"] = e2.astype(ml_dtypes.bfloat16)
    NC = (S - CMP_LEN) // CMP_STRIDE + 1
    NCP = 256
    starts = np.arange(NC) * CMP_STRIDE
    t = np.arange(S)
    cm = ((starts[:, None] + CMP_LEN - 1) <= t[None, :]).astype(np.float32)
    cmp_mask = np.zeros((NCP, S), np.float32)
    cmp_mask[:NC] = cm
    c["nsa_cmask"] = np.ascontiguousarray(cmp_mask.reshape(2, 128, S).transpose(1, 0, 2)).astype(ml_dtypes.bfloat16)
    nsel = S // SEL_BLOCK
    sel_start = np.arange(nsel) * SEL_BLOCK
    ov = ((starts[:, None] < sel_start[None, :] + SEL_BLOCK) & (starts[:, None] + CMP_LEN > sel_start[None, :])).astype(np.float32)
    ovp = np.zeros((NCP, 64), np.float32)
    ovp[:NC, :nsel] = ov
    c["nsa_ov"] = np.ascontiguousarray(ovp.reshape(2, 128, 64).transpose(1, 0, 2)).astype(ml_dtypes.bfloat16)
    qsb = t // SEL_BLOCK
    j = np.arange(64)[None, :]
    forced = (j == 0) | (j == qsb[:, None]) | (j == qsb[:, None] - 1)
    allowed = j <= qsb[:, None]
    mul = (allowed & ~forced).astype(np.float32)
    addt = np.where(forced, 1e9 + 1e4 * (64 - j), np.where(allowed, 0.0, -1e9 - 1e4 * j)).astype(np.float32)
    c["nsa_selmul"] = np.ascontiguousarray(mul.reshape(nt, 128, 64).transpose(1, 0, 2))
    c["nsa_seladd"] = np.ascontiguousarray(addt.reshape(nt, 128, 64).transpose(1, 0, 2))
    return c


CONST_SPECS = {
    "rope": (lambda S: [128, 4, S], F32),
    "perm": (lambda S: [128, 128], BF16),
    "ident": (lambda S: [128, 128], BF16),
    "identf": (lambda S: [128, 128], F32),
    "ret_decT": (lambda S: [128, NH, 128], F32),
    "ret_dfs": (lambda S: [128, NH, 128], F32),
    "ret_dte": (lambda S: [128, NH], F32),
    "tri4": (lambda S: [128, 512], BF16),
    "tris4": (lambda S: [128, 512], BF16),
    "moba_E": (lambda S: [16, S // 128, 128], BF16),
    "nsa_E2": (lambda S: [64, S // 128, 128], BF16),
    "nsa_cmask": (lambda S: [128, 2, S], BF16),
    "nsa_ov": (lambda S: [128, 2, 64], BF16),
    "nsa_selmul": (lambda S: [128, S // 128, 64], F32),
    "nsa_seladd": (lambda S: [128, S // 128, 64], F32),
}


_SBN = [0]


def _sb(nc, st, name, shape, dt):
    _SBN[0] += 1
    return st.enter_context(nc.sbuf_tensor("%s_%d" % (name, _SBN[0]), shape, dt))


def emit_retention(sc, nc, ps, cst, qT_d, kT_d, v_d, g_d, gain_d, y_d, S, heads=range(NH)):
    NT = S // 128
    NG = S // 512
    dchunk = cst["ret_dchunk"]
    with ExitStack() as st:
        rope = _sb(nc, st, "rt_rope", [128, 4, S], F32)
        perm = _sb(nc, st, "rt_perm", [128, 128], BF16)
        ident = _sb(nc, st, "rt_ident", [128, 128], BF16)
        decT = _sb(nc, st, "rt_decT", [128, NH, 128], F32)
        dfs = _sb(nc, st, "rt_dfs", [128, NH, 128], F32)
        dte = _sb(nc, st, "rt_dte", [128, NH], F32)
        gain = _sb(nc, st, "rt_gain", [128, NH * 128], F32)
        qT = _sb(nc, st, "rt_qT", [128, S], BF16)
        kT = _sb(nc, st, "rt_kT", [128, S], BF16)
        qr = _sb(nc, st, "rt_qr", [128, S], BF16)
        qrs = _sb(nc, st, "rt_qrs", [128, S], BF16)
        kr = _sb(nc, st, "rt_kr", [128, S], BF16)
        ktok = _sb(nc, st, "rt_ktok", [128, NT, 128], BF16)
        vsb = _sb(nc, st, "rt_v", [128, NT, 128], BF16)
        vs = _sb(nc, st, "rt_vs", [128, NT, 128], BF16)
        gsb = _sb(nc, st, "rt_g", [128, NT, 128], F32)
        yout = _sb(nc, st, "rt_yout", [128, NT, 128], BF16)
        t1 = [_sb(nc, st, "rt_t1_%d" % i, [128, 512], F32) for i in range(2)]
        t2 = [_sb(nc, st, "rt_t2_%d" % i, [128, 512], F32) for i in range(2)]
        state = _sb(nc, st, "rt_state", [128, 128], F32)
        state_bf = _sb(nc, st, "rt_state_bf", [128, 128], BF16)
        s_sb = [_sb(nc, st, "rt_s_%d" % i, [128, 128], BF16) for i in range(2)]
        stats = _sb(nc, st, "rt_stats", [128, 8], F32)
        mv = _sb(nc, st, "rt_mv", [128, 4], F32)
        rstd = _sb(nc, st, "rt_rstd", [128, 1], F32)
        yn = [_sb(nc, st, "rt_yn_%d" % i, [128, 128], F32) for i in range(2)]
        epsc = _sb(nc, st, "rt_eps", [128, 1], F32)
        sc.add("pool", lambda e: e.memset(epsc[:], EPS), w=["epsc"])

        sc.dma("sp", rope[:], cst["rope"], w=["rope"])
        sc.dma("sp", perm[:], cst["perm"], w=["perm"])
        sc.dma("sp", ident[:], cst["ident"], w=["ident"])
        sc.dma("sp", decT[:], cst["ret_decT"], w=["decT"])
        sc.dma("sp", dfs[:], cst["ret_dfs"], w=["dfs"])
        sc.dma("sp", dte[:], cst["ret_dte"], w=["dte"])
        sc.dma("sp", gain[:], gain_d.partition_broadcast(128), w=["gain"])
        psT = ps[2][:].bitcast(BF16)

        for h in heads:
            hs = slice(h * 128, (h + 1) * 128)
            sc.dma("sp", qT[:], qT_d[hs, :], w=["qT"])
            sc.dma("sp", kT[:], kT_d[hs, :], w=["kT"])
            sc.dma("sp", vsb[:], v_d[:, hs].rearrange("(n j) e -> j n e", j=128), w=["v"])
            sc.dma("sp", gsb[:], g_d[:, hs].rearrange("(n j) e -> j n e", j=128), w=["g"])
            cnt = 0
            for (src, skey, ci, si, dst, dkey, do_s) in ((qT, "qT", 0, 1, qr, "qr", True), (kT, "kT", 2, 3, kr, "kr", False)):
                for g in range(NG):
                    cs = slice(g * 512, (g + 1) * 512)
                    b = cnt % 2
                    cnt += 1
                    sc.add("pe", lambda e, b=b, src=src, cs=cs: e.matmul(ps[b][:, :], perm[:], src[:, cs], start=True, stop=True),
                           r=[skey, "perm"], w=[("ps", b)])
                    sc.add("dve", lambda e, b=b, src=src, cs=cs, ci=ci: e.tensor_tensor(out=t1[b][:], in0=src[:, cs], in1=rope[:, ci, cs], op=ALU.mult),
                           r=[skey, "rope"], w=[("t1", b)])
                    sc.add("dve", lambda e, b=b, cs=cs, si=si: e.tensor_tensor(out=t2[b][:], in0=ps[b][:, :], in1=rope[:, si, cs], op=ALU.mult),
                           r=[("ps", b), "rope"], w=[("t2", b)])
                    sc.add("pool", lambda e, b=b: e.tensor_tensor(out=t1[b][:], in0=t1[b][:], in1=t2[b][:], op=ALU.add),
                           r=[("t1", b), ("t2", b)], w=[("t1", b)])
                    sc.add("act", lambda e, b=b, dst=dst, cs=cs: e.copy(out=dst[:, cs], in_=t1[b][:]),
                           r=[("t1", b)], w=[dkey])
                    if do_s:
                        sc.add("pool", lambda e, b=b, cs=cs, h=h: e.tensor_tensor(
                            out=qrs[:, cs].rearrange("p (a c) -> p a c", c=128), in0=t1[b][:].rearrange("p (a c) -> p a c", c=128),
                            in1=dfs[:, h, :].unsqueeze(1).broadcast_to([128, 4, 128]), op=ALU.mult),
                            r=[("t1", b), "dfs"], w=["qrs"])
            for n in range(NT):
                sl = n % 8
                sc.add("pe", lambda e, n=n, sl=sl: e.transpose(psT[:, sl * 128:(sl + 1) * 128], kr[:, n * 128:(n + 1) * 128], ident[:]),
                       r=["kr", "ident"], w=[("ps", 2)])
                if sl == 7 or n == NT - 1:
                    n0 = n - sl
                    sc.add("act", lambda e, n0=n0, n=n, sl=sl: e.copy(
                        out=ktok[:, n0:n + 1, :], in_=psT[:, 0:(sl + 1) * 128].rearrange("p (a c) -> p a c", c=128)),
                        r=[("ps", 2)], w=["ktok"])
            sc.add("dve", lambda e, h=h: e.tensor_scalar(out=vs[:], in0=vsb[:], scalar1=dte[:, h:h + 1], scalar2=None, op0=ALU.mult),
                   r=["v", "dte"], w=["vs"])
            sc.add("act", lambda e: e.activation(out=gsb[:], in_=gsb[:], func=AF.Silu), r=["g"], w=["g"])
            units = []
            for n in range(NT):
                c = slice(n * 128, (n + 1) * 128)
                b = n % 2
                sbk = (3, 2)[n % 2]
                u = Unit()
                units.append(u)

                def stage_s(c=c, b=b, sbk=sbk, h=h):
                    sc.add("pe", lambda e: e.matmul(ps[sbk][:, :128], kr[:, c], qr[:, c], start=True, stop=True),
                           r=["kr", "qr"], w=[("ps", sbk)])
                    sc.add("dve", lambda e: e.tensor_tensor(out=s_sb[b][:], in0=ps[sbk][:, :128], in1=decT[:, h, :], op=ALU.mult),
                           r=[("ps", sbk), "decT"], w=[("s", b)])
                u.qk.append(stage_s)

                def stage_y(c=c, b=b, n=n, h=h):
                    sc.add("pe", lambda e: e.matmul(ps[6 + b][:, :128], ktok[:, n, :], vs[:, n, :], start=True, stop=True),
                           r=["ktok", "vs"], w=[("ps", 6 + b)])
                    sc.add("pe", lambda e: e.matmul(ps[4 + b][:, :128], s_sb[b][:], vsb[:, n, :], start=True, stop=(n == 0)),
                           r=[("s", b), "v"], w=[("ps", 4 + b)])
                    if n > 0:
                        sc.add("pe", lambda e: e.matmul(ps[4 + b][:, :128], qrs[:, c], state_bf[:], start=False, stop=True),
                               r=["qrs", "state_bf"], w=[("ps", 4 + b)])
                    if n == 0:
                        sc.add("dve", lambda e: e.tensor_copy(out=state[:], in_=ps[6 + b][:, :128]), r=[("ps", 6 + b)], w=["state"])
                    else:
                        sc.add("dve", lambda e: e.scalar_tensor_tensor(out=state[:], in0=state[:], scalar=dchunk[h], in1=ps[6 + b][:, :128],
                                                                       op0=ALU.mult, op1=ALU.add),
                               r=[("ps", 6 + b), "state"], w=["state"])
                    sc.add("act", lambda e: e.copy(out=state_bf[:], in_=state[:]), r=["state"], w=["state_bf"])
                u.pv.append(stage_y)

                def stage_e(b=b, n=n, hs=hs):
                    sc.add("dve", lambda e: e.bn_stats(out=stats[:, 0:6], in_=ps[4 + b][:, :128]), r=[("ps", 4 + b)], w=["stats"])
                    sc.add("dve", lambda e: e.bn_aggr(out=mv[:, 0:2], in_=stats[:, 0:6]), r=["stats"], w=["mv"])
                    sc.add("act", lambda e: e.activation(out=rstd[:], in_=mv[:, 1:2], func=AF.Sqrt, bias=epsc[:, 0:1], scale=1.0),
                           r=["mv", "epsc"], w=["rstd"])
                    sc.add("dve", lambda e: e.reciprocal(out=rstd[:], in_=rstd[:]), r=["rstd"], w=["rstd"])
                    sc.add("dve", lambda e: e.tensor_scalar(out=yn[b][:], in0=ps[4 + b][:, :128], scalar1=mv[:, 0:1], scalar2=rstd[:, 0:1],
                                                            op0=ALU.subtract, op1=ALU.mult),
                           r=[("ps", 4 + b), "mv", "rstd"], w=[("yn", b)])
                    sc.add("pool", lambda e: e.tensor_tensor(out=yn[b][:], in0=yn[b][:], in1=gain[:, hs], op=ALU.mult),
                           r=[("yn", b), "gain"], w=[("yn", b)])
                    sc.add("pool", lambda e: e.tensor_tensor(out=yout[:, n, :], in0=yn[b][:], in1=gsb[:, n, :], op=ALU.mult),
                           r=[("yn", b), "g"], w=["yout"])
                u.post.append(stage_e)
            run_pipelined(units, depth=1)
            sc.dma("pool", y_d[:, hs].rearrange("(n j) e -> j n e", j=128), yout[:], r=["yout"], w=[("y_d", h)])
        sc.flush()


class Unit:
    __slots__ = ("pre", "qk", "act", "pv", "post")

    def __init__(self):
        self.pre, self.qk, self.act, self.pv, self.post = [], [], [], [], []


def run_pipelined(units, depth=1):
    n = len(units)
    for idx in range(n + depth):
        if idx < n:
            for f in units[idx].pre:
                f()
            for f in units[idx].qk:
                f()
        if idx >= depth:
            u = units[idx - depth]
            for f in u.act:
                f()
            for f in u.pv:
                f()
            for f in u.post:
                f()


def emit_moba(sc, nc, ps, cst, qT_d, kT_d, v_d, y_d, S, heads=range(NH), side=None):
    NT = S // 128
    NB = S // MOBA_BLOCK
    scale = HD ** -0.5
    with ExitStack() as st:
        ident = _sb(nc, st, "mb_ident", [128, 128], BF16)
        identf = _sb(nc, st, "mb_identf", [128, 128], F32)
        tri = _sb(nc, st, "mb_tri", [128, 512], BF16)
        E = _sb(nc, st, "mb_E", [16, NT, 128], BF16)
        qT = [_sb(nc, st, "mb_qT%d" % i, [128, S], BF16) for i in range(2)]
        kT = [_sb(nc, st, "mb_kT%d" % i, [128, S], BF16) for i in range(2)]
        vaug = [_sb(nc, st, "mb_vaug%d" % i, [128, NT, 130], BF16) for i in range(2)]
        kmf = [_sb(nc, st, "mb_kmf%d" % i, [128, 16], F32) for i in range(2)]
        kmb = [_sb(nc, st, "mb_kmb%d" % i, [128, 16], BF16) for i in range(2)]
        gpad = [_sb(nc, st, "mb_gpad%d" % i, [128, 16], F32) for i in range(2)]
        max8 = [_sb(nc, st, "mb_max8%d" % i, [128, 8], F32) for i in range(2)]
        negm = [_sb(nc, st, "mb_negm%d" % i, [128, 16], F32) for i in range(2)]
        negmT = [_sb(nc, st, "mb_negmT%d" % i, [16, 128], BF16) for i in range(2)]
        p_sb = [_sb(nc, st, "mb_p%d" % i, [128, 512], BF16) for i in range(3)]
        rden = [_sb(nc, st, "mb_rden%d" % i, [128, 1], F32) for i in range(2)]
        yout = [_sb(nc, st, "mb_yout%d" % i, [128, NT, 128], BF16) for i in range(2)]
        sc.dma("sp", ident[:], cst["ident"], w=["ident"])
        sc.dma("sp", identf[:], cst["identf"], w=["identf"])
        sc.dma("sp", tri[:], cst["tri4"], w=["tri"])
        sc.dma("sp", E[:], cst["moba_E"], w=["E"])
        for i in range(2):
            sc.add("pool", lambda e, i=i: e.memset(vaug[i][:, :, 128:130], 1.0), w=[("vones", i)])
            sc.add("pool", lambda e, i=i: e.memset(kmf[i][:], 0.0), w=[("kmf", i)])
        units = []
        pcount = 0
        for h in heads:
            hs = slice(h * 128, (h + 1) * 128)
            hb = h % 2
            head_pre = []

            def load_head(hb=hb, hs=hs):
                sc.dma("sp", qT[hb][:], qT_d[hs, :], w=[("qT", hb)])
                sc.dma("sp", kT[hb][:], kT_d[hs, :], w=[("kT", hb)])
                sc.dma("sp", vaug[hb][:, :, 0:128], v_d[:, hs].rearrange("(n j) e -> j n e", j=128), w=[("v", hb)])
                sc.add("dve", lambda e: e.tensor_reduce(out=kmf[hb][:, 0:NB], in_=kT[hb][:].rearrange("p (n t) -> p n t", t=MOBA_BLOCK),
                                                        axis=AX.X, op=ALU.add), r=[("kT", hb)], w=[("kmf", hb)])
                sc.add("act", lambda e: e.activation(out=kmb[hb][:], in_=kmf[hb][:], func=AF.Copy, scale=1.0 / MOBA_BLOCK), r=[("kmf", hb)], w=[("kmb", hb)])
            head_pre.append(load_head)

            def mask_part1(i, hb=hb):
                mb = i % 2
                qs = slice(i * 128, (i + 1) * 128)
                qb = i // 2
                sc.add("pe", lambda e: e.matmul(ps[0][:, 0:16], qT[hb][:, qs], kmb[hb][:], start=True, stop=True),
                       r=[("qT", hb), ("kmb", hb)], w=[("ps", 0)])
                sc.add("pool", lambda e: e.memset(gpad[mb][:], -1e30), w=[("gpad", mb)])
                sc.add("dve", lambda e: e.tensor_copy(out=gpad[mb][:, 0:qb], in_=ps[0][:, 0:qb]), r=[("ps", 0)], w=[("gpad", mb)])
                sc.add("dve", lambda e: e.max(out=max8[mb][:], in_=gpad[mb][:]), r=[("gpad", mb)], w=[("max8", mb)])
                sc.add("dve", lambda e: e.tensor_scalar(out=negm[mb][:], in0=gpad[mb][:], scalar1=max8[mb][:, 2:3], scalar2=NEG,
                                                        op0=ALU.is_lt, op1=ALU.mult), r=[("gpad", mb), ("max8", mb)], w=[("negm", mb)])

            def mask_part2(i):
                mb = i % 2
                sc.add("pe", lambda e: e.transpose(ps[1][0:16, 0:128], negm[mb][:], identf[:]), r=[("negm", mb), "identf"], w=[("ps", 1)])
                sc.add("act", lambda e: e.copy(out=negmT[mb][:], in_=ps[1][0:16, 0:128]), r=[("ps", 1)], w=[("negmT", mb)])

            first_units = {}
            last_units = {}
            for i in range(NT):
                qs = slice(i * 128, (i + 1) * 128)
                qb = i // 2
                mb = i % 2
                ob = 5 + (i % 2)
                kts = list(range(i + 1))
                ngr = (len(kts) + 3) // 4
                for gi in range(ngr):
                    grp = kts[gi * 4:gi * 4 + 4]
                    u = Unit()
                    units.append(u)
                    if gi == 0:
                        first_units[i] = u
                        if i == 0:
                            u.pre.extend(head_pre)
                    if gi == ngr - 1:
                        last_units[i] = u
                    sb_ = (3, 4, 2)[pcount % 3]
                    pb = pcount % 3
                    pcount += 1
                    for a, kt in enumerate(grp):
                        ks = slice(kt * 128, (kt + 1) * 128)
                        cs = slice(a * 128, (a + 1) * 128)
                        masked = kt < 2 * qb
                        diag = kt == i

                        def qk(sb_=sb_, cs=cs, ks=ks, qs=qs, masked=masked, diag=diag, kt=kt, hb=hb, mb=mb):
                            sc.add("pe", lambda e: e.matmul(ps[sb_][:, cs], kT[hb][:, ks], qT[hb][:, qs], start=True, stop=not (masked or diag)),
                                   r=[("kT", hb), ("qT", hb)], w=[("ps", sb_)])
                            if masked:
                                sc.add("pe", lambda e: e.matmul(ps[sb_][:, cs], E[:, kt, :], negmT[mb][:], start=False, stop=True),
                                       r=["E", ("negmT", mb)], w=[("ps", sb_)])
                            elif diag:
                                sc.add("pe", lambda e: e.matmul(ps[sb_][:, cs], ident[:], tri[:, 0:128], start=False, stop=True),
                                       r=["ident", "tri"], w=[("ps", sb_)])
                        u.qk.append(qk)
                    w_ = len(grp) * 128

                    def act(sb_=sb_, pb=pb, w_=w_):
                        sc.add("act", lambda e: e.activation(out=p_sb[pb][:, 0:w_], in_=ps[sb_][:, 0:w_], func=AF.Exp, scale=scale),
                               r=[("ps", sb_)], w=[("p", pb)])
                    u.act.append(act)
                    for a, kt in enumerate(grp):
                        cs = slice(a * 128, (a + 1) * 128)

                        def pv(ob=ob, pb=pb, cs=cs, kt=kt, i=i, hb=hb):
                            sc.add("pe", lambda e: e.matmul(ps[ob][:, 0:129], p_sb[pb][:, cs], vaug[hb][:, kt, 0:129], start=(kt == 0), stop=(kt == i)),
                                   r=[("p", pb), ("v", hb), ("vones", hb)], w=[("ps", ob)])
                        u.pv.append(pv)

                def post(ob=ob, i=i, hb=hb, mb=mb, hs=hs):
                    sc.add("dve", lambda e: e.reciprocal(out=rden[mb][:], in_=ps[ob][:, 128:129]), r=[("ps", ob)], w=[("rden", mb)])
                    sc.add("dve", lambda e: e.tensor_scalar(out=yout[hb][:, i, :], in0=ps[ob][:, 0:128], scalar1=rden[mb][:, 0:1], scalar2=None, op0=ALU.mult),
                           r=[("ps", ob), ("rden", mb)], w=[("yout", hb)])
                    if i == NT - 1:
                        sc.dma("sp", y_d[:, hs].rearrange("(n j) e -> j n e", j=128), yout[hb][:], r=[("yout", hb)], w=[("y_d", hs.start)])
                last_units[i].post.append(post)
            for i in range(2, NT):
                first_units[i - 1].pre.append(lambda i=i, f=mask_part1: f(i))
                last_units[i - 1].pre.append(lambda i=i, f=mask_part2: f(i))
        if side is not None:
            sops = side(st)
            nu = len(units)
            for j, f in enumerate(sops):
                units[min(nu - 1, (j * nu) // len(sops))].pre.append(f)
        run_pipelined(units, depth=2)
        sc.flush()


def emit_moba2(sc, nc, ps, cst, qT_d, kT_d, v_d, y_d, S, heads=range(NH), side=None):
    NT = S // 128
    NB = S // MOBA_BLOCK
    NG = NT // 4
    scale = HD ** -0.5
    with ExitStack() as st:
        ident = _sb(nc, st, "mb_ident", [128, 128], BF16)
        identf = _sb(nc, st, "mb_identf", [128, 128], F32)
        tri = _sb(nc, st, "mb_tri", [128, 512], BF16)
        zeroT = _sb(nc, st, "mb_zeroT", [128, 128], BF16)
        E = _sb(nc, st, "mb_E", [16, NT, 128], BF16)
        qT = [_sb(nc, st, "mb_qT%d" % i, [128, S], BF16) for i in range(2)]
        kT = [_sb(nc, st, "mb_kT%d" % i, [128, S], BF16) for i in range(2)]
        vaug = [_sb(nc, st, "mb_vaug%d" % i, [128, NT, 130], BF16) for i in range(2)]
        kmf = [_sb(nc, st, "mb_kmf%d" % i, [128, 16], F32) for i in range(2)]
        kmb = [_sb(nc, st, "mb_kmb%d" % i, [128, 16], BF16) for i in range(2)]
        gpad = [_sb(nc, st, "mb_gpad%d" % i, [128, 16], F32) for i in range(2)]
        max8 = [_sb(nc, st, "mb_max8%d" % i, [128, 8], F32) for i in range(2)]
        negm = [_sb(nc, st, "mb_negm%d" % i, [128, 16], F32) for i in range(2)]
        negmT4 = [_sb(nc, st, "mb_negmT4%d" % i, [16, 512], BF16) for i in range(2)]
        p_sb = [_sb(nc, st, "mb_p%d" % i, [128, 512], BF16) for i in range(2)]
        rden = [_sb(nc, st, "mb_rden%d" % i, [128, 1], F32) for i in range(2)]
        yout = [_sb(nc, st, "mb_yout%d" % i, [128, NT, 128], BF16) for i in range(2)]
        sc.dma("sp", ident[:], cst["ident"], w=["ident"])
        sc.dma("sp", identf[:], cst["identf"], w=["identf"])
        sc.dma("sp", tri[:], cst["tri4"], w=["tri"])
        sc.dma("sp", E[:], cst["moba_E"], w=["E"])
        sc.add("pool", lambda e: e.memset(zeroT[:], 0.0), w=["zeroT"])
        for i in range(2):
            sc.add("pool", lambda e, i=i: e.memset(vaug[i][:, :, 128:130], 1.0), w=[("vones", i)])
            sc.add("pool", lambda e, i=i: e.memset(kmf[i][:], 0.0), w=[("kmf", i)])
            sc.add("pool", lambda e, i=i: e.memset(negmT4[i][:], 0.0), w=[("negmT4", i)])
        units = []
        pcount = 0
        for h in heads:
            hs = slice(h * 128, (h + 1) * 128)
            hb = h % 2

            def load_head(hb=hb, hs=hs):
                sc.dma("sp", qT[hb][:], qT_d[hs, :], w=[("qT", hb)])
                sc.dma("sp", kT[hb][:], kT_d[hs, :], w=[("kT", hb)])
                sc.dma("sp", vaug[hb][:, :, 0:128], v_d[:, hs].rearrange("(n j) e -> j n e", j=128), w=[("v", hb)])
                sc.add("dve", lambda e: e.tensor_reduce(out=kmf[hb][:, 0:NB], in_=kT[hb][:].rearrange("p (n t) -> p n t", t=MOBA_BLOCK),
                                                        axis=AX.X, op=ALU.add), r=[("kT", hb)], w=[("kmf", hb)])
                sc.add("act", lambda e: e.activation(out=kmb[hb][:], in_=kmf[hb][:], func=AF.Copy, scale=1.0 / MOBA_BLOCK), r=[("kmf", hb)], w=[("kmb", hb)])

            def mask_part1(i, hb=hb):
                mb = i % 2
                qs = slice(i * 128, (i + 1) * 128)
                qb = i // 2
                sc.add("pe", lambda e: e.matmul(ps[0][:, 0:16], qT[hb][:, qs], kmb[hb][:], start=True, stop=True),
                       r=[("qT", hb), ("kmb", hb)], w=[("ps", 0)])
                sc.add("pool", lambda e: e.memset(gpad[mb][:], -1e30), w=[("gpad", mb)])
                sc.add("dve", lambda e: e.tensor_copy(out=gpad[mb][:, 0:qb], in_=ps[0][:, 0:qb]), r=[("ps", 0)], w=[("gpad", mb)])
                sc.add("dve", lambda e: e.max(out=max8[mb][:], in_=gpad[mb][:]), r=[("gpad", mb)], w=[("max8", mb)])
                sc.add("dve", lambda e: e.tensor_scalar(out=negm[mb][:], in0=gpad[mb][:], scalar1=max8[mb][:, 2:3], scalar2=NEG,
                                                        op0=ALU.is_lt, op1=ALU.mult), r=[("gpad", mb), ("max8", mb)], w=[("negm", mb)])

            def mask_part2(i):
                mb = i % 2
                gp = (i // 4) % 2
                j = i % 4
                sc.add("pe", lambda e: e.transpose(ps[0][0:16, 128:256], negm[mb][:], identf[:]), r=[("negm", mb), "identf"], w=[("ps", 0)])
                sc.add("act", lambda e: e.copy(out=negmT4[gp][:, j * 128:(j + 1) * 128], in_=ps[0][0:16, 128:256]), r=[("ps", 0)], w=[("negmT4", gp)])

            group_units = {}
            for G in range(NG):
                i0 = 4 * G
                gp = G % 2
                obanks = (5, 6) if gp == 0 else (1, 2)
                gl = []
                group_units[G] = gl
                first_pv = [True]

                def zero_banks(obanks=obanks):
                    for ob in obanks:
                        sc.add("pe", lambda e, ob=ob: e.matmul(ps[ob][:, :], zeroT[:], tri[:], start=True, stop=False), r=["zeroT", "tri"], w=[("ps", ob)])
                for kt in range(i0):
                    u = Unit()
                    units.append(u)
                    gl.append(u)
                    sb_ = 3 + (pcount % 2)
                    pb = pcount % 2
                    pcount += 1
                    ks = slice(kt * 128, (kt + 1) * 128)

                    def qk(sb_=sb_, ks=ks, kt=kt, hb=hb, gp=gp, i0=i0):
                        sc.add("pe", lambda e: e.matmul(ps[sb_][:, :], kT[hb][:, ks], qT[hb][:, i0 * 128:(i0 + 4) * 128], start=True, stop=False),
                               r=[("kT", hb), ("qT", hb)], w=[("ps", sb_)])
                        sc.add("pe", lambda e: e.matmul(ps[sb_][:, :], E[:, kt, :], negmT4[gp][:], start=False, stop=True),
                               r=["E", ("negmT4", gp)], w=[("ps", sb_)])
                    u.qk.append(qk)

                    def act(sb_=sb_, pb=pb):
                        sc.add("act", lambda e: e.activation(out=p_sb[pb][:], in_=ps[sb_][:, :], func=AF.Exp, scale=scale), r=[("ps", sb_)], w=[("p", pb)])
                    u.act.append(act)
                    if first_pv[0]:
                        u.pv.append(zero_banks)
                        first_pv[0] = False

                    def pv(pb=pb, kt=kt, hb=hb, obanks=obanks):
                        for j in range(4):
                            ob = obanks[j // 2]
                            oc = (j % 2) * 256
                            sc.add("pe", lambda e, j=j, ob=ob, oc=oc: e.matmul(ps[ob][:, oc:oc + 129], p_sb[pb][:, j * 128:(j + 1) * 128], vaug[hb][:, kt, 0:129],
                                                                               start=False, stop=False),
                                   r=[("p", pb), ("v", hb), ("vones", hb)], w=[("ps", ob)])
                    u.pv.append(pv)
                for j in range(4):
                    i = i0 + j
                    qs = slice(i * 128, (i + 1) * 128)
                    qb = i // 2
                    mb = i % 2
                    ob = obanks[j // 2]
                    oc = (j % 2) * 256
                    kts = list(range(i0, i + 1))
                    u = Unit()
                    units.append(u)
                    gl.append(u)
                    if G == 0 and j == 0:
                        u.pre.append(load_head)
                    sb_ = 3 + (pcount % 2)
                    pb = pcount % 2
                    pcount += 1
                    for a, kt in enumerate(kts):
                        ks = slice(kt * 128, (kt + 1) * 128)
                        cs = slice(a * 128, (a + 1) * 128)
                        masked = kt < 2 * qb
                        diag = kt == i

                        def qk(sb_=sb_, cs=cs, ks=ks, qs=qs, masked=masked, diag=diag, kt=kt, hb=hb, gp=gp, j=j):
                            sc.add("pe", lambda e: e.matmul(ps[sb_][:, cs], kT[hb][:, ks], qT[hb][:, qs], start=True, stop=not (masked or diag)),
                                   r=[("kT", hb), ("qT", hb)], w=[("ps", sb_)])
                            if masked:
                                sc.add("pe", lambda e: e.matmul(ps[sb_][:, cs], E[:, kt, :], negmT4[gp][:, j * 128:(j + 1) * 128], start=False, stop=True),
                                       r=["E", ("negmT4", gp)], w=[("ps", sb_)])
                            elif diag:
                                sc.add("pe", lambda e: e.matmul(ps[sb_][:, cs], ident[:], tri[:, 0:128], start=False, stop=True),
                                       r=["ident", "tri"], w=[("ps", sb_)])
                        u.qk.append(qk)
                    w_ = len(kts) * 128

                    def act(sb_=sb_, pb=pb, w_=w_):
                        sc.add("act", lambda e: e.activation(out=p_sb[pb][:, 0:w_], in_=ps[sb_][:, 0:w_], func=AF.Exp, scale=scale),
                               r=[("ps", sb_)], w=[("p", pb)])
                    u.act.append(act)
                    if first_pv[0]:
                        u.pv.append(zero_banks)
                        first_pv[0] = False
                    for a, kt in enumerate(kts):
                        cs = slice(a * 128, (a + 1) * 128)

                        def pv(ob=ob, oc=oc, pb=pb, cs=cs, kt=kt, i=i, hb=hb):
                            sc.add("pe", lambda e: e.matmul(ps[ob][:, oc:oc + 129], p_sb[pb][:, cs], vaug[hb][:, kt, 0:129], start=False, stop=(kt == i)),
                                   r=[("p", pb), ("v", hb), ("vones", hb)], w=[("ps", ob)])
                        u.pv.append(pv)

                    def post(ob=ob, oc=oc, i=i, hb=hb, mb=mb, hs=hs):
                        sc.add("dve", lambda e: e.reciprocal(out=rden[mb][:], in_=ps[ob][:, oc + 128:oc + 129]), r=[("ps", ob)], w=[("rden", mb)])
                        sc.add("dve", lambda e: e.tensor_scalar(out=yout[hb][:, i, :], in0=ps[ob][:, oc:oc + 128], scalar1=rden[mb][:, 0:1], scalar2=None, op0=ALU.mult),
                               r=[("ps", ob), ("rden", mb)], w=[("yout", hb)])
                        if i == NT - 1:
                            sc.dma("sp", y_d[:, hs].rearrange("(n j) e -> j n e", j=128), yout[hb][:], r=[("yout", hb)], w=[("y_d", hs.start)])
                    u.post.append(post)
            g0 = group_units[0]
            g0[0].pre.append(lambda f=mask_part1: f(2))
            g0[1].pre.append(lambda f=mask_part2: f(2))
            g0[1].pre.append(lambda f=mask_part1: f(3))
            g0[2].pre.append(lambda f=mask_part2: f(3))
            for G in range(NG - 1):
                gl = group_units[G]
                n = len(gl)
                for j in range(4):
                    i = 4 * (G + 1) + j
                    a = min(n - 1, (j * n) // 4)
                    b = min(n - 1, a + max(1, n // 8))
                    if G == 0:
                        a, b = j, min(3, j + 1)
                    gl[a].pre.append(lambda i=i, f=mask_part1: f(i))
                    gl[b].pre.append(lambda i=i, f=mask_part2: f(i))
        if side is not None:
            sops = side(st)
            nu = len(units)
            for j, f in enumerate(sops):
                units[min(nu - 1, (j * nu) // len(sops))].pre.append(f)
        run_pipelined(units, depth=1)
        sc.flush()


def _col(ap1d, lo):
    return ap1d[lo:lo + 128].rearrange("(c o) -> c o", o=1)


def lru_side(sc, nc, ps, st, lxT_d, lgT_d, conv_w_d, conv_b_d, wa_d, ba_d, wx_d, bx_d, lam_d, yT_d, S, blocks=range(8), bank=7):
    GK = 1.5957691216057308
    CW = min(1024, S)
    xpad = _sb(nc, st, "lr_xpad", [128, S + 4], F32)
    xc = _sb(nc, st, "lr_xc", [128, S], F32)
    xcb = _sb(nc, st, "lr_xcb", [128, S], BF16)
    r_sb = _sb(nc, st, "lr_r", [128, S], F32)
    i_sb = _sb(nc, st, "lr_i", [128, S], F32)
    lg = _sb(nc, st, "lr_lg", [128, S], F32)
    t_sb = _sb(nc, st, "lr_t", [128, S], F32)
    yo = _sb(nc, st, "lr_yo", [128, S], BF16)
    wa = _sb(nc, st, "lr_wa", [128, 128], BF16)
    wx = _sb(nc, st, "lr_wx", [128, 128], BF16)
    par = _sb(nc, st, "lr_par", [128, 8], F32)
    sp = _sb(nc, st, "lr_sp", [128, 2], F32)
    one = _sb(nc, st, "lr_one", [128, 1], F32)
    h_sb = xc
    pk = ("ps", bank)
    ops = []

    def init():
        sc.add("pool", lambda e: e.memset(xpad[:, 0:3], 0.0), w=["lr_xpad0"])
        sc.add("pool", lambda e: e.memset(one[:], 1.0), w=["lr_one"])
    ops.append(init)
    for c in blocks:
        lo = c * 128
        cs = slice(lo, lo + 128)

        def loads(c=c, lo=lo, cs=cs):
            sc.dma("sp", xpad[:, 3:3 + S], lxT_d[cs, :], w=["lr_xpad"])
            sc.dma("sp", lg[:], lgT_d[cs, :], w=["lr_lg"])
            for tap in range(4):
                sc.dma("sp", par[:, tap:tap + 1], _col(conv_w_d[tap], lo), w=["lr_par"])
            for j, d in enumerate((conv_b_d, ba_d, bx_d, lam_d)):
                sc.dma("sp", par[:, 4 + j:5 + j], _col(d, lo), w=["lr_par"])
            sc.dma("pool", wa[:], wa_d[c], w=["lr_wa"])
            sc.dma("pool", wx[:], wx_d[c], w=["lr_wx"])
        ops.append(loads)

        def spchain():
            sc.add("act", lambda e: e.activation(out=sp[:, 0:1], in_=par[:, 7:8], func=AF.Exp, scale=-1.0), r=["lr_par"], w=["lr_sp0"])
            sc.add("act", lambda e: e.activation(out=sp[:, 1:2], in_=sp[:, 0:1], func=AF.Ln, bias=one[:, 0:1], scale=1.0), r=["lr_sp0", "lr_one"], w=["lr_sp1"])
            sc.add("dve", lambda e: e.tensor_scalar(out=sp[:, 1:2], in0=sp[:, 1:2], scalar1=-LRU_C, scalar2=None, op0=ALU.mult), r=["lr_sp1"], w=["lr_sp1"])
        ops.append(spchain)
        for k0 in range(0, S, CW):
            ks = slice(k0, k0 + CW)

            def conv(k0=k0, ks=ks):
                sc.add("dve", lambda e: e.tensor_scalar(out=xc[:, ks], in0=xpad[:, k0:k0 + CW], scalar1=par[:, 0:1], scalar2=par[:, 4:5], op0=ALU.mult, op1=ALU.add),
                       r=["lr_xpad", "lr_xpad0", "lr_par"], w=["lr_xc"])
                for tap in range(1, 4):
                    sc.add("dve", lambda e, tap=tap: e.scalar_tensor_tensor(out=xc[:, ks], in0=xpad[:, k0 + tap:k0 + tap + CW], scalar=par[:, tap:tap + 1], in1=xc[:, ks],
                                                                            op0=ALU.mult, op1=ALU.add),
                           r=["lr_xpad", "lr_xpad0", "lr_par", "lr_xc"], w=["lr_xc"])
                sc.add("act", lambda e: e.copy(out=xcb[:, ks], in_=xc[:, ks]), r=["lr_xc"], w=["lr_xcb"])
            ops.append(conv)
            for g0 in range(k0, k0 + CW, 512):
                gs = slice(g0, g0 + 512)

                def gates(gs=gs):
                    sc.add("pe", lambda e: e.matmul(ps[bank][:, :], wa[:], xcb[:, gs], start=True, stop=True), r=["lr_wa", "lr_xcb"], w=[pk])
                    sc.add("act", lambda e: e.activation(out=r_sb[:, gs], in_=ps[bank][:, :], func=AF.Sigmoid, bias=par[:, 5:6], scale=1.0),
                           r=[pk, "lr_par"], w=["lr_r"])
                    sc.add("pe", lambda e: e.matmul(ps[bank][:, :], wx[:], xcb[:, gs], start=True, stop=True), r=["lr_wx", "lr_xcb"], w=[pk])
                    sc.add("act", lambda e: e.activation(out=i_sb[:, gs], in_=ps[bank][:, :], func=AF.Sigmoid, bias=par[:, 6:7], scale=1.0),
                           r=[pk, "lr_par"], w=["lr_i"])
                ops.append(gates)

            def recur(k0=k0, ks=ks):
                sc.add("act", lambda e: e.activation(out=r_sb[:, ks], in_=r_sb[:, ks], func=AF.Exp, scale=sp[:, 1:2]), r=["lr_r", "lr_sp1"], w=["lr_r"])
                sc.add("dve", lambda e: e.tensor_tensor(out=t_sb[:, ks], in0=r_sb[:, ks], in1=r_sb[:, ks], op=ALU.mult), r=["lr_r"], w=["lr_t"])
                sc.add("dve", lambda e: e.tensor_scalar(out=t_sb[:, ks], in0=t_sb[:, ks], scalar1=-1.0, scalar2=1.0, op0=ALU.mult, op1=ALU.add), r=["lr_t"], w=["lr_t"])
                sc.add("dve", lambda e: e.tensor_scalar_max(out=t_sb[:, ks], in0=t_sb[:, ks], scalar1=0.0), r=["lr_t"], w=["lr_t"])
                sc.add("act", lambda e: e.activation(out=t_sb[:, ks], in_=t_sb[:, ks], func=AF.Sqrt), r=["lr_t"], w=["lr_t"])
                sc.add("pool", lambda e: e.tensor_tensor(out=i_sb[:, ks], in0=i_sb[:, ks], in1=xc[:, ks], op=ALU.mult), r=["lr_i", "lr_xc"], w=["lr_i"])
                sc.add("dve", lambda e: e.tensor_tensor(out=i_sb[:, ks], in0=i_sb[:, ks], in1=t_sb[:, ks], op=ALU.mult), r=["lr_i", "lr_t"], w=["lr_i"])
                init_ = 0.0 if k0 == 0 else h_sb[:, k0 - 1:k0]
                sc.add("dve", lambda e: e.tensor_tensor_scan(out=h_sb[:, ks], data0=r_sb[:, ks], data1=i_sb[:, ks], initial=init_, op0=ALU.mult, op1=ALU.add),
                       r=["lr_r", "lr_i", "lr_xc"], w=["lr_xc"])
            ops.append(recur)

            def gelu(ks=ks):
                sc.add("pool", lambda e: e.tensor_tensor(out=t_sb[:, ks], in0=lg[:, ks], in1=lg[:, ks], op=ALU.mult), r=["lr_lg", "lr_t"], w=["lr_t"])
                sc.add("pool", lambda e: e.tensor_tensor(out=t_sb[:, ks], in0=t_sb[:, ks], in1=lg[:, ks], op=ALU.mult), r=["lr_lg", "lr_t"], w=["lr_t"])
                sc.add("dve", lambda e: e.scalar_tensor_tensor(out=t_sb[:, ks], in0=t_sb[:, ks], scalar=0.044715, in1=lg[:, ks], op0=ALU.mult, op1=ALU.add),
                       r=["lr_lg", "lr_t"], w=["lr_t"])
                sc.add("act", lambda e: e.activation(out=t_sb[:, ks], in_=t_sb[:, ks], func=AF.Sigmoid, scale=GK), r=["lr_t"], w=["lr_t"])
                sc.add("dve", lambda e: e.tensor_tensor(out=t_sb[:, ks], in0=t_sb[:, ks], in1=lg[:, ks], op=ALU.mult), r=["lr_lg", "lr_t"], w=["lr_t"])
                sc.add("dve", lambda e: e.tensor_tensor(out=yo[:, ks], in0=t_sb[:, ks], in1=h_sb[:, ks], op=ALU.mult), r=["lr_t", "lr_xc"], w=["lr_yo"])
            ops.append(gelu)

        def store(cs=cs, c=c):
            sc.dma("sp", yT_d[cs, :], yo[:], r=["lr_yo"], w=[("lr_yT_d", c)])
        ops.append(store)
    return ops


def emit_lru(sc, nc, ps, lxT_d, lgT_d, conv_w_d, conv_b_d, wa_d, ba_d, wx_d, bx_d, lam_d, yT_d, S, blocks=range(8)):
    with ExitStack() as st:
        for f in lru_side(sc, nc, ps, st, lxT_d, lgT_d, conv_w_d, conv_b_d, wa_d, ba_d, wx_d, bx_d, lam_d, yT_d, S, blocks=blocks):
            f()
        sc.flush()


def emit_nsa(sc, nc, ps, cst, qT_d, kcT_d, vcT_d, ksT_d, vs_d, kwT_d, vw_d, gate_d,
             pos_k_d, w1_k_d, w2_k_d, pos_v_d, w1_v_d, w2_v_d, y_d, S, kvhs=range(NSA_KVH)):
    NT = S // 128
    NC = (S - CMP_LEN) // CMP_STRIDE + 1
    scale = HD ** -0.5
    GK = 1.5957691216057308
    with ExitStack() as st:
        ident = _sb(nc, st, "ns_ident", [128, 128], BF16)
        identf = _sb(nc, st, "ns_identf", [128, 128], F32)
        tri = _sb(nc, st, "ns_tri", [128, 512], BF16)
        tris = _sb(nc, st, "ns_tris", [128, 512], BF16)
        E2 = _sb(nc, st, "ns_E2", [64, NT, 128], BF16)
        cmask = _sb(nc, st, "ns_cmask", [128, 2, S], BF16)
        ovt = _sb(nc, st, "ns_ov", [128, 2, 64], BF16)
        selmul = _sb(nc, st, "ns_selmul", [128, NT, 64], F32)
        seladd = _sb(nc, st, "ns_seladd", [128, NT, 64], F32)
        gsb = _sb(nc, st, "ns_gate", [128, NT, 24], F32)
        xT = _sb(nc, st, "ns_xT", [128, S], BF16)
        Xl = _sb(nc, st, "ns_Xl", [128, 32, 256], BF16)
        w1 = _sb(nc, st, "ns_w1", [128, 32, 256], BF16)
        w2 = _sb(nc, st, "ns_w2", [128, 2, 128], BF16)
        pos = _sb(nc, st, "ns_pos", [32, 128], F32)
        posT = _sb(nc, st, "ns_posT", [128, 32], F32)
        tg = _sb(nc, st, "ns_tg", [128, 256], F32)
        hf = _sb(nc, st, "ns_hf", [128, 256], F32)
        hid = _sb(nc, st, "ns_hid", [128, 2, 256], BF16)
        kcT = _sb(nc, st, "ns_kcT", [128, 256], BF16)
        vcaug = _sb(nc, st, "ns_vcaug", [128, 2, 194], BF16)
        ksT = _sb(nc, st, "ns_ksT", [128, S], BF16)
        kwT = _sb(nc, st, "ns_kwT", [128, S], BF16)
        vsaug = _sb(nc, st, "ns_vsaug", [128, NT, 130], BF16)
        vwaug = _sb(nc, st, "ns_vwaug", [128, NT, 130], BF16)
        q4 = [_sb(nc, st, "ns_q4_%d" % i, [128, 4, 128], BF16) for i in range(2)]
        pc = _sb(nc, st, "ns_pc", [128, 512], BF16)
        pcm = [_sb(nc, st, "ns_pcm%d" % i, [128, 512], BF16) for i in range(2)]
        p_sb = [_sb(nc, st, "ns_p%d" % i, [128, 512], BF16) for i in range(2)]
        ocmp = [_sb(nc, st, "ns_ocmp%d" % i, [128, 4, 128], F32) for i in range(2)]
        rdc = [_sb(nc, st, "ns_rdc%d" % i, [128, 4], F32) for i in range(2)]
        impacc = [_sb(nc, st, "ns_imp%d" % i, [128, 64], F32) for i in range(2)]
        imp3 = _sb(nc, st, "ns_imp3", [128, 64], F32)
        max8a = _sb(nc, st, "ns_max8a", [128, 8], F32)
        max8b = _sb(nc, st, "ns_max8b", [128, 8], F32)
        negs = [_sb(nc, st, "ns_negs%d" % i, [128, 64], F32) for i in range(2)]
        negsT4 = [_sb(nc, st, "ns_negsT4%d" % i, [64, 4, 128], BF16) for i in range(2)]
        rsw = _sb(nc, st, "ns_rsw", [128, 8], F32)
        acc = _sb(nc, st, "ns_acc", [128, 128], F32)
        yt = [_sb(nc, st, "ns_yt%d" % i, [128, 512], BF16) for i in range(2)]
        zeroT = _sb(nc, st, "ns_zeroT", [128, 128], BF16)
        sc.add("pool", lambda e: e.memset(zeroT[:], 0.0), w=["zeroT"])

        for (t_, name) in ((ident, "ident"), (identf, "identf"), (tri, "tri4"), (tris, "tris4"), (E2, "nsa_E2"), (cmask, "nsa_cmask"),
                           (ovt, "nsa_ov"), (selmul, "nsa_selmul"), (seladd, "nsa_seladd")):
            sc.dma("sp", t_[:], cst[name], w=[name])
        sc.dma("sp", gsb[:], gate_d.rearrange("(n j) c -> j n c", j=128), w=["gate"])
        sc.add("act", lambda e: e.activation(out=gsb[:], in_=gsb[:], func=AF.Sigmoid), r=["gate"], w=["gate"])
        sc.add("pool", lambda e: e.memset(vsaug[:, :, 128:130], 1.0), w=["vs1"])
        sc.add("pool", lambda e: e.memset(vwaug[:, :, 128:130], 1.0), w=["vw1"])
        sc.add("pool", lambda e: e.memset(vcaug[:, :, 128:129], 1.0), w=["vc1"])
        sc.add("pool", lambda e: e.tensor_copy(out=vcaug[:, :, 129:193], in_=ovt[:]), r=["nsa_ov"], w=["vcov"])
        sc.add("pool", lambda e: e.memset(hid[:], 0.0), w=["hid"])
        pcount = 0
        tcount = 0
        for kvh in kvhs:
            ks_ = slice(kvh * 128, (kvh + 1) * 128)
            for which, src_d, pos_d, w1_d, w2_d in (("k", kcT_d, pos_k_d, w1_k_d, w2_k_d), ("v", vcT_d, pos_v_d, w1_v_d, w2_v_d)):
                sc.dma("sp", xT[:], src_d[ks_, :], w=["xT"])
                sc.dma("sp", pos[:], pos_d, w=["pos"])
                sc.dma("pool", w1[:], w1_d.rearrange("(l d) c -> d l c", d=128), w=["w1"])
                sc.dma("pool", w2[:], w2_d.rearrange("(m p) d -> p m d", p=128), w=["w2"])
                sc.add("pe", lambda e: e.transpose(ps[0][:, 0:32], pos[:], identf[0:32, 0:32]), r=["pos", "identf"], w=[("ps", 0)])
                sc.add("act", lambda e: e.copy(out=posT[:], in_=ps[0][:, 0:32]), r=[("ps", 0)], w=["posT"])
                V = xT[:].rearrange("p (m s) -> p m s", s=16)
                for a in range(2):
                    sc.add("dve", lambda e, a=a, V=V: e.tensor_tensor(
                        out=Xl[:, a * 16:(a + 1) * 16, 0:NC], in0=V[:, a:a + NC, :].rearrange("p n s -> p s n"),
                        in1=posT[:, a * 16:(a + 1) * 16].unsqueeze(2).broadcast_to([128, 16, NC]), op=ALU.add),
                        r=["xT", "posT"], w=["Xl"])
                for m in range(2):
                    for l in range(32):
                        sc.add("pe", lambda e, m=m, l=l: e.matmul(ps[1 + m][:, 0:NC], w1[:, l, m * 128:(m + 1) * 128], Xl[:, l, 0:NC],
                                                                  start=(l == 0), stop=(l == 31)),
                               r=["w1", "Xl"], w=[("ps", 1 + m)])
                    sc.add("act", lambda e, m=m: e.copy(out=hf[:, 0:NC], in_=ps[1 + m][:, 0:NC]), r=[("ps", 1 + m)], w=["hf"])
                    pm = hf[:, 0:NC]
                    sc.add("dve", lambda e, pm=pm: e.tensor_tensor(out=tg[:, 0:NC], in0=pm, in1=pm, op=ALU.mult), r=["hf"], w=["tg"])
                    sc.add("dve", lambda e, pm=pm: e.tensor_tensor(out=tg[:, 0:NC], in0=tg[:, 0:NC], in1=pm, op=ALU.mult), r=["hf", "tg"], w=["tg"])
                    sc.add("dve", lambda e, pm=pm: e.scalar_tensor_tensor(out=tg[:, 0:NC], in0=tg[:, 0:NC], scalar=0.044715, in1=pm, op0=ALU.mult, op1=ALU.add),
                           r=["hf", "tg"], w=["tg"])
                    sc.add("act", lambda e: e.activation(out=tg[:, 0:NC], in_=tg[:, 0:NC], func=AF.Sigmoid, scale=GK), r=["tg"], w=["tg"])
                    sc.add("dve", lambda e, pm=pm, m=m: e.tensor_tensor(out=hid[:, m, 0:NC], in0=tg[:, 0:NC], in1=pm, op=ALU.mult),
                           r=["hf", "tg"], w=["hid"])
                if which == "k":
                    for m in range(2):
                        sc.add("pe", lambda e, m=m: e.matmul(ps[3][:, 0:256], w2[:, m, :], hid[:, m, :], start=(m == 0), stop=(m == 1)),
                               r=["w2", "hid"], w=[("ps", 3)])
                    sc.add("act", lambda e: e.copy(out=kcT[:], in_=ps[3][:, 0:256]), r=[("ps", 3)], w=["kcT"])
                else:
                    for nt in range(2):
                        for m in range(2):
                            sc.add("pe", lambda e, m=m, nt=nt: e.matmul(ps[3][:, nt * 128:(nt + 1) * 128], hid[:, m, nt * 128:(nt + 1) * 128], w2[:, m, :],
                                                                        start=(m == 0), stop=(m == 1)),
                                   r=["w2", "hid"], w=[("ps", 3)])
                    sc.add("act", lambda e: e.copy(out=vcaug[:, :, 0:128], in_=ps[3][:, 0:256].rearrange("p (a c) -> p a c", c=128)),
                           r=[("ps", 3)], w=["vc"])
            sc.dma("sp", ksT[:], ksT_d[ks_, :], w=["ksT"])
            sc.dma("sp", kwT[:], kwT_d[ks_, :], w=["kwT"])
            sc.dma("sp", vsaug[:, :, 0:128], vs_d[:, ks_].rearrange("(n j) e -> j n e", j=128), w=["vs"])
            sc.dma("sp", vwaug[:, :, 0:128], vw_d[:, ks_].rearrange("(n j) e -> j n e", j=128), w=["vw"])

            def cmp_a(j, kvh=kvh):
                s_ = j % 2
                qs = slice(j * 128, (j + 1) * 128)
                qq = q4[s_]
                sc.dma("sp", qq[:], qT_d[kvh * 512:(kvh + 1) * 512, qs].rearrange("(g d) q -> d g q", d=128), w=[("q4", s_)])
                nts = [0] if 8 * j + 6 < 128 else [0, 1]
                for nt in nts:
                    sc.add("pe", lambda e, nt=nt: e.matmul(ps[0][:, :], kcT[:, nt * 128:(nt + 1) * 128], qq[:], start=True, stop=True),
                           r=["kcT", ("q4", s_)], w=[("ps", 0)])
                    sc.add("act", lambda e: e.activation(out=pc[:], in_=ps[0][:, :], func=AF.Exp, scale=scale), r=[("ps", 0)], w=["pc"])
                    sc.add("dve", lambda e, nt=nt: e.tensor_tensor(
                        out=pcm[nt][:].rearrange("p (g q) -> p g q", g=4), in0=pc[:].rearrange("p (g q) -> p g q", g=4),
                        in1=cmask[:, nt, qs].unsqueeze(1).broadcast_to([128, 4, 128]), op=ALU.mult), r=["pc", "nsa_cmask"], w=[("pcm", nt)])

            def cmp_b(j):
                s_ = j % 2
                nts = [0] if 8 * j + 6 < 128 else [0, 1]
                for pss in range(2):
                    sc.add("pe", lambda e: e.matmul(ps[7][:, :], zeroT[:], tri[:], start=True, stop=False), r=["zeroT", "tri4"], w=[("ps", 7)])
                    for g in (2 * pss, 2 * pss + 1):
                        oc = (g % 2) * 256
                        for nt in nts:
                            sc.add("pe", lambda e, g=g, oc=oc, nt=nt: e.matmul(
                                ps[7][:, oc:oc + 193], pcm[nt][:, g * 128:(g + 1) * 128], vcaug[:, nt, 0:193], start=False, stop=(nt == nts[-1])),
                                r=[("pcm", nt), "vc", "vc1", "vcov"], w=[("ps", 7)])
                    for g in (2 * pss, 2 * pss + 1):
                        oc = (g % 2) * 256
                        sc.add("dve", lambda e, g=g, oc=oc: e.tensor_scalar_max(out=rdc[s_][:, g:g + 1], in0=ps[7][:, oc + 128:oc + 129], scalar1=1e-30),
                               r=[("ps", 7)], w=[("rdc", s_)])
                        sc.add("dve", lambda e, g=g: e.reciprocal(out=rdc[s_][:, g:g + 1], in_=rdc[s_][:, g:g + 1]), r=[("rdc", s_)], w=[("rdc", s_)])
                        sc.add("dve", lambda e, g=g, oc=oc: e.tensor_scalar(out=ocmp[s_][:, g, :], in0=ps[7][:, oc:oc + 128], scalar1=rdc[s_][:, g:g + 1],
                                                                          scalar2=None, op0=ALU.mult), r=[("ps", 7), ("rdc", s_)], w=[("ocmp", s_)])
                        if g == 0:
                            sc.add("dve", lambda e, g=g, oc=oc: e.tensor_scalar(out=impacc[s_][:], in0=ps[7][:, oc + 129:oc + 193], scalar1=rdc[s_][:, g:g + 1],
                                                                              scalar2=None, op0=ALU.mult), r=[("ps", 7), ("rdc", s_)], w=[("imp", s_)])
                        else:
                            sc.add("dve", lambda e, g=g, oc=oc: e.scalar_tensor_tensor(out=impacc[s_][:], in0=ps[7][:, oc + 129:oc + 193], scalar=rdc[s_][:, g:g + 1],
                                                                                     in1=impacc[s_][:], op0=ALU.mult, op1=ALU.add),
                                   r=[("ps", 7), ("rdc", s_), ("imp", s_)], w=[("imp", s_)])
                sc.add("dve", lambda e: e.tensor_tensor(out=impacc[s_][:], in0=impacc[s_][:], in1=selmul[:, j, :], op=ALU.mult), r=[("imp", s_), "nsa_selmul"], w=[("imp", s_)])
                sc.add("dve", lambda e: e.tensor_tensor(out=impacc[s_][:], in0=impacc[s_][:], in1=seladd[:, j, :], op=ALU.add), r=[("imp", s_), "nsa_seladd"], w=[("imp", s_)])
                sc.add("dve", lambda e: e.max(out=max8a[:], in_=impacc[s_][:]), r=[("imp", s_)], w=["max8a"])
                sc.add("dve", lambda e: e.match_replace(out=imp3[:], in_to_replace=max8a[:], in_values=impacc[s_][:], imm_value=-3e9), r=[("imp", s_), "max8a"], w=["imp3"])
                sc.add("dve", lambda e: e.max(out=max8b[:], in_=imp3[:]), r=["imp3"], w=["max8b"])
                sc.add("dve", lambda e: e.tensor_scalar(out=negs[s_][:], in0=impacc[s_][:], scalar1=max8b[:, 7:8], scalar2=NEG, op0=ALU.is_lt, op1=ALU.mult),
                       r=[("imp", s_), "max8b"], w=[("negs", s_)])

            def cmp_c(j):
                s_ = j % 2
                sc.add("pe", lambda e: e.transpose(ps[0][0:64, 0:128], negs[s_][:], identf[:]), r=[("negs", s_), "identf"], w=[("ps", 0)])
                sc.add("act", lambda e: e.copy(out=negsT4[s_][:], in_=ps[0][0:64, 0:128].unsqueeze(1).broadcast_to([64, 4, 128])),
                       r=[("ps", 0)], w=[("negsT4", s_)])

            units = []
            tile_units = {}
            for i in range(NT):
                qs = slice(i * 128, (i + 1) * 128)
                s_ = i % 2
                qq = q4[s_]
                tl = []
                for branch in ("win", "sel"):
                    if branch == "win":
                        kts = list(range(max(0, i - 4), i + 1))
                        kT_, kkey, vaug_, vkeys, obase = kwT, "kwT", vwaug, ["vw", "vw1"], 1
                    else:
                        kts = list(range(i + 1))
                        kT_, kkey, vaug_, vkeys, obase = ksT, "ksT", vsaug, ["vs", "vs1"], 5
                    for kt in kts:
                        u = Unit()
                        units.append(u)
                        tl.append(u)
                        ksl = slice(kt * 128, (kt + 1) * 128)
                        sb_ = 3 + (pcount % 2)
                        pb = pcount % 2
                        pcount += 1
                        extra = []
                        if branch == "sel":
                            extra.append((E2[:, kt, :], negsT4[s_][:].rearrange("p g q -> p (g q)"), ["nsa_E2", ("negsT4", s_)]))
                        if kt == i:
                            extra.append((ident[:], tri[:], ["ident", "tri4"]))
                        if branch == "win" and kt == i - 4:
                            extra.append((ident[:], tris[:], ["ident", "tris4"]))

                        def qk(sb_=sb_, ksl=ksl, kT_=kT_, kkey=kkey, qq=qq, s_=s_, extra=extra):
                            sc.add("pe", lambda e: e.matmul(ps[sb_][:, :], kT_[:, ksl], qq[:], start=True, stop=(len(extra) == 0)),
                                   r=[kkey, ("q4", s_)], w=[("ps", sb_)])
                            for xi, (l_, r_, keys) in enumerate(extra):
                                sc.add("pe", lambda e, l_=l_, r_=r_, last=(xi == len(extra) - 1): e.matmul(ps[sb_][:, :], l_, r_, start=False, stop=last),
                                       r=keys, w=[("ps", sb_)])
                        u.qk.append(qk)

                        def act(sb_=sb_, pb=pb):
                            sc.add("act", lambda e: e.activation(out=p_sb[pb][:], in_=ps[sb_][:, :], func=AF.Exp, scale=scale),
                                   r=[("ps", sb_)], w=[("p", pb)])
                        u.act.append(act)

                        def pv(pb=pb, kt=kt, kts=kts, vaug_=vaug_, vkeys=vkeys, obase=obase):
                            if kt == kts[0]:
                                for ob in (obase, obase + 1):
                                    sc.add("pe", lambda e, ob=ob: e.matmul(ps[ob][:, :], zeroT[:], tri[:], start=True, stop=False), r=["zeroT", "tri4"], w=[("ps", ob)])
                            for g in range(4):
                                ob = obase + g // 2
                                oc = (g % 2) * 256
                                sc.add("pe", lambda e, g=g, ob=ob, oc=oc: e.matmul(
                                    ps[ob][:, oc:oc + 129], p_sb[pb][:, g * 128:(g + 1) * 128], vaug_[:, kt, 0:129], start=False, stop=(kt == kts[-1])),
                                    r=[("p", pb)] + vkeys, w=[("ps", ob)])
                        u.pv.append(pv)

                def post(i=i, s_=s_, qs=qs, kvh=kvh):
                    yb = i % 2
                    for g in range(4):
                        hq = kvh * 4 + g
                        oc = (g % 2) * 256
                        bw, bs = 1 + g // 2, 5 + g // 2
                        sc.add("dve", lambda e, bw=bw, oc=oc: e.reciprocal(out=rsw[:, 0:1], in_=ps[bw][:, oc + 128:oc + 129]), r=[("ps", bw)], w=["rsw"])
                        sc.add("dve", lambda e, bs=bs, oc=oc: e.reciprocal(out=rsw[:, 1:2], in_=ps[bs][:, oc + 128:oc + 129]), r=[("ps", bs)], w=["rsw"])
                        sc.add("dve", lambda e, hq=hq: e.tensor_tensor(out=rsw[:, 2:3], in0=rsw[:, 0:1], in1=gsb[:, i, hq * 3 + 2:hq * 3 + 3], op=ALU.mult),
                               r=["rsw", "gate"], w=["rsw"])
                        sc.add("dve", lambda e, hq=hq: e.tensor_tensor(out=rsw[:, 3:4], in0=rsw[:, 1:2], in1=gsb[:, i, hq * 3 + 1:hq * 3 + 2], op=ALU.mult),
                               r=["rsw", "gate"], w=["rsw"])
                        sc.add("dve", lambda e, g=g, hq=hq: e.tensor_scalar(out=acc[:], in0=ocmp[s_][:, g, :], scalar1=gsb[:, i, hq * 3:hq * 3 + 1], scalar2=None, op0=ALU.mult),
                               r=[("ocmp", s_), "gate"], w=["acc"])
                        sc.add("dve", lambda e, bw=bw, oc=oc: e.scalar_tensor_tensor(out=acc[:], in0=ps[bw][:, oc:oc + 128], scalar=rsw[:, 2:3], in1=acc[:],
                                                                                    op0=ALU.mult, op1=ALU.add), r=[("ps", bw), "rsw", "acc"], w=["acc"])
                        sc.add("dve", lambda e, bs=bs, oc=oc, g=g: e.scalar_tensor_tensor(out=yt[yb][:, g * 128:(g + 1) * 128], in0=ps[bs][:, oc:oc + 128], scalar=rsw[:, 3:4],
                                                                                         in1=acc[:], op0=ALU.mult, op1=ALU.add),
                               r=[("ps", bs), "rsw", "acc"], w=[("yt", yb)])
                    sc.dma("sp", y_d[qs, kvh * 512:(kvh + 1) * 512], yt[yb][:], r=[("yt", yb)], w=[("y_d", kvh, i)])
                tl[-1].post.append(post)
                tile_units[i] = tl
            tile_units[0][0].pre.extend([lambda f=cmp_a: f(0), lambda f=cmp_b: f(0), lambda f=cmp_c: f(0)])
            for i in range(NT - 1):
                tl = tile_units[i]
                n = len(tl)
                tl[1].pre.append(lambda j=i + 1, f=cmp_a: f(j))
                tl[min(2, n - 1)].pre.append(lambda j=i + 1, f=cmp_b: f(j))
                tl[n - 1].pre.append(lambda j=i + 1, f=cmp_c: f(j))
            run_pipelined(units)
        sc.flush()


class PsRot:
    def __init__(self, banks):
        self.banks = list(banks)
        self.i = 0

    def next(self):
        b = self.banks[self.i % len(self.banks)]
        self.i += 1
        return b


def emit_fill_norm(sc, nc, ps, AT, TG, x_d, tok0, w_d, ident_d):
    KC = D_MODEL // 128
    with ExitStack() as st:
        ident = _sb(nc, st, "fn_ident", [128, 128], BF16)
        wb = _sb(nc, st, "fn_wb", [128, D_MODEL], F32)
        xt = [_sb(nc, st, "fn_x%d" % i, [128, D_MODEL], F32) for i in range(2)]
        xn = [_sb(nc, st, "fn_xn%d" % i, [128, D_MODEL], BF16) for i in range(2)]
        ssq = _sb(nc, st, "fn_ssq", [128, 2], F32)
        epsc = _sb(nc, st, "fn_eps", [128, 1], F32)
        sc.dma("sp", ident[:], ident_d, w=["ident"])
        sc.dma("sp", wb[:], w_d.partition_broadcast(128), w=["wb"])
        sc.add("pool", lambda e: e.memset(epsc[:], EPS), w=["epsc"])
        rot = PsRot(range(8))
        ev = 0
        for ti in range(TG // 128):
            b = ti % 2
            t0 = tok0 + ti * 128
            hD = D_MODEL // 2
            sc.dma("sp", xt[b][:, 0:hD], x_d[t0:t0 + 128, 0:hD], w=[("x", b)])
            sc.dma("pool", xt[b][:, hD:D_MODEL], x_d[t0:t0 + 128, hD:D_MODEL], w=[("x", b)])
            sc.add("pool", lambda e: e.memset(ssq[:, 0:1], 0.0), w=["ssq"])
            sc.add("act", lambda e, b=b: e.activation(out=xn[b][:], in_=xt[b][:], func=AF.Square, accum_out=ssq[:, 0:1]),
                   r=[("x", b)], w=[("xn", b), "ssq"])
            sc.add("act", lambda e: e.activation(out=ssq[:, 1:2], in_=ssq[:, 0:1], func=AF.Sqrt, bias=epsc[:, 0:1], scale=1.0 / D_MODEL),
                   r=["ssq", "epsc"], w=["rstd"])
            sc.add("dve", lambda e: e.reciprocal(out=ssq[:, 1:2], in_=ssq[:, 1:2]), r=["rstd"], w=["rstd"])
            sc.add("dve", lambda e, b=b: e.scalar_tensor_tensor(out=xn[b][:], in0=xt[b][:], scalar=ssq[:, 1:2], in1=wb[:], op0=ALU.mult, op1=ALU.mult),
                   r=[("x", b), "rstd", "wb"], w=[("xn", b)])
            for k0 in range(0, KC, 8):
                pb = rot.next()
                pT = ps[pb][:].bitcast(BF16)
                for j in range(8):
                    kc = k0 + j
                    sc.add("pe", lambda e, pT=pT, j=j, kc=kc, b=b: e.transpose(pT[:, j * 128:(j + 1) * 128], xn[b][:, kc * 128:(kc + 1) * 128], ident[:]),
                           r=[("xn", b), "ident"], w=[("ps", pb)])
                eng = "act" if ev % 2 == 0 else "dve"
                ev += 1
                dst = AT[:, k0:k0 + 8, ti * 128:(ti + 1) * 128]
                src = pT[:, 0:1024].rearrange("p (a c) -> p a c", c=128)
                if eng == "act":
                    sc.add("act", lambda e, dst=dst, src=src: e.copy(out=dst, in_=src), r=[("ps", pb)], w=[("AT", ti)])
                else:
                    sc.add("dve", lambda e, dst=dst, src=src: e.tensor_copy(out=dst, in_=src), r=[("ps", pb)], w=[("AT", ti)])
        sc.flush()


def emit_fill_y(sc, nc, ps, AT, TG, ycat_d, lruT_d, tok0, ident_d):
    with ExitStack() as st:
        ident = _sb(nc, st, "fy_ident", [128, 128], BF16)
        yt = [_sb(nc, st, "fy_y%d" % i, [128, D_MODEL], BF16) for i in range(2)]
        sc.dma("sp", ident[:], ident_d, w=["ident"])
        sc.dma("sp", AT[:, 16:24, :], lruT_d[:, tok0:tok0 + TG].rearrange("(kc p) t -> p kc t", p=128), w=["AT_lru"])
        rot = PsRot(range(8))
        ev = 0
        for ti in range(TG // 128):
            b = ti % 2
            t0 = tok0 + ti * 128
            sc.dma("sp", yt[b][:, 0:2048], ycat_d[t0:t0 + 128, 0:2048], w=[("y", b)])
            sc.dma("pool", yt[b][:, 3072:4096], ycat_d[t0:t0 + 128, 3072:4096], w=[("y", b)])
            for k0 in (0, 8, 24):
                pb = rot.next()
                pT = ps[pb][:].bitcast(BF16)
                for j in range(8):
                    kc = k0 + j
                    sc.add("pe", lambda e, pT=pT, j=j, kc=kc, b=b: e.transpose(pT[:, j * 128:(j + 1) * 128], yt[b][:, kc * 128:(kc + 1) * 128], ident[:]),
                           r=[("y", b), "ident"], w=[("ps", pb)])
                dst = AT[:, k0:k0 + 8, ti * 128:(ti + 1) * 128]
                src = pT[:, 0:1024].rearrange("p (a c) -> p a c", c=128)
                if ev % 2 == 0:
                    sc.add("act", lambda e, dst=dst, src=src: e.copy(out=dst, in_=src), r=[("ps", pb)], w=[("AT", ti)])
                else:
                    sc.add("dve", lambda e, dst=dst, src=src: e.tensor_copy(out=dst, in_=src), r=[("ps", pb)], w=[("AT", ti)])
                ev += 1
        sc.flush()


def emit_gemm(sc, nc, ps, AT, KC, TG, W_d, blocks, wslots, at_keys=None):
    rot = PsRot(range(8))
    atk = at_keys if at_keys is not None else [("AT", i) for i in range(TG // 128)] + ["AT_lru"]
    for bi, blk in enumerate(blocks):
        slot = bi % len(wslots)
        wsl = wslots[slot]
        c0, width = blk["c0"], blk["width"]
        for k0 in range(0, KC, 16):
            k1 = min(KC, k0 + 16)
            sc.dma("pool", wsl[:, k0:k1, 0:width], W_d[k0 * 128:k1 * 128, c0:c0 + width].rearrange("(kc p) n -> p kc n", p=128), w=[("w", slot)])
        if blk["variant"] == "F":
            for m in range((width + 127) // 128):
                mw = min(128, width - m * 128)
                for tg in range(TG // 512):
                    pb = rot.next()
                    for kc in range(KC):
                        sc.add("pe", lambda e, pb=pb, mw=mw, wsl=wsl, kc=kc, m=m, tg=tg: e.matmul(
                            ps[pb][0:mw, :], wsl[:, kc, m * 128:m * 128 + mw], AT[:, kc, tg * 512:(tg + 1) * 512], start=(kc == 0), stop=(kc == KC - 1)),
                            r=[("w", slot)] + atk, w=[("ps", pb)])
                    blk["epi"](sc, ps[pb][0:mw, :], ("ps", pb), blk, (m, tg, mw))
        else:
            for tt in range(TG // 128):
                pb = rot.next()
                for kc in range(KC):
                    sc.add("pe", lambda e, pb=pb, wsl=wsl, kc=kc, tt=tt, width=width: e.matmul(
                        ps[pb][:, 0:width], AT[:, kc, tt * 128:(tt + 1) * 128], wsl[:, kc, 0:width], start=(kc == 0), stop=(kc == KC - 1)),
                        r=[("w", slot)] + atk, w=[("ps", pb)])
                blk["epi"](sc, ps[pb][:, 0:width], ("ps", pb), blk, (tt,))


class Stager:
    def __init__(self, nc, st, name, shape, dt, n=4):
        self.t = [_sb(nc, st, "%s%d" % (name, i), shape, dt) for i in range(n)]
        self.name = name
        self.i = 0

    def next(self):
        k = self.i % len(self.t)
        self.i += 1
        return self.t[k], (self.name, k)


def _split_blocks(c0, width, step=512):
    out = []
    o = 0
    while o < width:
        w = min(step, width - o)
        out.append((c0 + o, w, o))
        o += w
    return out


def emit_gateup(sc, nc, ps, AT, TG, tok0, wg_d, wu_d, uT_d, wslots_g, wslots_u, st_f, st_b):
    KC = D_MODEL // 128
    rot = PsRot(range(8))
    atk = [("AT", i) for i in range(TG // 128)]
    WC = 256
    for bi, (c0, width, _) in enumerate(_split_blocks(0, D_FF, WC)):
        slot = bi % 2
        wg, wu = wslots_g[slot], wslots_u[slot]
        for k0 in range(0, KC, 16):
            sc.dma("pool", wg[:, k0:k0 + 16, 0:width], wg_d[k0 * 128:(k0 + 16) * 128, c0:c0 + width].rearrange("(kc p) n -> p kc n", p=128), w=[("wg", slot)])
            sc.dma("pool", wu[:, k0:k0 + 16, 0:width], wu_d[k0 * 128:(k0 + 16) * 128, c0:c0 + width].rearrange("(kc p) n -> p kc n", p=128), w=[("wu", slot)])
        for m in range(width // 128):
            for tg in range(TG // 512):
                pg, pu = rot.next(), rot.next()
                for kc in range(KC):
                    sc.add("pe", lambda e, pg=pg, wg=wg, kc=kc, m=m, tg=tg: e.matmul(
                        ps[pg][:, :], wg[:, kc, m * 128:(m + 1) * 128], AT[:, kc, tg * 512:(tg + 1) * 512], start=(kc == 0), stop=(kc == KC - 1)),
                        r=[("wg", slot)] + atk, w=[("ps", pg)])
                for kc in range(KC):
                    sc.add("pe", lambda e, pu=pu, wu=wu, kc=kc, m=m, tg=tg: e.matmul(
                        ps[pu][:, :], wu[:, kc, m * 128:(m + 1) * 128], AT[:, kc, tg * 512:(tg + 1) * 512], start=(kc == 0), stop=(kc == KC - 1)),
                        r=[("wu", slot)] + atk, w=[("ps", pu)])
                sg, sgk = st_f.next()
                ub, ubk = st_b.next()
                sc.add("act", lambda e, sg=sg, pg=pg: e.activation(out=sg[:], in_=ps[pg][:, :], func=AF.Silu), r=[("ps", pg)], w=[sgk])
                sc.add("dve", lambda e, sg=sg, ub=ub, pu=pu: e.tensor_tensor(out=ub[:], in0=sg[:], in1=ps[pu][:, :], op=ALU.mult), r=[sgk, ("ps", pu)], w=[ubk])
                r0 = c0 + m * 128
                t0 = tok0 + tg * 512
                sc.dma("sp", uT_d[r0:r0 + 128, t0:t0 + 512], ub[:], r=[ubk], w=[("uT", r0, t0)])


def emit_final_norm(sc, nc, x_d, w_d, out_d, S):
    with ExitStack() as st:
        wb = _sb(nc, st, "ff_wb", [128, D_MODEL], F32)
        xt = [_sb(nc, st, "ff_x%d" % i, [128, D_MODEL], F32) for i in range(2)]
        xo = [_sb(nc, st, "ff_o%d" % i, [128, D_MODEL], F32) for i in range(2)]
        ssq = _sb(nc, st, "ff_ssq", [128, 2], F32)
        epsc = _sb(nc, st, "ff_eps", [128, 1], F32)
        sc.dma("sp", wb[:], w_d.partition_broadcast(128), w=["wb"])
        sc.add("pool", lambda e: e.memset(epsc[:], EPS), w=["epsc"])
        for ti in range(S // 128):
            b = ti % 2
            sc.dma("sp", xt[b][:, 0:2048], x_d[ti * 128:(ti + 1) * 128, 0:2048], w=[("x", b)])
            sc.dma("pool", xt[b][:, 2048:4096], x_d[ti * 128:(ti + 1) * 128, 2048:4096], w=[("x", b)])
            sc.add("pool", lambda e: e.memset(ssq[:, 0:1], 0.0), w=["ssq"])
            sc.add("act", lambda e, b=b: e.activation(out=xo[b][:], in_=xt[b][:], func=AF.Square, accum_out=ssq[:, 0:1]), r=[("x", b)], w=[("xo", b), "ssq"])
            sc.add("act", lambda e: e.activation(out=ssq[:, 1:2], in_=ssq[:, 0:1], func=AF.Sqrt, bias=epsc[:, 0:1], scale=1.0 / D_MODEL), r=["ssq", "epsc"], w=["rstd"])
            sc.add("dve", lambda e: e.reciprocal(out=ssq[:, 1:2], in_=ssq[:, 1:2]), r=["rstd"], w=["rstd"])
            sc.add("dve", lambda e, b=b: e.scalar_tensor_tensor(out=xo[b][:], in0=xt[b][:], scalar=ssq[:, 1:2], in1=wb[:], op0=ALU.mult, op1=ALU.mult),
                   r=[("x", b), "rstd", "wb"], w=[("xo", b)])
            sc.dma("sp", out_d[ti * 128:(ti + 1) * 128, :], xo[b][:], r=[("xo", b)], w=[("out", ti)])
        sc.flush()


WEIGHT_SPECS = [
    ("norm_mix", [D_MODEL]), ("w_in", [D_MODEL, IN_WIDTH]), ("w_out", [D_MODEL, D_MODEL]), ("ret_norm", [1024]),
    ("lru_conv_w", [4, 1024]), ("lru_conv_b", [1024]), ("lru_wa", [8, 128, 128]), ("lru_ba", [1024]),
    ("lru_wx", [8, 128, 128]), ("lru_bx", [1024]), ("lru_lambda", [1024]),
    ("cmp_pos_k", [32, 128]), ("cmp_w1_k", [4096, 256]), ("cmp_w2_k", [256, 128]),
    ("cmp_pos_v", [32, 128]), ("cmp_w1_v", [4096, 256]), ("cmp_w2_v", [256, 128]),
    ("norm_ffn", [D_MODEL]), ("w_gate", [D_MODEL, D_FF]), ("w_up", [D_MODEL, D_FF]), ("w_down", [D_FF, D_MODEL]),
]

W_IN_SEGS = [
    (C_RQ, 1024, "F", "bf", "PF_bf", 0), (C_RK, 1024, "F", "bf", "PF_bf", 1024),
    (C_RV, 1024, "T", "bf", "PT_bf", 0), (C_RG, 1024, "T", "f", "PT_f", 0),
    (C_MQ, 1024, "F", "bf", "PF_bf", 2048), (C_MK, 1024, "F", "bf", "PF_bf", 3072),
    (C_MV, 1024, "T", "bf", "PT_bf", 1024),
    (C_LX, 1024, "F", "f", "PF_f", 0), (C_LG, 1024, "F", "f", "PF_f", 1024),
    (C_NQ, 1024, "F", "bf", "PF_bf", 4096),
    (C_NKC, 256, "F", "bf", "PF_bf", 5120), (C_NVC, 256, "F", "bf", "PF_bf", 5376),
    (C_NKS, 256, "F", "bf", "PF_bf", 5632), (C_NVS, 256, "T", "bf", "PT_bf", 2048),
    (C_NKW, 256, "F", "bf", "PF_bf", 5888), (C_NVW, 256, "T", "bf", "PT_bf", 2304),
    (C_NG, 24, "T", "f", "PT_f", 1024),
]


def build_program(S, depth, stages=("all",)):
    nc = bass.Bass("TRN2", target_bir_lowering=False)
    ALL = "all" in stages
    need = {"win": ["norm_mix", "w_in"], "wout": ["w_out"], "gateup": ["norm_ffn", "w_gate", "w_up"], "down": ["w_down"],
            "mix": ["ret_norm", "lru_conv_w", "lru_conv_b", "lru_wa", "lru_ba", "lru_wx", "lru_bx", "lru_lambda",
                    "cmp_pos_k", "cmp_w1_k", "cmp_w2_k", "cmp_pos_v", "cmp_w1_v", "cmp_w2_v"]}
    needed = set(n for st_ in stages if st_ in need for n in need[st_])
    TGK = min(2048, S)
    KC = D_MODEL // 128
    KCF = D_FF // 128
    x_d = nc.dram_tensor("x", [S, D_MODEL], F32, kind="ExternalInput").ap()
    W = {}
    for name, shp in WEIGHT_SPECS:
        if ALL or name in needed:
            W[name] = nc.dram_tensor(name, [depth] + shp, F32, kind="ExternalInput").ap()
    nf_d = nc.dram_tensor("norm_final", [D_MODEL], F32, kind="ExternalInput").ap()
    out_d = nc.dram_tensor("out", [S, D_MODEL], F32, kind="ExternalOutput").ap()
    cst = {}
    for n, (shf, dt) in CONST_SPECS.items():
        cst[n] = nc.dram_tensor("c_" + n, list(shf(S)), dt, kind="ExternalInput").ap()
    cst["ret_dchunk"] = host_constants(128)["ret_dchunk"]
    D = {
        "xa": nc.dram_tensor("s_xa", [S, D_MODEL], F32).ap(),
        "xb": nc.dram_tensor("s_xb", [S, D_MODEL], F32).ap(),
        "PF_bf": nc.dram_tensor("s_pfb", [6144, S], BF16).ap(),
        "PF_f": nc.dram_tensor("s_pff", [2048, S], F32).ap(),
        "PT_bf": nc.dram_tensor("s_ptb", [S, 2560], BF16).ap(),
        "PT_f": nc.dram_tensor("s_ptf", [S, 1048], F32).ap(),
        "ycat": nc.dram_tensor("s_ycat", [S, D_MODEL], BF16).ap(),
        "lruT": nc.dram_tensor("s_lruT", [1024, S], BF16).ap(),
        "uT": nc.dram_tensor("s_uT", [D_FF, S], BF16).ap(),
    }
    with ExitStack() as gst:
        ps = [gst.enter_context(nc.psum_tensor("ps%d" % i, [128, 512], F32)) for i in range(8)]
        sc = Sched(nc, gst)
        evc = [0]

        def evac(dst, src, rkeys, wkeys):
            if evc[0] % 2 == 0:
                sc.add("act", lambda e: e.copy(out=dst, in_=src), r=rkeys, w=wkeys)
            else:
                sc.add("dve", lambda e: e.tensor_copy(out=dst, in_=src), r=rkeys, w=wkeys)
            evc[0] += 1

        for l in range(depth):
            xin = x_d if l == 0 else D["xb"]
            for t0 in (range(0, S, TGK) if (ALL or "win" in stages) else ()):
                with ExitStack() as st:
                    AT = _sb(nc, st, "AT", [128, KC, TGK], BF16)
                    emit_fill_norm(sc, nc, ps, AT, TGK, xin, t0, W["norm_mix"][l], cst["ident"])
                    with ExitStack() as st2:
                        wslots = [_sb(nc, st2, "wsl%d" % i, [128, KC, 512], BF16) for i in range(2)]
                        st_f = Stager(nc, st2, "stf", [128, 512], F32, 3)
                        st_b = Stager(nc, st2, "stb", [128, 512], BF16, 3)
                        blocks = []
                        for (c0, width, variant, kind, dname, doff) in W_IN_SEGS:
                            for (bc0, bw, bo) in _split_blocks(c0, width):
                                def epi(sc_, ps_ap, pskey, blk, sub, variant=variant, kind=kind, dname=dname, doff=doff, bo=bo, bw=bw, t0=t0):
                                    stg, sk = (st_f if kind == "f" else st_b).next()
                                    if variant == "F":
                                        m, tg, mw = sub
                                        evac(stg[0:mw, :], ps_ap, [pskey], [sk])
                                        r0 = doff + bo + m * 128
                                        c = t0 + tg * 512
                                        sc_.dma("sp", D[dname][r0:r0 + mw, c:c + 512], stg[0:mw, :], r=[sk], w=[(dname, r0, c)])
                                    else:
                                        (tt,) = sub
                                        evac(stg[:, 0:bw], ps_ap, [pskey], [sk])
                                        r0 = t0 + tt * 128
                                        c = doff + bo
                                        sc_.dma("sp", D[dname][r0:r0 + 128, c:c + bw], stg[:, 0:bw], r=[sk], w=[(dname, r0, c)])
                                blocks.append(dict(c0=bc0, width=bw, variant=variant, epi=epi))
                        emit_gemm(sc, nc, ps, AT, KC, TGK, W["w_in"][l], blocks, wslots)
                        sc.flush()
            PFb, PFf, PTb, PTf = D["PF_bf"], D["PF_f"], D["PT_bf"], D["PT_f"]
            if ALL or "mix" in stages:
                emit_retention(sc, nc, ps, cst, PFb[0:1024, :], PFb[1024:2048, :], PTb[:, 0:1024], PTf[:, 0:1024], W["ret_norm"][l],
                               D["ycat"][:, 0:1024], S)
                emit_moba2(sc, nc, ps, cst, PFb[2048:3072, :], PFb[3072:4096, :], PTb[:, 1024:2048], D["ycat"][:, 1024:2048], S,
                          side=lambda st, l=l: lru_side(sc, nc, ps, st, PFf[0:1024, :], PFf[1024:2048, :], W["lru_conv_w"][l], W["lru_conv_b"][l],
                                                        W["lru_wa"][l], W["lru_ba"][l], W["lru_wx"][l], W["lru_bx"][l], W["lru_lambda"][l], D["lruT"], S))
                emit_nsa(sc, nc, ps, cst, PFb[4096:5120, :], PFb[5120:5376, :], PFb[5376:5632, :], PFb[5632:5888, :], PTb[:, 2048:2304],
                         PFb[5888:6144, :], PTb[:, 2304:2560], PTf[:, 1024:1048],
                         W["cmp_pos_k"][l], W["cmp_w1_k"][l], W["cmp_w2_k"][l], W["cmp_pos_v"][l], W["cmp_w1_v"][l], W["cmp_w2_v"][l],
                         D["ycat"][:, 3072:4096], S)
            for t0 in (range(0, S, TGK) if (ALL or "wout" in stages) else ()):
                with ExitStack() as st:
                    AT = _sb(nc, st, "AT", [128, KC, TGK], BF16)
                    emit_fill_y(sc, nc, ps, AT, TGK, D["ycat"], D["lruT"], t0, cst["ident"])
                    with ExitStack() as st2:
                        wslots = [_sb(nc, st2, "wsl%d" % i, [128, KC, 512], BF16) for i in range(2)]
                        st_r = Stager(nc, st2, "str", [128, 512], F32, 4)
                        blocks = []
                        for (bc0, bw, bo) in _split_blocks(0, D_MODEL):
                            def epi(sc_, ps_ap, pskey, blk, sub, bc0=bc0, bw=bw, t0=t0):
                                (tt,) = sub
                                stg, sk = st_r.next()
                                r0 = t0 + tt * 128
                                sc_.dma("sp", stg[:, 0:bw], xin[r0:r0 + 128, bc0:bc0 + bw], w=[sk])
                                sc_.add("dve", lambda e: e.tensor_tensor(out=stg[:, 0:bw], in0=stg[:, 0:bw], in1=ps_ap, op=ALU.add), r=[pskey, sk], w=[sk])
                                sc_.dma("sp", D["xa"][r0:r0 + 128, bc0:bc0 + bw], stg[:, 0:bw], r=[sk], w=[("xa", r0, bc0)])
                            blocks.append(dict(c0=bc0, width=bw, variant="T", epi=epi))
                        emit_gemm(sc, nc, ps, AT, KC, TGK, W["w_out"][l], blocks, wslots)
                        sc.flush()
            for t0 in (range(0, S, TGK) if (ALL or "gateup" in stages) else ()):
                with ExitStack() as st:
                    AT = _sb(nc, st, "AT", [128, KC, TGK], BF16)
                    emit_fill_norm(sc, nc, ps, AT, TGK, D["xa"], t0, W["norm_ffn"][l], cst["ident"])
                    with ExitStack() as st2:
                        wsg = [_sb(nc, st2, "wsg%d" % i, [128, KC, 256], BF16) for i in range(2)]
                        wsu = [_sb(nc, st2, "wsu%d" % i, [128, KC, 256], BF16) for i in range(2)]
                        st_f = Stager(nc, st2, "gsf", [128, 512], F32, 3)
                        st_b = Stager(nc, st2, "gsb", [128, 512], BF16, 3)
                        emit_gateup(sc, nc, ps, AT, TGK, t0, W["w_gate"][l], W["w_up"][l], D["uT"], wsg, wsu, st_f, st_b)
                        sc.flush()
            TGD = min(2048, S)
            kq = [(0, 29), (29, 58), (58, 86)]
            with ExitStack() as st:
                ATd = _sb(nc, st, "ATd", [128, 29, TGD], BF16)
                wslots = [_sb(nc, st, "wsd%d" % i, [128, 29, 512], BF16) for i in range(2)]
                st_r = Stager(nc, st, "dsr", [128, 512], F32, 4)
                for t0 in (range(0, S, TGD) if (ALL or "down" in stages) else ()):
                    for qi, (k0, k1) in enumerate(kq):
                        for ka in range(k0, k1, 8):
                            kb = min(k1, ka + 8)
                            sc.dma("sp", ATd[:, ka - k0:kb - k0, :],
                                   D["uT"][ka * 128:kb * 128, t0:t0 + TGD].rearrange("(kc p) t -> p kc t", p=128), w=["ATd"])
                        src_d = D["xa"] if qi == 0 else D["xb"]
                        blocks = []
                        for (bc0, bw, bo) in _split_blocks(0, D_MODEL, 512):
                            def epi(sc_, ps_ap, pskey, blk, sub, bc0=bc0, bw=bw, t0=t0, src_d=src_d):
                                (tt,) = sub
                                stg, sk = st_r.next()
                                r0 = t0 + tt * 128
                                sc_.dma("sp", stg[:, 0:bw], src_d[r0:r0 + 128, bc0:bc0 + bw], r=[("xb", r0, bc0)], w=[sk])
                                sc_.add("dve", lambda e: e.tensor_tensor(out=stg[:, 0:bw], in0=stg[:, 0:bw], in1=ps_ap, op=ALU.add), r=[pskey, sk], w=[sk])
                                sc_.dma("sp", D["xb"][r0:r0 + 128, bc0:bc0 + bw], stg[:, 0:bw], r=[sk], w=[("xb", r0, bc0)])
                            blocks.append(dict(c0=bc0, width=bw, variant="T", epi=epi))
                        emit_gemm(sc, nc, ps, ATd, k1 - k0, TGD, W["w_down"][l][k0 * 128:k1 * 128, :], blocks, wslots, at_keys=["ATd"])
                sc.flush()
        emit_final_norm(sc, nc, D["xb"], nf_d, out_d, S)
        sc.flush()
        n_ops = sc.n_emitted
    return nc, n_ops


_CACHE = {}


def _get_program(S, depth):
    key = (S, depth)
    if key not in _CACHE:
        _CACHE[key] = build_program(S, depth)
    return _CACHE[key]


def kernel(**inputs):
    x = np.asarray(inputs["x"], np.float32)
    B, S, Dm = x.shape
    depth = int(np.asarray(inputs["w_in"]).shape[0])
    nc, _ = _get_program(S, depth)
    hc = host_constants(S)
    shared = {}
    for name, _shp in WEIGHT_SPECS:
        shared[name] = np.ascontiguousarray(np.asarray(inputs[name], np.float32))
    shared["norm_final"] = np.ascontiguousarray(np.asarray(inputs["norm_final"], np.float32))
    for n in CONST_SPECS:
        shared["c_" + n] = np.ascontiguousarray(hc[n])
    in_maps = []
    for b in range(B):
        m = dict(shared)
        m["x"] = np.ascontiguousarray(x[b])
        in_maps.append(m)
    res = run_bass_kernel_spmd(nc, in_maps, core_ids=list(range(B)))
    out = np.stack([np.asarray(res.results[b]["out"], np.float32) for b in range(B)], 0)
    return out
```
